# Optimizing a Trainium2 kernel written in Bass

```python
import jax, jax.numpy as jnp
from jax import lax
import numpy as np

D_MODEL = 2048
BATCH = 4
SEQ = 4096
DEPTH = 4

BRANCH_WIDTH = D_MODEL // 2
N_BRANCHES = 3
RET_HEADS = 4
RET_WIDTH = BRANCH_WIDTH
RET_HEAD_DIM = RET_WIDTH // RET_HEADS
RET_CHUNK = 128
RET_ROPE_BASE = 10000.0
POOL_WINDOWS = (2, 4, 8, 16)
POOL_GROUPS = len(POOL_WINDOWS)
POOL_WIDTH = BRANCH_WIDTH
POOL_GROUP_DIM = POOL_WIDTH // POOL_GROUPS
ATT_HEAD_DIM = 128
ATT_WIDTH = BRANCH_WIDTH
ATT_Q_HEADS = ATT_WIDTH // ATT_HEAD_DIM
ATT_KV_HEADS = ATT_Q_HEADS // 4
ATT_KV_WIDTH = ATT_KV_HEADS * ATT_HEAD_DIM
ATT_WINDOW = 128
ATT_BLOCK = 128
ROPE_THETA = 500000.0
ROPE_DIMS = ATT_HEAD_DIM // 4
RMS_EPS = 1e-6
NEG_BIG = -1e30

IN_SIZES = (RET_WIDTH, RET_WIDTH, RET_WIDTH, RET_WIDTH,
            POOL_WIDTH, POOL_WIDTH,
            ATT_WIDTH, ATT_KV_WIDTH, ATT_KV_WIDTH, ATT_WIDTH,
            N_BRANCHES * D_MODEL)
IN_WIDTH = sum(IN_SIZES)

kernel_name = "hybrid_retention_pool_swa_encoder"


def rms_norm(x, gain=None):
    x32 = x.astype(jnp.float32)
    y = x32 * lax.rsqrt(jnp.mean(x32 * x32, axis=-1, keepdims=True) + RMS_EPS)
    if gain is not None:
        y = y * gain.astype(jnp.float32)
    return y.astype(x.dtype)


def rotary(x, inv_freq):
    S = x.shape[1]
    half = inv_freq.shape[0]
    ang = jnp.arange(S, dtype=jnp.float32)[:, None] * inv_freq[None, :]
    cos = jnp.cos(ang)[None, :, None, :]
    sin = jnp.sin(ang)[None, :, None, :]
    xr = x[..., :2 * half].astype(jnp.float32)
    x1, x2 = xr[..., :half], xr[..., half:]
    rot = jnp.concatenate([x1 * cos - x2 * sin, x2 * cos + x1 * sin], axis=-1).astype(x.dtype)
    return jnp.concatenate([rot, x[..., 2 * half:]], axis=-1)


def retention(q, k, v, a_fwd, a_bwd):
    B, S, H, Dh = q.shape
    C = RET_CHUNK
    nC = S // C
    dt = q.dtype
    inv = 1.0 / (RET_ROPE_BASE ** jnp.linspace(0.0, 1.0, Dh // 2, dtype=jnp.float32))
    q = rotary(q, inv)
    k = rotary(k, inv) * jnp.asarray(Dh ** -0.5, dt)
    lg_f = -jnp.exp(a_fwd.astype(jnp.float32))
    lg_b = -jnp.exp(a_bwd.astype(jnp.float32))
    j = jnp.arange(C, dtype=jnp.float32)
    lag = j[:, None] - j[None, :]
    alag = jnp.abs(lag)[None]
    dmask = jnp.where(lag[None] >= 0,
                      jnp.exp(lg_f[:, None, None] * alag),
                      jnp.exp(lg_b[:, None, None] * alag)).astype(dt)
    qc = q.reshape(B, nC, C, H, Dh)
    kc = k.reshape(B, nC, C, H, Dh)
    vc = v.reshape(B, nC, C, H, Dh)
    scores = jnp.einsum('bnjhd,bnlhd->bnhjl', qc, kc) * dmask[None, None]
    out = jnp.einsum('bnhjl,bnlhe->bnjhe', scores, vc)
    w_f = jnp.exp(lg_f[None, :] * (C - 1 - j)[:, None]).astype(dt)
    w_b = jnp.exp(lg_b[None, :] * j[:, None]).astype(dt)
    kv_f = jnp.einsum('bnlhd,lh,bnlhe->nbhde', kc, w_f, vc)
    kv_b = jnp.einsum('bnlhd,lh,bnlhe->nbhde', kc, w_b, vc)
    dec_f = jnp.exp(lg_f * C).astype(dt)[None, :, None, None]
    dec_b = jnp.exp(lg_b * C).astype(dt)[None, :, None, None]

    def step_f(state, kv):
        return state * dec_f + kv, state

    def step_b(state, kv):
        return state * dec_b + kv, state

    init = jnp.zeros((B, H, Dh, Dh), dt)
    _, s_f = lax.scan(step_f, init, kv_f)
    _, s_b = lax.scan(step_b, init, kv_b, reverse=True)
    q_f = jnp.exp(lg_f[None, :] * (j + 1.0)[:, None]).astype(dt)
    q_b = jnp.exp(lg_b[None, :] * (C - j)[:, None]).astype(dt)
    out = (out
           + jnp.einsum('bnjhd,nbhde->bnjhe', qc * q_f[None, None, :, :, None], s_f)
           + jnp.einsum('bnjhd,nbhde->bnjhe', qc * q_b[None, None, :, :, None], s_b))
    out = rms_norm(out.reshape(B, S, H, Dh))
    return out.reshape(B, S, H * Dh)


def multiscale_pool(u, pool_w, pool_scale):
    B, S, _ = u.shape
    ug = u.reshape(B, S, POOL_GROUPS, POOL_GROUP_DIM).astype(jnp.float32)
    cs = jnp.pad(jnp.cumsum(ug, axis=1), ((0, 0), (1, 0), (0, 0), (0, 0)))
    pos = jnp.arange(S)
    groups = []
    for g, w in enumerate(POOL_WINDOWS):
        lo = jnp.clip(pos - w // 2, 0, S)
        hi = jnp.clip(pos + w // 2, 0, S)
        cnt = (hi - lo).astype(jnp.float32)[None, :, None]
        csg = cs[:, :, g]
        mean = (csg[:, hi] - csg[:, lo]) / cnt
        groups.append(mean - ug[:, :, g])
    p = jnp.stack(groups, axis=2).astype(u.dtype)
    y = jnp.einsum('bsgd,gde->bsge', p, pool_w).reshape(B, S, POOL_WIDTH)
    return y * pool_scale


def windowed_gqa(q, k, v, q_gain, k_gain, sink):
    B, S, Hq, Dh = q.shape
    Hkv = k.shape[2]
    G = Hq // Hkv
    nB = S // ATT_BLOCK
    L = ATT_BLOCK
    inv = ROPE_THETA ** (-jnp.arange(ROPE_DIMS // 2, dtype=jnp.float32) / (ROPE_DIMS // 2))
    q = rotary(rms_norm(q, q_gain), inv)
    k = rotary(rms_norm(k, k_gain), inv)
    qb = q.reshape(B, nB, L, Hkv, G, Dh)
    pad = ((0, 0), (1, 1), (0, 0), (0, 0), (0, 0))
    kp = jnp.pad(k.reshape(B, nB, L, Hkv, Dh), pad)
    vp = jnp.pad(v.reshape(B, nB, L, Hkv, Dh), pad)
    kw = jnp.concatenate([kp[:, :-2], kp[:, 1:-1], kp[:, 2:]], axis=2)
    vw = jnp.concatenate([vp[:, :-2], vp[:, 1:-1], vp[:, 2:]], axis=2)
    s = jnp.einsum('bnqkgd,bnskd->bnkgqs', qb, kw).astype(jnp.float32) * (Dh ** -0.5)
    blk = jnp.arange(nB)[:, None]
    qpos = blk * L + jnp.arange(L)[None, :]
    kpos = (blk - 1) * L + jnp.arange(3 * L)[None, :]
    diff = kpos[:, None, :] - qpos[:, :, None]
    valid = ((jnp.abs(diff) <= ATT_WINDOW)
             & (kpos >= 0)[:, None, :] & (kpos < S)[:, None, :])
    s = jnp.where(valid[None, :, None, None], s, NEG_BIG)
    sk = sink.astype(jnp.float32).reshape(Hkv, G)[None, None, :, :, None, None]
    m = jnp.maximum(jnp.max(s, axis=-1, keepdims=True), sk)
    p = jnp.exp(s - m)
    p = p / (jnp.sum(p, axis=-1, keepdims=True) + jnp.exp(sk - m))
    o = jnp.einsum('bnkgqs,bnskd->bnqkgd', p.astype(v.dtype), vw)
    return o.reshape(B, S, Hq * Dh)


def hybrid_layer(x, norm_g, w_in, a_fwd, a_bwd, pool_w, pool_scale,
                 q_gain, k_gain, sink, w_ret, w_pool, w_att, w_out):
    B, S, D = x.shape
    h = rms_norm(x, norm_g)
    z = jnp.einsum('bsd,de->bse', h, w_in)
    (rq, rk, rv, rg, pv, pg, aq, ak, av, ag, mg) = jnp.split(
        z, list(np.cumsum(IN_SIZES)[:-1]), axis=-1)
    ya = retention(rq.reshape(B, S, RET_HEADS, RET_HEAD_DIM),
                   rk.reshape(B, S, RET_HEADS, RET_HEAD_DIM),
                   rv.reshape(B, S, RET_HEADS, RET_HEAD_DIM), a_fwd, a_bwd)
    ya = jnp.einsum('bse,ed->bsd', ya * jax.nn.silu(rg), w_ret)
    yb = multiscale_pool(pv, pool_w, pool_scale)
    yb = jnp.einsum('bse,ed->bsd', yb * jax.nn.silu(pg), w_pool)
    yc = windowed_gqa(aq.reshape(B, S, ATT_Q_HEADS, ATT_HEAD_DIM),
                      ak.reshape(B, S, ATT_KV_HEADS, ATT_HEAD_DIM),
                      av.reshape(B, S, ATT_KV_HEADS, ATT_HEAD_DIM), q_gain, k_gain, sink)
    yc = jnp.einsum('bse,ed->bsd', yc * jax.nn.silu(ag), w_att)
    gates = jax.nn.sigmoid(mg.astype(jnp.float32)).astype(x.dtype).reshape(B, S, N_BRANCHES, D)
    merged = gates[:, :, 0] * ya + gates[:, :, 1] * yb + gates[:, :, 2] * yc
    return x + jnp.einsum('bsd,de->bse', merged, w_out)


def setup_inputs(seed: int = 0) -> dict:
    key = jax.random.key(seed)
    ks = jax.random.split(key, 14)
    f32 = jnp.float32
    D = D_MODEL
    nrm = jax.random.normal
    base = np.log(-np.log1p(-(2.0 ** (-5.0 - np.arange(RET_HEADS))))).astype(np.float32)
    base = jnp.asarray(base)[None, :]
    return {
        "x": nrm(ks[0], (BATCH, SEQ, D), f32),
        "norm_g": 1.0 + 0.02 * nrm(ks[1], (DEPTH, D), f32),
        "w_in": nrm(ks[2], (DEPTH, D, IN_WIDTH), f32) * (D ** -0.5),
        "ret_decay_fwd": base + 0.1 * nrm(ks[3], (DEPTH, RET_HEADS), f32),
        "ret_decay_bwd": base + 0.1 * nrm(ks[4], (DEPTH, RET_HEADS), f32),
        "pool_w": nrm(ks[5], (DEPTH, POOL_GROUPS, POOL_GROUP_DIM, POOL_GROUP_DIM), f32) * (POOL_GROUP_DIM ** -0.5),
        "pool_scale": 1.0 + 0.02 * nrm(ks[6], (DEPTH, POOL_WIDTH), f32),
        "attn_q_gain": 1.0 + 0.02 * nrm(ks[7], (DEPTH, ATT_HEAD_DIM), f32),
        "attn_k_gain": 1.0 + 0.02 * nrm(ks[8], (DEPTH, ATT_HEAD_DIM), f32),
        "attn_sink": 0.5 * nrm(ks[9], (DEPTH, ATT_Q_HEADS), f32),
        "w_ret": nrm(ks[10], (DEPTH, RET_WIDTH, D), f32) * (RET_WIDTH ** -0.5),
        "w_pool": nrm(ks[11], (DEPTH, POOL_WIDTH, D), f32) * (POOL_WIDTH ** -0.5),
        "w_att": nrm(ks[12], (DEPTH, ATT_WIDTH, D), f32) * (ATT_WIDTH ** -0.5),
        "w_out": nrm(ks[13], (DEPTH, D, D), f32) * (D ** -0.5),
    }


def reference(x, norm_g, w_in, ret_decay_fwd, ret_decay_bwd, pool_w, pool_scale,
              attn_q_gain, attn_k_gain, attn_sink, w_ret, w_pool, w_att, w_out):
    for l in range(DEPTH):
        x = hybrid_layer(x, norm_g[l], w_in[l], ret_decay_fwd[l], ret_decay_bwd[l],
                         pool_w[l], pool_scale[l], attn_q_gain[l], attn_k_gain[l],
                         attn_sink[l], w_ret[l], w_pool[l], w_att[l], w_out[l])
    return x
```

```python
import contextlib
import numpy as np
import ml_dtypes
import concourse.bass as bass
import concourse.mybir as mybir
from concourse.bass_utils import run_bass_kernel_spmd

F32 = mybir.dt.float32
BF16 = mybir.dt.bfloat16
AF = mybir.ActivationFunctionType
ALU = mybir.AluOpType

SEQ = 4096
DM = 2048
NLAYER = 4
NCHK = SEQ // 128
EPS = 1e-6
M0 = 12.0

ZRQ, ZRK, ZRG, ZPV, ZPG, ZAQ, ZAK, ZAG, ZMG = 0, 1024, 2048, 3072, 4096, 5120, 6144, 6400, 7424
ZROWS = 13568
NFM = ZROWS // 128
FM_SEGS = [(0, 1024, "copy"), (1024, 1024, "copy"), (3072, 1024, "silu"), (4096, 1024, "copy"),
           (5120, 1024, "silu"), (6144, 1024, "copy"), (7168, 256, "copy"), (7680, 1024, "silu"),
           (8704, 6144, "sigmoid")]
V_SEGS = [(2048, 1024), (7424, 256)]
NPV = 42
CF_LF, CF_LB, CF_JF, CF_JB, CF_COL, CF_PINV = 0, 128, 256, 384, 512, 516
NCF = 516 + 64 + 8
CF_B = 580
CB_ONES, CB_ID, CB_PT, CB_MP, CB_MN = 0, 128, 256, 384, 512
NCB = 640


class Sched:
    ENG = ("pe", "act", "dve", "pool", "sp")

    def __init__(self, nc, sems):
        self.nc = nc
        pool = list(sems)
        self.esem = {e: pool.pop() for e in self.ENG[:4]}
        self.ecnt = {e: 0 for e in self.ENG[:4]}
        self.pe_pending = False
        self.dma_pool = pool
        self.dsem = {}
        self.dcnt = {id(s): 0 for s in pool}
        self.semobj = {id(s): s for s in pool}
        for s in self.esem.values():
            self.semobj[id(s)] = s
        self.known = {e: {} for e in self.ENG}
        self.lastw = {}
        self.readers = {}
        self.ops = {e: [] for e in self.ENG}
        self.nops = 0
        self.check = False
        self.maxops = None
        self.simval = {}

    def _dma_sem(self, key):
        if key not in self.dsem:
            idx = len(self.dsem)
            assert idx < len(self.dma_pool), "out of DMA semaphores"
            self.dsem[key] = self.dma_pool[idx]
        return self.dsem[key]

    def _need(self, eng, dep, waits, is_mm):
        semid, val, src = dep
        if src == "pe" and eng == "pe" and is_mm:
            return
        if semid in self.dcnt:
            val = self.dcnt[semid]
        if self.known[eng].get(semid, -1) >= val:
            return
        self.known[eng][semid] = val
        waits[semid] = max(waits.get(semid, 0), val)

    def op(self, eng, fn, reads=(), writes=(), dma_key=None, is_mm=False, inc=True):
        if self.maxops is not None and self.nops >= self.maxops:
            return
        waits = {}
        for k in list(reads) + list(writes):
            d = self.lastw.get(k)
            if d is not None:
                self._need(eng, d, waits, is_mm)
        for k in writes:
            for semid, (val, src) in self.readers.get(k, {}).items():
                self._need(eng, (semid, val, src), waits, is_mm)
        if dma_key is not None:
            s = self._dma_sem(dma_key)
            self.dcnt[id(s)] += 16
            ev = (id(s), self.dcnt[id(s)], "dma")
            incr = (s, 16)
        else:
            s = self.esem[eng]
            if inc:
                self.ecnt[eng] += 1
                ev = (id(s), self.ecnt[eng], eng)
                incr = (s, 1)
                if eng == "pe":
                    self.pe_pending = False
            else:
                assert eng == "pe" and is_mm
                ev = (id(s), self.ecnt[eng] + 1, eng)
                incr = None
                self.pe_pending = True
        for k in writes:
            self.lastw[k] = ev
            self.readers[k] = {}
        for k in reads:
            r = self.readers.setdefault(k, {})
            r[ev[0]] = (ev[1], ev[2])
        self.ops[eng].append((fn, [(self.semobj[sid], v) for sid, v in waits.items()], incr))
        self.nops += 1

    def _simulate(self, ops):
        val = self.simval
        pc = {e: 0 for e in self.ENG}
        progress = True
        while progress:
            progress = False
            for e in self.ENG:
                lst = ops[e]
                while pc[e] < len(lst):
                    fn, waits, incr = lst[pc[e]]
                    if any(val.get(id(s), 0) < v for s, v in waits):
                        break
                    if incr is not None:
                        val[id(incr[0])] = val.get(id(incr[0]), 0) + incr[1]
                    pc[e] += 1
                    progress = True
        stuck = {e: (pc[e], len(ops[e])) for e in self.ENG if pc[e] < len(ops[e])}
        if stuck:
            msg = []
            for e, (p, n) in stuck.items():
                fn, waits, incr = ops[e][p]
                msg.append(f"{e}: op {p}/{n} waits " + str([(s.name if hasattr(s, 'name') else id(s), v, val.get(id(s), 0)) for s, v in waits]))
            raise RuntimeError("DEADLOCK in recorded program: " + " | ".join(msg))

    def end_phase(self):
        assert self.maxops is not None or not self.pe_pending
        targets = []
        for e in self.ENG[:4]:
            if self.ecnt[e] > 0:
                targets.append((id(self.esem[e]), self.ecnt[e]))
        for sid, c in self.dcnt.items():
            if c > 0:
                targets.append((sid, c))
        for e in self.ENG:
            waits = []
            for sid, v in targets:
                if self.known[e].get(sid, -1) >= v:
                    continue
                self.known[e][sid] = v
                waits.append((self.semobj[sid], v))
            if waits:
                self.ops[e].append((None, waits, None))
        self.lastw = {}
        self.readers = {}
        self.dsem = {}
        if self.check:
            self._simulate(self.ops)
        nc = self.nc
        ops = self.ops
        self.ops = {e: [] for e in self.ENG}

        def replay(engine, lst):
            for fn, waits, incr in lst:
                for s, v in waits:
                    engine.wait_ge(s, v)
                if fn is not None:
                    ins = fn(engine)
                    if incr is not None:
                        ins.then_inc(incr[0], incr[1])

        with nc.Block() as block:
            @block.tensor
            def _(e):
                replay(e, ops["pe"])

            @block.scalar
            def _(e):
                replay(e, ops["act"])

            @block.vector
            def _(e):
                replay(e, ops["dve"])

            @block.gpsimd
            def _(e):
                replay(e, ops["pool"])

            @block.sync
            def _(e):
                replay(e, ops["sp"])


class Prog:
    def __init__(self, nlayers=NLAYER, dbg=False, stop_after=None, only=None):
        self.only = only
        self.nlayers = nlayers
        self.dbg = dbg
        self.stop_after = stop_after
        nc = bass.Bass("TRN2", target_bir_lowering=False)
        self.nc = nc
        ein = "ExternalInput"
        sk = "ExternalOutput" if dbg else "Internal"
        dt = nc.dram_tensor
        if not only:
            self.xT = dt("xT", [DM, SEQ], F32, kind=ein).ap()
            self.win_fm = dt("win_fm", [NLAYER, NFM, 128, 16, 128], F32, kind=ein).ap()
            self.win_v = dt("win_v", [NLAYER, 128, 16, 1280], F32, kind=ein).ap()
            self.w_br = dt("w_br", [NLAYER, 16, 128, 3, 8, 128], F32, kind=ein).ap()
            self.w_o = dt("w_o", [NLAYER, 16, 128, 16, 128], F32, kind=ein).ap()
        self.w_pl = dt("w_pl", [NLAYER, 128, 4, 2, 256], F32, kind=ein).ap()
        self.pvec = dt("pvec", [128, NLAYER, NPV], F32, kind=ein).ap()
        self.cf = dt("cf", [128, NCF], F32, kind=ein).ap()
        self.cb = dt("cb", [128, NCB], BF16, kind=ein).ap()
        self.rcos = dt("rcos", [128, SEQ], F32, kind=ein).ap()
        self.rsin = dt("rsin", [128, SEQ], F32, kind=ein).ap()
        self.acos = dt("acos", [128, SEQ], F32, kind=ein).ap()
        self.asin = dt("asin", [128, SEQ], F32, kind=ein).ap()
        if not only:
            self.outT = dt("outT", [DM, SEQ], F32, kind="ExternalOutput").ap()
        self.zT = dt("zT", [ZROWS, SEQ], BF16, kind=(ein if only else sk)).ap()
        self.vtok = dt("vtok", [SEQ, 1280], BF16, kind=(ein if only else sk)).ap()
        self.bin = dt("bin", [3072, SEQ], BF16, kind=sk).ap()
        if not only:
            self.xs = [dt("xs0", [DM, SEQ], F32, kind=sk).ap(), dt("xs1", [DM, SEQ], F32, kind=sk).ap()]
        self._uid = 0

    def sb(self, es, shape, dtype, name=None):
        self._uid += 1
        return es.enter_context(self.nc.sbuf_tensor(f"{name or 't'}_{self._uid}", list(shape), dtype))

    def pbanks(self, es, n=8, dtype=F32, cols=512):
        out = []
        for _ in range(n):
            self._uid += 1
            out.append(es.enter_context(self.nc.psum_tensor(f"ps_{self._uid}", [128, cols], dtype)))
        return out

    def dma(self, q, out, in_, reads, writes, key):
        self.S.op(q, lambda e: e.dma_start(out=out, in_=in_), reads=reads, writes=writes, dma_key=key)

    def mm(self, out, lhsT, rhs, start, stop, reads, writes, inc=None):
        if inc is None:
            inc = stop
        self.S.op("pe", lambda e: e.matmul(out, lhsT=lhsT, rhs=rhs, start=start, stop=stop),
                  reads=reads, writes=writes, is_mm=True, inc=inc)

    def act(self, out, in_, func, reads, writes, scale=None, bias=None):
        kw = {}
        if scale is not None:
            kw["scale"] = scale
        if bias is not None:
            kw["bias"] = bias
        self.S.op("act", lambda e: e.activation(out=out, in_=in_, func=func, **kw), reads=reads, writes=writes)

    def tt(self, eng, out, in0, in1, op, reads, writes):
        self.S.op(eng, lambda e: e.tensor_tensor(out=out, in0=in0, in1=in1, op=op), reads=reads, writes=writes)

    def ts(self, eng, out, in0, s1, s2, op0, op1, reads, writes):
        if op1 is None:
            self.S.op(eng, lambda e: e.tensor_scalar(out=out, in0=in0, scalar1=s1, scalar2=None, op0=op0),
                      reads=reads, writes=writes)
        else:
            self.S.op(eng, lambda e: e.tensor_scalar(out=out, in0=in0, scalar1=s1, scalar2=s2, op0=op0, op1=op1),
                      reads=reads, writes=writes)

    def stt(self, eng, out, in0, scalar, in1, op0, op1, reads, writes):
        self.S.op(eng, lambda e: e.scalar_tensor_tensor(out=out, in0=in0, scalar=scalar, in1=in1, op0=op0, op1=op1),
                  reads=reads, writes=writes)

    def cp(self, eng, out, in_, reads, writes):
        if eng == "act":
            self.act(out, in_, AF.Copy, reads, writes)
        else:
            self.S.op(eng, lambda e: e.tensor_copy(out=out, in_=in_), reads=reads, writes=writes)

    def rsqrt(self, out, ps_ap, bias_ap, tmp, reads, ktmp, kout):
        self.act(tmp, ps_ap, AF.Ln, reads, [ktmp], bias=bias_ap)
        self.act(out, tmp, AF.Exp, [ktmp], [kout], scale=-0.5)

    def memset(self, eng, ap, val, writes):
        self.S.op(eng, lambda e: e.memset(ap, val), writes=writes)

    def build(self):
        nc = self.nc
        with contextlib.ExitStack() as es:
            sems = [es.enter_context(nc.semaphore(f"sem{i}")) for i in range(100)]
            fin = es.enter_context(nc.semaphore("fin"))
            for s_ in sems + [fin]:
                nc.gpsimd.sem_clear(s_)
            nc.all_engine_barrier()
            self.S = Sched(nc, sems)
            self._build_body()
            with nc.Block() as block:
                @block.tensor
                def _(e):
                    e.sem_inc(fin, 1)

                @block.scalar
                def _(e):
                    e.sem_inc(fin, 1)

                @block.vector
                def _(e):
                    e.sem_inc(fin, 1)

                @block.sync
                def _(e):
                    e.sem_inc(fin, 1)

                @block.gpsimd
                def _(e):
                    e.wait_ge(fin, 4)
                    for s_ in sems + [fin]:
                        e.sem_clear(s_)
            nc.all_engine_barrier()
        return nc

    def _build_body(self):
        nc = self.nc
        if True:
            if self.only:
                if self.only.startswith("ret"):
                    for h in ([int(self.only[3:])] if len(self.only) > 3 else range(4)):
                        self.phase_ret(0, h)
                elif self.only == "pool":
                    self.phase_pool(0)
                elif self.only == "att":
                    for g in range(2):
                        self.phase_att(0, g)
                return
            for l in range(self.nlayers):
                x_cur = self.xT if l == 0 else self.xs[(l - 1) % 2]
                x_nxt = self.outT if l == self.nlayers - 1 else self.xs[l % 2]
                for th in range(2):
                    self.phase_p1(l, th, x_cur)
                if self.stop_after == "p1":
                    break
                for h in range(4):
                    self.phase_ret(l, h)
                if self.stop_after == "ret":
                    break
                self.phase_pool(l)
                if self.stop_after == "pool":
                    break
                for g in range(2):
                    self.phase_att(l, g)
                if self.stop_after == "att":
                    break
                for tq in range(4):
                    self.phase_merge(l, tq, x_cur, x_nxt)

    def phase_p1(self, l, th, x_cur):
        S = self.S
        T0 = th * 2048
        with contextlib.ExitStack() as es:
            hT = self.sb(es, [128, 16, 2048], BF16, "hT")
            xin = [self.sb(es, [128, 16, 256], F32, "xin") for _ in range(2)]
            sq = [self.sb(es, [128, 16, 256], BF16, "sq") for _ in range(2)]
            rstd = [self.sb(es, [128, 256], F32, "rstd") for _ in range(2)]
            gcol = self.sb(es, [128, 16], F32, "gcol")
            ones = self.sb(es, [128, 128], BF16, "ones")
            wch = [self.sb(es, [128, 16, 128], BF16, "wch") for _ in range(3)]
            zst = [self.sb(es, [128, 2048], BF16, "zst") for _ in range(3)]
            wv = [self.sb(es, [128, 16, 256], BF16, "wv") for _ in range(2)]
            vst = [self.sb(es, [128, 16, 256], BF16, "vst") for _ in range(2)]
            ps = self.pbanks(es)

            cfb = self.sb(es, [128, 8], F32, "cfb")
            rtmp = [self.sb(es, [128, 256], F32, "rtmp") for _ in range(2)]
            self.dma("sp", cfb[:], self.cf[:, CF_B:CF_B + 8], [], ["cfb"], "cfb")
            self.dma("sp", gcol[:], self.pvec[:, l, 0:16], [], ["gcol"], "gcol")
            self.dma("sp", ones[:], self.cb[:, CB_ONES:CB_ONES + 128], [], ["ones"], "ones")
            self.ts("dve", gcol[:], gcol[:], float(np.sqrt(2048.0)), None, ALU.mult, None, ["gcol"], ["gcol"])
            xv = x_cur.rearrange("(kc p) t -> p kc t", p=128)
            for t in range(8):
                s = t % 2
                self.dma("sp", xin[s][:], xv[:, :, T0 + t * 256:T0 + (t + 1) * 256], [], [("xin", s)], ("xin", s))
                self.act(sq[s][:].rearrange("p k t -> p (k t)"), xin[s][:].rearrange("p k t -> p (k t)"),
                         AF.Square, [("xin", s)], [("sq", s)])
                for kc in range(16):
                    self.mm(ps[s][:, 0:256], ones[:], sq[s][:, kc, :], kc == 0, kc == 15,
                            [("sq", s), "ones"], [("ps", s)])
                self.rsqrt(rstd[s][:], ps[s][:, 0:256], cfb[:, 0:1], rtmp[s][:], [("ps", s), "cfb"], ("rtmp", s), ("rstd", s))
                self.tt("pool", xin[s][:], xin[s][:], rstd[s][:].unsqueeze(1).to_broadcast([128, 16, 256]),
                        ALU.mult, [("xin", s), ("rstd", s)], [("xin", s)])
                self.tt("dve", hT[:, :, t * 256:(t + 1) * 256], xin[s][:],
                        gcol[:].unsqueeze(2).to_broadcast([128, 16, 256]), ALU.mult,
                        [("xin", s), "gcol"], [("hT", t)])
            hkeys = [("hT", t) for t in range(8)]
            nb = 0
            for c in range(NFM):
                s = c % 3
                fn = "copy"
                row = c * 128
                for (z0, w, f), r0 in zip(FM_SEGS, (ZRQ, ZRK, ZRG, ZPV, ZPG, ZAQ, ZAK, ZAG, ZMG)):
                    if r0 <= row < r0 + w:
                        fn = f
                self.dma("pool", wch[s][:], self.win_fm[l, c], [], [("wch", s)], ("wch", s))
                for tt in range(4):
                    b = 2 + nb % 6
                    nb += 1
                    for kc in range(16):
                        self.mm(ps[b][:], wch[s][:, kc, :], hT[:, kc, tt * 512:(tt + 1) * 512], kc == 0, kc == 15,
                                [("wch", s), ("hT", 2 * tt), ("hT", 2 * tt + 1)], [("ps", b)])
                    o = zst[s][:, tt * 512:(tt + 1) * 512]
                    if fn == "copy":
                        eng = "dve" if (tt % 2 == 0) else "act"
                        self.cp(eng, o, ps[b][:], [("ps", b)], [("zst", s, tt)])
                    else:
                        self.act(o, ps[b][:], AF.Silu if fn == "silu" else AF.Sigmoid, [("ps", b)], [("zst", s, tt)])
                self.dma("sp", self.zT[row:row + 128, T0:T0 + 2048], zst[s][:],
                         [("zst", s, i) for i in range(4)], [], ("zst", s))
            vv = self.vtok[T0:T0 + 2048, :].rearrange("(k p) c -> p k c", p=128)
            for cpi in range(5):
                s = cpi % 2
                self.dma("pool", wv[s][:], self.win_v[l, :, :, cpi * 256:(cpi + 1) * 256], [], [("wv", s)], ("wv", s))
                for tk in range(16):
                    b = 2 + nb % 6
                    nb += 1
                    for kc in range(16):
                        self.mm(ps[b][:, 0:256], hT[:, kc, tk * 128:(tk + 1) * 128], wv[s][:, kc, :], kc == 0, kc == 15,
                                [("wv", s), ("hT", tk // 2)], [("ps", b)])
                    eng = "dve" if (tk % 2 == 0) else "act"
                    self.cp(eng, vst[s][:, tk, :], ps[b][:, 0:256], [("ps", b)], [("vst", s, tk)])
                self.dma("sp", vv[:, :, cpi * 256:(cpi + 1) * 256], vst[s][:],
                         [("vst", s, i) for i in range(16)], [], ("vst", s))
            S.end_phase()

    def phase_ret(self, l, h):
        S = self.S
        with contextlib.ExitStack() as es:
            sb = lambda shape, dtype, name=None: self.sb(es, shape, dtype, name)
            cf = sb([128, NCF], F32, "cf")
            cbt = sb([128, NCB], BF16, "cb")
            pvt = sb([128, NPV], F32, "pv")
            lg = sb([128, 8], F32, "lg")
            dmask = sb([128, 128], F32, "dmask")
            tmpm = sb([128, 128], F32, "tmpm")
            qft = sb([128, 128], F32, "qft")
            qbt = sb([128, 128], F32, "qbt")
            colv = sb([128, 4], F32, "colv")
            qrot = sb([128, 2, SEQ], BF16, "qrot")
            krot = sb([128, 2, SEQ], BF16, "krot")
            ktok = sb([128, NCHK, 256], BF16, "ktok")
            v = sb([128, NCHK, 256], BF16, "v")
            sbb = sb([128, NCHK, 2, 256], BF16, "sbb")
            gst = sb([128, 2, SEQ], BF16, "gst")
            raw = [sb([128, 2, 1024], BF16, "raw") for _ in range(2)]
            cs = [sb([128, 2, 1024], F32, "cs") for _ in range(2)]
            tmp = [sb([128, 1024], F32, "rt") for _ in range(4)]
            St = [sb([128, 2, 256], F32, "St") for _ in range(2)]
            sfc = [sb([128, 2, 256], BF16, "sfc") for _ in range(2)]
            vw = [sb([128, 256], BF16, "vw") for _ in range(2)]
            sT = [sb([128, 128], BF16, "sT") for _ in range(2)]
            qf = [sb([128, 2, 128], BF16, "qf") for _ in range(2)]
            qb = [sb([128, 2, 128], BF16, "qb") for _ in range(2)]
            sqo = [sb([128, 2, 128], BF16, "sqo") for _ in range(2)]
            rs = [sb([128, 128], F32, "rs") for _ in range(2)]
            y1 = [sb([128, 2, 128], F32, "y1") for _ in range(2)]
            rtm = [sb([128, 128], F32, "rtm") for _ in range(2)]
            ps = self.pbanks(es, 6)
            pst = self.pbanks(es, 2, BF16, 1024)

            self.dma("sp", cf[:], self.cf[:, :], [], ["cf"], "cf")
            self.dma("sp", cbt[:], self.cb[:, :], [], ["cb"], "cb")
            self.dma("sp", pvt[:], self.pvec[:, l, :], [], ["pv"], "pv")
            vsrc = self.vtok[:, h * 256:(h + 1) * 256].rearrange("(n p) c -> p n c", p=128)
            for i in range(4):
                self.dma("sp", v[:, i * 8:(i + 1) * 8, :], vsrc[:, i * 8:(i + 1) * 8, :], [], ["v"], "v")
            self.dma("sp", gst[:], self.zT[ZRG + h * 256:ZRG + (h + 1) * 256, :].rearrange("(c p) t -> p c t", p=128),
                     [], ["gst"], "gst")
            ones = cbt[:, CB_ONES:CB_ONES + 128]
            ident = cbt[:, CB_ID:CB_ID + 128]
            self.act(lg[:], pvt[:, 34:42], AF.Exp, ["pv"], ["lg"])
            self.ts("dve", lg[:], lg[:], -1.0, None, ALU.mult, None, ["lg"], ["lg"])
            lgf = lg[:, h:h + 1]
            lgb = lg[:, 4 + h:5 + h]
            self.ts("dve", tmpm[:], cf[:, CF_LF:CF_LF + 128], lgf, None, ALU.mult, None, ["cf", "lg"], ["tmpm"])
            self.stt("dve", tmpm[:], cf[:, CF_LB:CF_LB + 128], lgb, tmpm[:], ALU.mult, ALU.add,
                     ["cf", "lg", "tmpm"], ["tmpm"])
            self.act(dmask[:], tmpm[:], AF.Exp, ["tmpm"], ["dmask"], bias=cf[:, CF_B + 4:CF_B + 5])
            self.act(qft[:], cf[:, CF_JF:CF_JF + 128], AF.Exp, ["cf", "lg"], ["qft"], scale=lgf)
            self.act(qbt[:], cf[:, CF_JB:CF_JB + 128], AF.Exp, ["cf", "lg"], ["qbt"], scale=lgb)
            self.act(colv[:, 0:1], cf[:, CF_COL:CF_COL + 1], AF.Exp, ["cf", "lg"], ["colv"], scale=lgf,
                     bias=cf[:, CF_B + 4:CF_B + 5])
            self.act(colv[:, 1:2], cf[:, CF_COL + 1:CF_COL + 2], AF.Exp, ["cf", "lg", "colv"], ["colv"], scale=lgb,
                     bias=cf[:, CF_B + 4:CF_B + 5])
            self.act(colv[:, 2:3], cf[:, CF_COL + 2:CF_COL + 3], AF.Exp, ["cf", "lg", "colv"], ["colv"], scale=lgf)
            self.act(colv[:, 3:4], cf[:, CF_COL + 2:CF_COL + 3], AF.Exp, ["cf", "lg", "colv"], ["colv"], scale=lgb)
            it = 0
            for which, zr, dst in (("q", ZRQ, qrot), ("k", ZRK, krot)):
                for pc in range(4):
                    s = it % 2
                    it += 1
                    t0 = pc * 1024
                    self.dma("sp", raw[s][:], self.zT[zr + h * 256:zr + (h + 1) * 256, t0:t0 + 1024]
                             .rearrange("(c p) t -> p c t", p=128), [], [("raw", s)], ("raw", s))
                    self.dma("sp", cs[s][:, 0, :], self.rcos[:, t0:t0 + 1024], [], [("cs", s, 0)], ("cs", s, 0))
                    self.dma("sp", cs[s][:, 1, :], self.rsin[:, t0:t0 + 1024], [], [("cs", s, 1)], ("cs", s, 1))
                    x1 = raw[s][:, 0, :]
                    x2 = raw[s][:, 1, :]
                    co = cs[s][:, 0, :]
                    si = cs[s][:, 1, :]
                    rk_ = [("raw", s), ("cs", s, 0), ("cs", s, 1)]
                    self.tt("dve", tmp[0][:], x1, co, ALU.mult, rk_, [("rt", 0)])
                    self.tt("pool", tmp[1][:], x2, si, ALU.mult, rk_, [("rt", 1)])
                    self.tt("dve", dst[:, 0, t0:t0 + 1024], tmp[0][:], tmp[1][:], ALU.subtract,
                            [("rt", 0), ("rt", 1)], [(which, pc, 0)])
                    self.tt("pool", tmp[2][:], x2, co, ALU.mult, rk_, [("rt", 2)])
                    self.tt("dve", tmp[3][:], x1, si, ALU.mult, rk_, [("rt", 3)])
                    self.tt("pool", dst[:, 1, t0:t0 + 1024], tmp[2][:], tmp[3][:], ALU.add,
                            [("rt", 2), ("rt", 3)], [(which, pc, 1)])
            qkeys = lambda n: [("q", n // 8, 0), ("q", n // 8, 1)]
            kkeys = lambda n: [("k", n // 8, 0), ("k", n // 8, 1)]
            for n4 in range(NCHK // 4):
                b = n4 % 2
                for i in range(4):
                    n = n4 * 4 + i
                    for dc in range(2):
                        o = pst[b][:, (i * 2 + dc) * 128:(i * 2 + dc + 1) * 128]
                        self.S.op("pe", lambda e, o=o, n=n, dc=dc: e.transpose(o, krot[:, dc, n * 128:(n + 1) * 128], ident),
                                  reads=kkeys(n) + ["cb"], writes=[("pst", b)], is_mm=True, inc=(i == 3 and dc == 1))
                self.cp("act" if n4 % 2 else "dve", ktok[:, n4 * 4:(n4 + 1) * 4, :].rearrange("p n c -> p (n c)"),
                        pst[b][:, :], [("pst", b)], [("ktok", n4)])
            self.memset("dve", St[0][:], 0.0, [("St", 0)])
            for n in range(NCHK - 1, -1, -1):
                cur = (n + 1) % 2
                nxt = n % 2
                self.cp("act", sbb[:, n].rearrange("p a b -> p (a b)"), St[cur][:].rearrange("p a b -> p (a b)"),
                        [("St", cur)], [("sbb", n)])
                if n == 0:
                    break
                s = n % 2
                self.act(vw[s][:], v[:, n, :], AF.Copy, ["v", "colv"], [("vw", s)], scale=colv[:, 1:2])
                b = 4 + n % 2
                for dc in range(2):
                    self.mm(ps[b][:, dc * 256:(dc + 1) * 256], ktok[:, n, dc * 128:(dc + 1) * 128], vw[s][:], True, True,
                            [("ktok", n // 4), ("vw", s)], [("ps", b)], inc=(dc == 1))
                self.stt("dve", St[nxt][:].rearrange("p a b -> p (a b)"), St[cur][:].rearrange("p a b -> p (a b)"),
                         colv[:, 3:4], ps[b][:], ALU.mult, ALU.add, [("St", cur), ("ps", b), "colv"], [("St", nxt)])
            self.memset("dve", St[0][:], 0.0, [("St", 0)])
            def stage_a(n):
                s = n % 2
                cur = n % 2
                nxt = (n + 1) % 2
                tk = slice(n * 128, (n + 1) * 128)
                self.cp("act", sfc[s][:].rearrange("p a b -> p (a b)"), St[cur][:].rearrange("p a b -> p (a b)"),
                        [("St", cur)], [("sfc", s)])
                b_s = 0 + s
                b_o = 2 + s
                b_k = 4
                b_n = 5
                for dc in range(2):
                    self.mm(ps[b_s][:, 0:128], krot[:, dc, tk], qrot[:, dc, tk], dc == 0, dc == 1,
                            kkeys(n) + qkeys(n), [("ps", b_s)])
                self.tt("dve", sT[s][:], ps[b_s][:, 0:128], dmask[:], ALU.mult, [("ps", b_s), "dmask"], [("sT", s)])
                self.tt("dve", qf[s][:], qrot[:, :, tk], qft[:].unsqueeze(1).to_broadcast([128, 2, 128]), ALU.mult,
                        qkeys(n) + ["qft"], [("qf", s)])
                self.tt("dve", qb[s][:], qrot[:, :, tk], qbt[:].unsqueeze(1).to_broadcast([128, 2, 128]), ALU.mult,
                        qkeys(n) + ["qbt"], [("qb", s)])
                for ec in range(2):
                    o = ps[b_o][:, ec * 128:(ec + 1) * 128]
                    es_ = slice(ec * 128, (ec + 1) * 128)
                    self.mm(o, v[:, n, es_], sT[s][:], True, False, ["v", ("sT", s)], [("ps", b_o)])
                    for dc in range(2):
                        self.mm(o, sfc[s][:, dc, es_], qf[s][:, dc, :], False, False, [("sfc", s), ("qf", s)], [("ps", b_o)])
                    for dc in range(2):
                        self.mm(o, sbb[:, n, dc, es_], qb[s][:, dc, :], False, dc == 1, [("sbb", n), ("qb", s)],
                                [("ps", b_o)], inc=(dc == 1 and ec == 1))
                if n < NCHK - 1:
                    self.act(vw[s][:], v[:, n, :], AF.Copy, ["v", "colv"], [("vw", s)], scale=colv[:, 0:1])
                    for dc in range(2):
                        self.mm(ps[b_k][:, dc * 256:(dc + 1) * 256], ktok[:, n, dc * 128:(dc + 1) * 128], vw[s][:], True, True,
                                [("ktok", n // 4), ("vw", s)], [("ps", b_k)], inc=(dc == 1))
                    self.stt("dve", St[nxt][:].rearrange("p a b -> p (a b)"), St[cur][:].rearrange("p a b -> p (a b)"),
                             colv[:, 2:3], ps[b_k][:], ALU.mult, ALU.add, [("St", cur), ("ps", b_k), "colv"], [("St", nxt)])
            def stage_b(n):
                s = n % 2
                b_o = 2 + s
                b_n = 5
                tk = slice(n * 128, (n + 1) * 128)
                self.act(sqo[s][:].rearrange("p a b -> p (a b)"), ps[b_o][:, 0:256], AF.Square, [("ps", b_o)], [("sqo", s)])
                for ec in range(2):
                    self.mm(ps[b_n][:, 0:128], ones, sqo[s][:, ec, :], ec == 0, ec == 1, [("sqo", s), "cb"], [("ps", b_n)])
                self.rsqrt(rs[s][:], ps[b_n][:, 0:128], cf[:, CF_B + 1:CF_B + 2], rtm[s][:], [("ps", b_n), "cf"], ("rtm", s), ("rs", s))
                self.stt("dve", y1[s][:], ps[b_o][:, 0:256].rearrange("p (a b) -> p a b", a=2), 16.0,
                         rs[s][:].unsqueeze(1).to_broadcast([128, 2, 128]), ALU.mult, ALU.mult,
                         [("ps", b_o), ("rs", s)], [("y1", s)])
                self.tt("dve", gst[:, :, tk], y1[s][:], gst[:, :, tk], ALU.mult, [("y1", s), "gst", ("go", n - 1)], [("go", n)])
            for n in range(NCHK):
                stage_a(n)
                if n >= 1:
                    stage_b(n - 1)
            stage_b(NCHK - 1)
            self.dma("sp", self.bin[h * 256:(h + 1) * 256, :].rearrange("(c p) t -> p c t", p=128), gst[:],
                     [("go", NCHK - 1), "gst"], [], "gst")
            S.end_phase()

    def phase_pool(self, l):
        S = self.S
        PADL = 16
        W = SEQ + 32
        with contextlib.ExitStack() as es:
            sb = lambda shape, dtype, name=None: self.sb(es, shape, dtype, name)
            cf = sb([128, NCF], F32, "cf")
            pvt = sb([128, NPV], F32, "pv")
            wpl_f = sb([128, 4, 2, 256], BF16, "wpl")
            ub = [sb([128, SEQ], BF16, "ub") for _ in range(2)]
            bufs = [[sb([128, W], F32, "pb") for _ in range(3)] for _ in range(2)]
            pT = [sb([128, SEQ], BF16, "pT") for _ in range(4)]
            pg = [sb([128, SEQ], BF16, "pg") for _ in range(2)]
            et = [sb([128, 16], F32, "et") for _ in range(2)]
            ps = self.pbanks(es)
            self.dma("sp", cf[:], self.cf[:, :], [], ["cf"], "cf")
            self.dma("sp", pvt[:], self.pvec[:, l, :], [], ["pv"], "pv")
            self.dma("pool", wpl_f[:], self.w_pl[l], [], ["wpl"], "wpl")
            for st in range(2):
                for i in range(3):
                    self.memset("pool" if st else "dve", bufs[st][i][:], 0.0, [("pb", st, i)])
            nb = 0
            for g in range(4):
                w = (2, 4, 8, 16)[g]
                for dc in range(2):
                    ct = g * 2 + dc
                    st = ct % 2
                    eng = "dve" if st == 0 else "pool"
                    U, A, B = bufs[st]
                    kU, kA, kB = ("pb", st, 0), ("pb", st, 1), ("pb", st, 2)
                    row = ZPV + ct * 128
                    self.dma("sp", ub[st][:], self.zT[row:row + 128, :], [], [("ub", st)], ("ub", st))
                    self.cp("act", U[:, PADL:PADL + SEQ], ub[st][:], [("ub", st)], [kU])
                    lo, hi = PADL - 8, PADL + SEQ + 8
                    src, ksrc = U, kU
                    dsts = [(A, kA), (B, kB)]
                    k = 1
                    di = 0
                    while 2 * k < w:
                        d, kd = dsts[di % 2]
                        self.tt(eng, d[:, lo:hi], src[:, lo:hi], src[:, lo + k:hi + k], ALU.add, [ksrc], [kd])
                        src, ksrc = d, kd
                        k *= 2
                        di += 1
                    d, kd = dsts[di % 2]
                    hw = w // 2
                    self.tt(eng, d[:, PADL:PADL + SEQ], src[:, PADL - hw:PADL - hw + SEQ], src[:, PADL:PADL + SEQ],
                            ALU.add, [ksrc], [kd])
                    pk = ("pT", g % 2, dc)
                    pdst = pT[(g % 2) * 2 + dc]
                    if True:
                        self.stt("dve", pdst[:], d[:, PADL:PADL + SEQ], 1.0 / w, U[:, PADL:PADL + SEQ], ALU.mult, ALU.subtract,
                                 [kd, kU], [pk])
                    else:
                        o_, ko_ = dsts[(di + 1) % 2]
                        self.ts(eng, o_[:, PADL:PADL + SEQ], d[:, PADL:PADL + SEQ], 1.0 / w, None, ALU.mult, None, [kd], [ko_])
                        self.tt(eng, pdst[:], o_[:, PADL:PADL + SEQ], U[:, PADL:PADL + SEQ], ALU.subtract, [ko_, kU], [pk])
                    e_ = et[st]
                    c0 = CF_PINV + g * 16
                    self.tt(eng, e_[:, 0:8], d[:, PADL:PADL + 8], cf[:, c0:c0 + 8], ALU.mult, [kd, "cf"], [("et", st)])
                    self.tt(eng, pdst[:, 0:8], e_[:, 0:8], U[:, PADL:PADL + 8], ALU.subtract, [("et", st), kU, pk], [pk])
                    self.tt(eng, e_[:, 8:16], d[:, PADL + SEQ - 8:PADL + SEQ], cf[:, c0 + 8:c0 + 16], ALU.mult,
                            [kd, "cf", ("et", st)], [("et", st)])
                    self.tt(eng, pdst[:, SEQ - 8:SEQ], e_[:, 8:16], U[:, PADL + SEQ - 8:PADL + SEQ], ALU.subtract,
                            [("et", st), kU, pk], [pk])
                for ec in range(2):
                    s = ec
                    row = ZPG + g * 256 + ec * 128
                    self.dma("sp", pg[s][:], self.zT[row:row + 128, :], [], [("pg", s)], ("pg", s))
                    for tt in range(8):
                        b = nb % 8
                        nb += 1
                        tk = slice(tt * 512, (tt + 1) * 512)
                        for dc in range(2):
                            self.mm(ps[b][:], wpl_f[:, g, dc, ec * 128:(ec + 1) * 128], pT[(g % 2) * 2 + dc][:, tk],
                                    dc == 0, dc == 1, ["wpl", ("pT", g % 2, dc)], [("ps", b)])
                        self.stt("dve", pg[s][:, tk], ps[b][:], pvt[:, 16 + g * 2 + ec:17 + g * 2 + ec], pg[s][:, tk],
                                 ALU.mult, ALU.mult, [("ps", b), "pv", ("pg", s)], [("pg", s)])
                    orow = 1024 + g * 256 + ec * 128
                    self.dma("sp", self.bin[orow:orow + 128, :], pg[s][:], [("pg", s)], [], ("pg", s))
            S.end_phase()

    def phase_att(self, l, g):
        S = self.S
        with contextlib.ExitStack() as es:
            sb = lambda shape, dtype, name=None: self.sb(es, shape, dtype, name)
            cbt = sb([128, NCB], BF16, "cb")
            pvt = sb([128, NPV], F32, "pv")
            gains = sb([128, 2], F32, "gains")
            sinkt = sb([128, 4, 128], F32, "sinkt")
            sinke = sb([128, 8], F32, "sinke")
            qn = sb([128, 5, SEQ], BF16, "qn")
            v = sb([128, NCHK, 128], BF16, "v")
            agst = sb([128, 4, SEQ], BF16, "agst")
            ps = self.pbanks(es)
            cfb = sb([128, 8], F32, "cfb")
            self.dma("sp", cfb[:], self.cf[:, CF_B:CF_B + 8], [], ["cfb"], "cfb")
            self.dma("sp", cbt[:], self.cb[:, :], [], ["cb"], "cb")
            self.dma("sp", pvt[:], self.pvec[:, l, :], [], ["pv"], "pv")
            vsrc = self.vtok[:, 1024 + g * 128:1024 + (g + 1) * 128].rearrange("(n p) c -> p n c", p=128)
            for i in range(4):
                self.dma("sp", v[:, i * 8:(i + 1) * 8, :], vsrc[:, i * 8:(i + 1) * 8, :], [], ["v"], "v")
            self.dma("sp", agst[:], self.zT[ZAG + g * 512:ZAG + (g + 1) * 512, :].rearrange("(c p) t -> p c t", p=128),
                     [], ["agst"], "agst")
            ones = cbt[:, CB_ONES:CB_ONES + 128]
            PT = cbt[:, CB_PT:CB_PT + 128]
            self.ts("dve", gains[:], pvt[:, 24:26], float(np.sqrt(128.0)), None, ALU.mult, None, ["pv"], ["gains"])
            self.act(sinke[:], pvt[:, 26:34], AF.Exp, ["pv", "cfb"], ["sinke"], bias=cfb[:, 3:4])
            self.cp("dve", sinkt[:], sinke[:, g * 4:(g + 1) * 4].unsqueeze(2).to_broadcast([128, 4, 128]),
                    ["sinke"], ["sinkt"])
            sinkrow = sb([128, 4, 128], BF16, "sinkrow")
            self.cp("dve", sinkrow[:], sinkt[:], ["sinkt"], ["sinkrow"])
            with contextlib.ExitStack() as es2:
                sb2 = lambda shape, dtype, name=None: self.sb(es2, shape, dtype, name)
                ctab = sb2([128, SEQ], F32, "ctab")
                stab = sb2([128, SEQ], F32, "stab")
                raw = [sb2([128, SEQ], BF16, "raw") for _ in range(2)]
                sq = [sb2([128, 512], BF16, "sq") for _ in range(4)]
                rs = [sb2([128, 512], F32, "rs") for _ in range(4)]
                qq = [sb2([128, 512], BF16, "qq") for _ in range(4)]
                t1 = [sb2([128, 512], F32, "t1") for _ in range(4)]
                t2 = [sb2([128, 512], F32, "t2") for _ in range(4)]
                rtm = [sb2([128, 512], F32, "rtm") for _ in range(4)]
                self.dma("sp", ctab[:], self.acos[:, :], [], ["ctab"], "ctab")
                self.dma("sp", stab[:], self.asin[:, :], [], ["stab"], "stab")
                it = 0
                for hh in range(5):
                    r = hh % 2
                    row = (ZAQ + (g * 4 + hh) * 128) if hh < 4 else (ZAK + g * 128)
                    gcol = gains[:, 0:1] if hh < 4 else gains[:, 1:2]
                    self.dma("sp", raw[r][:], self.zT[row:row + 128, :], [], [("raw", r)], ("raw", r))
                    for pc in range(8):
                        s = it % 4
                        it += 1
                        tk = slice(pc * 512, (pc + 1) * 512)
                        self.act(sq[s][:], raw[r][:, tk], AF.Square, [("raw", r)], [("sq", s)])
                        self.mm(ps[s][:], ones, sq[s][:], True, True, [("sq", s), "cb"], [("ps", s)])
                        self.rsqrt(rs[s][:], ps[s][:], cfb[:, 2:3], rtm[s][:], [("ps", s), "cfb"], ("rtm", s), ("rs", s))
                        self.stt("dve", qq[s][:], raw[r][:, tk], gcol, rs[s][:], ALU.mult, ALU.mult,
                                 [("raw", r), ("rs", s), "gains"], [("qq", s)])
                        self.mm(ps[4 + s][:], PT, qq[s][:], True, True, [("qq", s), "cb"], [("ps", 4 + s)])
                        self.tt("pool", t1[s][:], qq[s][:], ctab[:, tk], ALU.mult, [("qq", s), "ctab"], [("t1", s)])
                        self.tt("dve", t2[s][:], ps[4 + s][:], stab[:, tk], ALU.mult, [("ps", 4 + s), "stab"], [("t2", s)])
                        self.tt("pool", qn[:, hh, tk], t1[s][:], t2[s][:], ALU.add, [("t1", s), ("t2", s)], [("qn", hh, pc)])
                S.end_phase()
            with contextlib.ExitStack() as es3:
                sb3 = lambda shape, dtype, name=None: self.sb(es3, shape, dtype, name)
                pT = [sb3([128, 3, 512], BF16, "pT") for _ in range(2)]
                den = [sb3([128, 512], F32, "den") for _ in range(2)]
                o1 = [sb3([128, 512], F32, "o1") for _ in range(2)]
                MP = cbt[:, CB_MP:CB_MP + 128]
                MN = cbt[:, CB_MN:CB_MN + 128]
                scale = float(128.0 ** -0.5)
                for n in range(NCHK):
                    s = n % 2
                    segs = [m for m in (n - 1, n, n + 1) if 0 <= m < NCHK]
                    qtk = slice(n * 128, (n + 1) * 128)
                    for si, m in enumerate(segs):
                        b = s * 3 + si
                        self.mm(ps[b][:].rearrange("p (a b) -> p a b", a=4), qn[:, 4, m * 128:(m + 1) * 128], qn[:, 0:4, qtk], True, True, [], [("ps", b)])
                        self.act(pT[s][:, si, :], ps[b][:], AF.Exp, [("ps", b)], [("pT", s, si)], scale=scale, bias=cfb[:, 3:4])
                        if m != n:
                            mk = MP if m < n else MN
                            pv_ = pT[s][:, si, :].rearrange("p (a b) -> p a b", a=4)
                            self.tt("dve", pv_, pv_, mk.unsqueeze(1).to_broadcast([128, 4, 128]),
                                    ALU.mult, [("pT", s, si), "cb"], [("pT", s, si)])
                    for si, m in enumerate(segs):
                        self.mm(ps[6][:], v[:, m, :], pT[s][:, si, :], si == 0, si == len(segs) - 1,
                                [("pT", s, si), "v"], [("ps", 6)])
                    for si, m in enumerate(segs):
                        self.mm(ps[7][:], ones, pT[s][:, si, :], si == 0, False,
                                [("pT", s, si), "cb"], [("ps", 7)], inc=False)
                    self.mm(ps[7][:], cbt[0:1, CB_ONES:CB_ONES + 128], sinkrow[0:1].rearrange("p a b -> p (a b)"), False, True,
                            ["cb", "sinkrow"], [("ps", 7)])
                    self.act(den[s][:], ps[7][:], AF.Ln, [("ps", 7)], [("den", s)])
                    self.act(den[s][:], den[s][:], AF.Exp, [("den", s)], [("den", s)], scale=-1.0)
                    self.tt("dve", o1[s][:], ps[6][:], den[s][:], ALU.mult, [("ps", 6), ("den", s)], [("o1", s)])
                    self.tt("dve", agst[:, :, qtk], o1[s][:].rearrange("p (a b) -> p a b", a=4), agst[:, :, qtk], ALU.mult,
                            [("o1", s), "agst", ("ao", n - 1)], [("ao", n)])
                orow = 2048 + g * 512
                self.dma("sp", self.bin[orow:orow + 512, :].rearrange("(c p) t -> p c t", p=128), agst[:],
                         [("ao", NCHK - 1), "agst"], [], "agst")
                S.end_phase()

    def phase_merge(self, l, tq, x_cur, x_nxt):
        S = self.S
        T0 = tq * 1024
        with contextlib.ExitStack() as es:
            sb = lambda shape, dtype, name=None: self.sb(es, shape, dtype, name)
            binq = sb([128, 24, 1024], BF16, "binq")
            mT = sb([128, 16, 1024], BF16, "mT")
            wbr = [sb([128, 3, 8, 128], BF16, "wbr") for _ in range(2)]
            wbs = [sb([128, 3, 8, 128], F32, "wbs") for _ in range(2)]
            wos = [sb([128, 16, 128], F32, "wos") for _ in range(2)]
            gq = [sb([128, 3, 1024], BF16, "gq") for _ in range(2)]
            wo = [sb([128, 16, 128], BF16, "wo") for _ in range(2)]
            xq = [sb([128, 1024], F32, "xq") for _ in range(2)]
            s1 = [sb([128, 512], F32, "s1") for _ in range(2)]
            s2 = [sb([128, 512], F32, "s2") for _ in range(2)]
            ta = [sb([128, 512], F32, "ta") for _ in range(2)]
            ps = self.pbanks(es)
            for br in range(3):
                self.dma("sp", binq[:, br * 8:(br + 1) * 8, :],
                         self.bin[br * 1024:(br + 1) * 1024, T0:T0 + 1024].rearrange("(c p) t -> p c t", p=128),
                         [], [("binq", br)], ("binq", br))
            gv = self.zT[ZMG:ZMG + 6144, :].rearrange("(br jj p) t -> p br jj t", br=3, p=128)
            it = 0
            for j in range(16):
                s = j % 2
                self.dma("sp", wbs[s][:], self.w_br[l, j], [], [("wbs", s)], ("wbs", s))
                self.cp("act", wbr[s][:].rearrange("p a b c -> p (a b c)"), wbs[s][:].rearrange("p a b c -> p (a b c)"),
                        [("wbs", s)], [("wbr", s)])
                self.dma("sp", gq[s][:], gv[:, :, j, T0:T0 + 1024], [], [("gq", s)], ("gq", s))
                for tt in range(2):
                    u = it % 2
                    it += 1
                    tk = slice(tt * 512, (tt + 1) * 512)
                    bb = [(it % 2) * 3 + br for br in range(3)]
                    for br in range(3):
                        for ec in range(8):
                            self.mm(ps[bb[br]][:], wbr[s][:, br, ec, :], binq[:, br * 8 + ec, tk], ec == 0, ec == 7,
                                    [("wbr", s), ("binq", br)], [("ps", bb[br])])
                    self.tt("dve", ta[u][:], ps[bb[0]][:], gq[s][:, 0, tk], ALU.mult, [("ps", bb[0]), ("gq", s)], [("ta", u)])
                    self.tt("dve", s1[u][:], ps[bb[1]][:], gq[s][:, 1, tk], ALU.mult, [("ps", bb[1]), ("gq", s)], [("s1", u)])
                    self.tt("dve", s2[u][:], ps[bb[2]][:], gq[s][:, 2, tk], ALU.mult, [("ps", bb[2]), ("gq", s)], [("s2", u)])
                    self.tt("pool", ta[u][:], ta[u][:], s1[u][:], ALU.add, [("ta", u), ("s1", u)], [("ta", u)])
                    self.tt("dve", mT[:, j, tk], ta[u][:], s2[u][:], ALU.add, [("ta", u), ("s2", u)], [("mT", j, tt)])
            mkeys = lambda tt: [("mT", j, tt) for j in range(16)]
            for i in range(16):
                s = i % 2
                self.dma("sp", wos[s][:], self.w_o[l, i], [], [("wos", s)], ("wos", s))
                self.cp("act", wo[s][:].rearrange("p a b -> p (a b)"), wos[s][:].rearrange("p a b -> p (a b)"),
                        [("wos", s)], [("wo", s)])
                self.dma("sp", xq[s][:], x_cur[i * 128:(i + 1) * 128, T0:T0 + 1024], [], [("xq", s)], ("xq", s))
                for tt in range(2):
                    b = 6 + tt
                    tk = slice(tt * 512, (tt + 1) * 512)
                    for jc in range(16):
                        self.mm(ps[b][:], wo[s][:, jc, :], mT[:, jc, tk], jc == 0, jc == 15,
                                [("wo", s)] + mkeys(tt), [("ps", b)])
                    self.tt("dve", xq[s][:, tk], ps[b][:], xq[s][:, tk], ALU.add, [("ps", b), ("xq", s)], [("xq", s)])
                self.dma("sp", x_nxt[i * 128:(i + 1) * 128, T0:T0 + 1024], xq[s][:], [("xq", s)], [], ("xq", s))
            S.end_phase()


def _consts():
    f32 = np.float32
    t = np.arange(SEQ, dtype=f32)
    inv_r = (1.0 / (f32(10000.0) ** np.linspace(0.0, 1.0, 128, dtype=f32))).astype(f32)
    ang = (t[:, None] * inv_r[None, :]).astype(f32)
    rcos = np.ascontiguousarray(np.cos(ang).T.astype(f32))
    rsin = np.ascontiguousarray(np.sin(ang).T.astype(f32))
    inv_a = (f32(500000.0) ** (-np.arange(16, dtype=f32) / f32(16.0))).astype(f32)
    anga = (t[:, None] * inv_a[None, :]).astype(f32)
    acos = np.ones((128, SEQ), f32)
    asin = np.zeros((128, SEQ), f32)
    acos[0:16] = np.cos(anga).T
    acos[16:32] = np.cos(anga).T
    asin[0:16] = np.sin(anga).T
    asin[16:32] = np.sin(anga).T
    cf = np.zeros((128, NCF), f32)
    li = np.arange(128, dtype=f32)[:, None]
    ji = np.arange(128, dtype=f32)[None, :]
    cf[:, CF_LF:CF_LF + 128] = np.maximum(ji - li, 0)
    cf[:, CF_LB:CF_LB + 128] = np.maximum(li - ji, 0)
    cf[:, CF_JF:CF_JF + 128] = ji + 1.0
    cf[:, CF_JB:CF_JB + 128] = 128.0 - ji
    cf[:, CF_COL] = 127.0 - li[:, 0]
    cf[:, CF_COL + 1] = li[:, 0]
    cf[:, CF_COL + 2] = 128.0
    cf[:, CF_B:CF_B + 5] = np.array([2048.0 * EPS, 256.0 * EPS, 128.0 * EPS, -M0, -np.log(16.0)], f32)[None, :]
    for g, w in enumerate((2, 4, 8, 16)):
        hw = w // 2
        left = np.zeros(8, f32)
        right = np.zeros(8, f32)
        for i in range(8):
            n = i
            lo, hi = max(n - hw, 0), min(n + hw, SEQ)
            left[i] = 1.0 / (hi - lo)
            n = SEQ - 8 + i
            lo, hi = max(n - hw, 0), min(n + hw, SEQ)
            right[i] = 1.0 / (hi - lo)
        cf[:, CF_PINV + g * 16:CF_PINV + g * 16 + 8] = left[None, :]
        cf[:, CF_PINV + g * 16 + 8:CF_PINV + g * 16 + 16] = right[None, :]
    cb = np.zeros((128, NCB), f32)
    cb[:, CB_ONES:CB_ONES + 128] = 1.0
    cb[:, CB_ID:CB_ID + 128] = np.eye(128, dtype=f32)
    PT = np.zeros((128, 128), f32)
    for m in range(16):
        PT[m + 16, m] = -1.0
        PT[m, m + 16] = 1.0
    cb[:, CB_PT:CB_PT + 128] = PT
    cb[:, CB_MP:CB_MP + 128] = (li >= ji).astype(f32)
    cb[:, CB_MN:CB_MN + 128] = (li <= ji).astype(f32)
    return dict(rcos=rcos, rsin=rsin, acos=acos, asin=asin, cf=cf, cb=cb.astype(ml_dtypes.bfloat16))


def _prep_weights(inputs):
    f32 = np.float32
    w_in = inputs["w_in"]
    fmcols = np.concatenate([np.arange(z0, z0 + w) for (z0, w, _) in FM_SEGS])
    vcols = np.concatenate([np.arange(z0, z0 + w) for (z0, w) in V_SEGS])
    win_fm = np.empty((NLAYER, NFM, 128, 16, 128), f32)
    win_v = np.empty((NLAYER, 128, 16, 1280), f32)
    for l in range(NLAYER):
        a = w_in[l][:, fmcols].reshape(16, 128, NFM, 128)
        win_fm[l] = a.transpose(2, 1, 0, 3)
        b = w_in[l][:, vcols].reshape(16, 128, 1280)
        win_v[l] = b.transpose(1, 0, 2)
    wb = np.stack([inputs["w_ret"], inputs["w_pool"], inputs["w_att"]], axis=1)
    w_br = np.ascontiguousarray(wb.reshape(NLAYER, 3, 8, 128, 16, 128).transpose(0, 4, 3, 1, 2, 5))
    w_o = np.ascontiguousarray(inputs["w_out"].reshape(NLAYER, 16, 128, 16, 128).transpose(0, 3, 2, 1, 4))
    w_pl = np.ascontiguousarray(inputs["pool_w"].reshape(NLAYER, 4, 2, 128, 256).transpose(0, 3, 1, 2, 4))
    pvec = np.zeros((128, NLAYER, NPV), f32)
    for l in range(NLAYER):
        pvec[:, l, 0:16] = inputs["norm_g"][l].reshape(16, 128).T
        pvec[:, l, 16:24] = inputs["pool_scale"][l].reshape(8, 128).T
        pvec[:, l, 24] = inputs["attn_q_gain"][l]
        pvec[:, l, 25] = inputs["attn_k_gain"][l]
        pvec[:, l, 26:34] = inputs["attn_sink"][l][None, :]
        pvec[:, l, 34:38] = inputs["ret_decay_fwd"][l][None, :]
        pvec[:, l, 38:42] = inputs["ret_decay_bwd"][l][None, :]
    return dict(win_fm=win_fm, win_v=win_v, w_br=w_br, w_o=w_o, w_pl=w_pl, pvec=pvec)


def kernel(**inputs):
    inputs = {k: np.asarray(v) for k, v in inputs.items()}
    x = inputs["x"]
    B = x.shape[0]
    shared = _prep_weights(inputs)
    shared.update(_consts())
    prog = Prog()
    nc = prog.build()
    in_maps = []
    for b in range(B):
        m = dict(shared)
        m["xT"] = np.ascontiguousarray(x[b].T)
        in_maps.append(m)
    res = run_bass_kernel_spmd(nc, in_maps, core_ids=list(range(B)))
    out = np.stack([np.ascontiguousarray(r["outT"].T) for r in res.results], axis=0)
    return out.astype(np.float32)
```

```python
import contextlib
import numpy as np
import ml_dtypes
import concourse.bass as bass
import concourse.mybir as mybir
from concourse.bass_utils import run_bass_kernel_spmd

F32 = mybir.dt.float32
BF16 = mybir.dt.bfloat16
AF = mybir.ActivationFunctionType
ALU = mybir.AluOpType

SEQ = 4096
DM = 2048
NLAYER = 4
NCHK = SEQ // 128
EPS = 1e-6
M0 = 12.0

ZRQ, ZRK, ZRG, ZPV, ZPG, ZAQ, ZAK, ZAG, ZMG = 0, 1024, 2048, 3072, 4096, 5120, 6144, 6400, 7424
ZROWS = 13568
NFM = ZROWS // 128
FM_SEGS = [(0, 1024, "copy"), (1024, 1024, "copy"), (3072, 1024, "silu"), (4096, 1024, "copy"),
           (5120, 1024, "silu"), (6144, 1024, "copy"), (7168, 256, "copy"), (7680, 1024, "silu"),
           (8704, 6144, "sigmoid")]
V_SEGS = [(2048, 1024), (7424, 256)]
NPV = 42
CF_LF, CF_LB, CF_JF, CF_JB, CF_COL, CF_PINV = 0, 128, 256, 384, 512, 516
NCF = 516 + 64 + 8
CF_B = 580
CB_ONES, CB_ID, CB_PT, CB_MP, CB_MN = 0, 128, 256, 384, 512
NCB = 640


class Sched:
    ENG = ("pe", "act", "dve", "pool", "sp")

    def __init__(self, nc, sems):
        self.nc = nc
        pool = list(sems)
        self.esem = {e: pool.pop() for e in self.ENG[:4]}
        self.ecnt = {e: 0 for e in self.ENG[:4]}
        self.pe_pending = False
        self.dma_pool = pool
        self.dsem = {}
        self.dcnt = {id(s): 0 for s in pool}
        self.semobj = {id(s): s for s in pool}
        for s in self.esem.values():
            self.semobj[id(s)] = s
        self.known = {e: {} for e in self.ENG}
        self.lastw = {}
        self.readers = {}
        self.ops = {e: [] for e in self.ENG}
        self.nops = 0
        self.check = False
        self.maxops = None
        self.simval = {}

    def _dma_sem(self, key):
        if key not in self.dsem:
            idx = len(self.dsem)
            assert idx < len(self.dma_pool), "out of DMA semaphores"
            self.dsem[key] = self.dma_pool[idx]
        return self.dsem[key]

    def _need(self, eng, dep, waits, is_mm):
        semid, val, src = dep
        if src == "pe" and eng == "pe" and is_mm:
            return
        if semid in self.dcnt:
            val = self.dcnt[semid]
        if self.known[eng].get(semid, -1) >= val:
            return
        self.known[eng][semid] = val
        waits[semid] = max(waits.get(semid, 0), val)

    def op(self, eng, fn, reads=(), writes=(), dma_key=None, is_mm=False, inc=True):
        if self.maxops is not None and self.nops >= self.maxops:
            return
        waits = {}
        for k in list(reads) + list(writes):
            d = self.lastw.get(k)
            if d is not None:
                self._need(eng, d, waits, is_mm)
        for k in writes:
            for semid, (val, src) in self.readers.get(k, {}).items():
                self._need(eng, (semid, val, src), waits, is_mm)
        if dma_key is not None:
            s = self._dma_sem(dma_key)
            self.dcnt[id(s)] += 16
            ev = (id(s), self.dcnt[id(s)], "dma")
            incr = (s, 16)
        else:
            s = self.esem[eng]
            if inc:
                self.ecnt[eng] += 1
                ev = (id(s), self.ecnt[eng], eng)
                incr = (s, 1)
                if eng == "pe":
                    self.pe_pending = False
            else:
                assert eng == "pe" and is_mm
                ev = (id(s), self.ecnt[eng] + 1, eng)
                incr = None
                self.pe_pending = True
        for k in writes:
            self.lastw[k] = ev
            self.readers[k] = {}
        for k in reads:
            r = self.readers.setdefault(k, {})
            r[ev[0]] = (ev[1], ev[2])
        self.ops[eng].append((fn, [(self.semobj[sid], v) for sid, v in waits.items()], incr))
        self.nops += 1

    def _simulate(self, ops):
        val = self.simval
        pc = {e: 0 for e in self.ENG}
        progress = True
        while progress:
            progress = False
            for e in self.ENG:
                lst = ops[e]
                while pc[e] < len(lst):
                    fn, waits, incr = lst[pc[e]]
                    if any(val.get(id(s), 0) < v for s, v in waits):
                        break
                    if incr is not None:
                        val[id(incr[0])] = val.get(id(incr[0]), 0) + incr[1]
                    pc[e] += 1
                    progress = True
        stuck = {e: (pc[e], len(ops[e])) for e in self.ENG if pc[e] < len(ops[e])}
        if stuck:
            msg = []
            for e, (p, n) in stuck.items():
                fn, waits, incr = ops[e][p]
                msg.append(f"{e}: op {p}/{n} waits " + str([(s.name if hasattr(s, 'name') else id(s), v, val.get(id(s), 0)) for s, v in waits]))
            raise RuntimeError("DEADLOCK in recorded program: " + " | ".join(msg))

    def end_phase(self):
        assert self.maxops is not None or not self.pe_pending
        targets = []
        for e in self.ENG[:4]:
            if self.ecnt[e] > 0:
                targets.append((id(self.esem[e]), self.ecnt[e]))
        for sid, c in self.dcnt.items():
            if c > 0:
                targets.append((sid, c))
        for e in self.ENG:
            waits = []
            for sid, v in targets:
                if self.known[e].get(sid, -1) >= v:
                    continue
                self.known[e][sid] = v
                waits.append((self.semobj[sid], v))
            if waits:
                self.ops[e].append((None, waits, None))
        self.lastw = {}
        self.readers = {}
        self.dsem = {}
        if self.check:
            self._simulate(self.ops)
        nc = self.nc
        ops = self.ops
        self.ops = {e: [] for e in self.ENG}

        def replay(engine, lst):
            for fn, waits, incr in lst:
                for s, v in waits:
                    engine.wait_ge(s, v)
                if fn is not None:
                    ins = fn(engine)
                    if incr is not None:
                        ins.then_inc(incr[0], incr[1])

        with nc.Block() as block:
            @block.tensor
            def _(e):
                replay(e, ops["pe"])

            @block.scalar
            def _(e):
                replay(e, ops["act"])

            @block.vector
            def _(e):
                replay(e, ops["dve"])

            @block.gpsimd
            def _(e):
                replay(e, ops["pool"])

            @block.sync
            def _(e):
                replay(e, ops["sp"])


class Prog:
    def __init__(self, nlayers=NLAYER, dbg=False, stop_after=None, only=None):
        self.only = only
        self.nlayers = nlayers
        self.dbg = dbg
        self.stop_after = stop_after
        nc = bass.Bass("TRN2", target_bir_lowering=False)
        self.nc = nc
        ein = "ExternalInput"
        sk = "ExternalOutput" if dbg else "Internal"
        dt = nc.dram_tensor
        if not only:
            self.xT = dt("xT", [DM, SEQ], F32, kind=ein).ap()
            self.win_fm = dt("win_fm", [NLAYER, NFM, 128, 16, 128], F32, kind=ein).ap()
            self.win_v = dt("win_v", [NLAYER, 128, 16, 1280], F32, kind=ein).ap()
            self.w_br = dt("w_br", [NLAYER, 16, 128, 3, 8, 128], F32, kind=ein).ap()
            self.w_o = dt("w_o", [NLAYER, 16, 128, 16, 128], F32, kind=ein).ap()
        self.w_pl = dt("w_pl", [NLAYER, 128, 4, 2, 256], F32, kind=ein).ap()
        self.pvec = dt("pvec", [128, NLAYER, NPV], F32, kind=ein).ap()
        self.cf = dt("cf", [128, NCF], F32, kind=ein).ap()
        self.cb = dt("cb", [128, NCB], BF16, kind=ein).ap()
        self.rcos = dt("rcos", [128, SEQ], F32, kind=ein).ap()
        self.rsin = dt("rsin", [128, SEQ], F32, kind=ein).ap()
        self.acos = dt("acos", [128, SEQ], F32, kind=ein).ap()
        self.asin = dt("asin", [128, SEQ], F32, kind=ein).ap()
        if not only:
            self.outT = dt("outT", [DM, SEQ], F32, kind="ExternalOutput").ap()
        self.zT = dt("zT", [ZROWS, SEQ], BF16, kind=(ein if only else sk)).ap()
        self.vtok = dt("vtok", [SEQ, 1280], BF16, kind=(ein if only else sk)).ap()
        self.bin = dt("bin", [3072, SEQ], BF16, kind=sk).ap()
        if not only:
            self.xs = [dt("xs0", [DM, SEQ], F32, kind=sk).ap(), dt("xs1", [DM, SEQ], F32, kind=sk).ap()]
        self._uid = 0

    def sb(self, es, shape, dtype, name=None):
        self._uid += 1
        return es.enter_context(self.nc.sbuf_tensor(f"{name or 't'}_{self._uid}", list(shape), dtype))

    def pbanks(self, es, n=8, dtype=F32, cols=512):
        out = []
        for _ in range(n):
            self._uid += 1
            out.append(es.enter_context(self.nc.psum_tensor(f"ps_{self._uid}", [128, cols], dtype)))
        return out

    def dma(self, q, out, in_, reads, writes, key):
        self.S.op(q, lambda e: e.dma_start(out=out, in_=in_), reads=reads, writes=writes, dma_key=key)

    def mm(self, out, lhsT, rhs, start, stop, reads, writes, inc=None):
        if inc is None:
            inc = stop
        self.S.op("pe", lambda e: e.matmul(out, lhsT=lhsT, rhs=rhs, start=start, stop=stop),
                  reads=reads, writes=writes, is_mm=True, inc=inc)

    def act(self, out, in_, func, reads, writes, scale=None, bias=None):
        kw = {}
        if scale is not None:
            kw["scale"] = scale
        if bias is not None:
            kw["bias"] = bias
        self.S.op("act", lambda e: e.activation(out=out, in_=in_, func=func, **kw), reads=reads, writes=writes)

    def tt(self, eng, out, in0, in1, op, reads, writes):
        self.S.op(eng, lambda e: e.tensor_tensor(out=out, in0=in0, in1=in1, op=op), reads=reads, writes=writes)

    def ts(self, eng, out, in0, s1, s2, op0, op1, reads, writes):
        if op1 is None:
            self.S.op(eng, lambda e: e.tensor_scalar(out=out, in0=in0, scalar1=s1, scalar2=None, op0=op0),
                      reads=reads, writes=writes)
        else:
            self.S.op(eng, lambda e: e.tensor_scalar(out=out, in0=in0, scalar1=s1, scalar2=s2, op0=op0, op1=op1),
                      reads=reads, writes=writes)

    def stt(self, eng, out, in0, scalar, in1, op0, op1, reads, writes):
        self.S.op(eng, lambda e: e.scalar_tensor_tensor(out=out, in0=in0, scalar=scalar, in1=in1, op0=op0, op1=op1),
                  reads=reads, writes=writes)

    def cp(self, eng, out, in_, reads, writes):
        if eng == "act":
            self.act(out, in_, AF.Copy, reads, writes)
        else:
            self.S.op(eng, lambda e: e.tensor_copy(out=out, in_=in_), reads=reads, writes=writes)

    def rsqrt(self, out, ps_ap, bias_ap, tmp, reads, ktmp, kout):
        self.act(tmp, ps_ap, AF.Ln, reads, [ktmp], bias=bias_ap)
        self.act(out, tmp, AF.Exp, [ktmp], [kout], scale=-0.5)

    def memset(self, eng, ap, val, writes):
        self.S.op(eng, lambda e: e.memset(ap, val), writes=writes)

    def build(self):
        nc = self.nc
        with contextlib.ExitStack() as es:
            sems = [es.enter_context(nc.semaphore(f"sem{i}")) for i in range(96)]
            fins = [es.enter_context(nc.semaphore(f"fin{i}")) for i in range(4)]
            for s_ in sems + fins:
                nc.gpsimd.sem_clear(s_)
            nc.all_engine_barrier()
            self.S = Sched(nc, sems)
            self._build_body()
            with nc.Block() as block:
                @block.tensor
                def _(e):
                    e.sem_inc(fins[0], 1)

                @block.scalar
                def _(e):
                    e.sem_inc(fins[1], 1)

                @block.vector
                def _(e):
                    e.sem_inc(fins[2], 1)

                @block.sync
                def _(e):
                    e.sem_inc(fins[3], 1)

                @block.gpsimd
                def _(e):
                    for f_ in fins:
                        e.wait_ge(f_, 1)
                    for s_ in sems + fins:
                        e.sem_clear(s_)
            nc.all_engine_barrier()
        return nc

    def _build_body(self):
        nc = self.nc
        if True:
            if self.only:
                if self.only.startswith("ret"):
                    for h in ([int(self.only[3:])] if len(self.only) > 3 else range(4)):
                        self.phase_ret(0, h)
                elif self.only == "pool":
                    self.phase_pool(0)
                elif self.only == "att":
                    for g in range(2):
                        self.phase_att(0, g)
                return
            for l in range(self.nlayers):
                x_cur = self.xT if l == 0 else self.xs[(l - 1) % 2]
                x_nxt = self.outT if l == self.nlayers - 1 else self.xs[l % 2]
                for th in range(2):
                    self.phase_p1(l, th, x_cur)
                if self.stop_after == "p1":
                    break
                for h in range(4):
                    self.phase_ret(l, h)
                if self.stop_after == "ret":
                    break
                self.phase_pool(l)
                if self.stop_after == "pool":
                    break
                for g in range(2):
                    self.phase_att(l, g)
                if self.stop_after == "att":
                    break
                for tq in range(4):
                    self.phase_merge(l, tq, x_cur, x_nxt)

    def phase_p1(self, l, th, x_cur):
        S = self.S
        T0 = th * 2048
        with contextlib.ExitStack() as es:
            hT = self.sb(es, [128, 16, 2048], BF16, "hT")
            xin = [self.sb(es, [128, 16, 256], F32, "xin") for _ in range(2)]
            sq = [self.sb(es, [128, 16, 256], BF16, "sq") for _ in range(2)]
            rstd = [self.sb(es, [128, 256], F32, "rstd") for _ in range(2)]
            gcol = self.sb(es, [128, 16], F32, "gcol")
            ones = self.sb(es, [128, 128], BF16, "ones")
            wch = [self.sb(es, [128, 16, 128], BF16, "wch") for _ in range(3)]
            zst = [self.sb(es, [128, 2048], BF16, "zst") for _ in range(3)]
            wv = [self.sb(es, [128, 16, 256], BF16, "wv") for _ in range(2)]
            vst = [self.sb(es, [128, 16, 256], BF16, "vst") for _ in range(2)]
            ps = self.pbanks(es)

            cfb = self.sb(es, [128, 8], F32, "cfb")
            rtmp = [self.sb(es, [128, 256], F32, "rtmp") for _ in range(2)]
            self.dma("sp", cfb[:], self.cf[:, CF_B:CF_B + 8], [], ["cfb"], "cfb")
            self.dma("sp", gcol[:], self.pvec[:, l, 0:16], [], ["gcol"], "gcol")
            self.dma("sp", ones[:], self.cb[:, CB_ONES:CB_ONES + 128], [], ["ones"], "ones")
            self.ts("dve", gcol[:], gcol[:], float(np.sqrt(2048.0)), None, ALU.mult, None, ["gcol"], ["gcol"])
            xv = x_cur.rearrange("(kc p) t -> p kc t", p=128)
            for t in range(8):
                s = t % 2
                self.dma("sp", xin[s][:], xv[:, :, T0 + t * 256:T0 + (t + 1) * 256], [], [("xin", s)], ("xin", s))
                self.act(sq[s][:].rearrange("p k t -> p (k t)"), xin[s][:].rearrange("p k t -> p (k t)"),
                         AF.Square, [("xin", s)], [("sq", s)])
                for kc in range(16):
                    self.mm(ps[s][:, 0:256], ones[:], sq[s][:, kc, :], kc == 0, kc == 15,
                            [("sq", s), "ones"], [("ps", s)])
                self.rsqrt(rstd[s][:], ps[s][:, 0:256], cfb[:, 0:1], rtmp[s][:], [("ps", s), "cfb"], ("rtmp", s), ("rstd", s))
                self.tt("pool", xin[s][:], xin[s][:], rstd[s][:].unsqueeze(1).to_broadcast([128, 16, 256]),
                        ALU.mult, [("xin", s), ("rstd", s)], [("xin", s)])
                self.tt("dve", hT[:, :, t * 256:(t + 1) * 256], xin[s][:],
                        gcol[:].unsqueeze(2).to_broadcast([128, 16, 256]), ALU.mult,
                        [("xin", s), "gcol"], [("hT", t)])
            hkeys = [("hT", t) for t in range(8)]
            nb = 0
            for c in range(NFM):
                s = c % 3
                fn = "copy"
                row = c * 128
                for (z0, w, f), r0 in zip(FM_SEGS, (ZRQ, ZRK, ZRG, ZPV, ZPG, ZAQ, ZAK, ZAG, ZMG)):
                    if r0 <= row < r0 + w:
                        fn = f
                self.dma("pool", wch[s][:], self.win_fm[l, c], [], [("wch", s)], ("wch", s))
                for tt in range(4):
                    b = 2 + nb % 6
                    nb += 1
                    for kc in range(16):
                        self.mm(ps[b][:], wch[s][:, kc, :], hT[:, kc, tt * 512:(tt + 1) * 512], kc == 0, kc == 15,
                                [("wch", s), ("hT", 2 * tt), ("hT", 2 * tt + 1)], [("ps", b)])
                    o = zst[s][:, tt * 512:(tt + 1) * 512]
                    if fn == "copy":
                        eng = "dve" if (tt % 2 == 0) else "act"
                        self.cp(eng, o, ps[b][:], [("ps", b)], [("zst", s, tt)])
                    else:
                        self.act(o, ps[b][:], AF.Silu if fn == "silu" else AF.Sigmoid, [("ps", b)], [("zst", s, tt)])
                self.dma("sp", self.zT[row:row + 128, T0:T0 + 2048], zst[s][:],
                         [("zst", s, i) for i in range(4)], [], ("zst", s))
            vv = self.vtok[T0:T0 + 2048, :].rearrange("(k p) c -> p k c", p=128)
            for cpi in range(5):
                s = cpi % 2
                self.dma("pool", wv[s][:], self.win_v[l, :, :, cpi * 256:(cpi + 1) * 256], [], [("wv", s)], ("wv", s))
                for tk in range(16):
                    b = 2 + nb % 6
                    nb += 1
                    for kc in range(16):
                        self.mm(ps[b][:, 0:256], hT[:, kc, tk * 128:(tk + 1) * 128], wv[s][:, kc, :], kc == 0, kc == 15,
                                [("wv", s), ("hT", tk // 2)], [("ps", b)])
                    eng = "dve" if (tk % 2 == 0) else "act"
                    self.cp(eng, vst[s][:, tk, :], ps[b][:, 0:256], [("ps", b)], [("vst", s, tk)])
                self.dma("sp", vv[:, :, cpi * 256:(cpi + 1) * 256], vst[s][:],
                         [("vst", s, i) for i in range(16)], [], ("vst", s))
            S.end_phase()

    def phase_ret(self, l, h):
        S = self.S
        with contextlib.ExitStack() as es:
            sb = lambda shape, dtype, name=None: self.sb(es, shape, dtype, name)
            cf = sb([128, NCF], F32, "cf")
            cbt = sb([128, NCB], BF16, "cb")
            pvt = sb([128, NPV], F32, "pv")
            lg = sb([128, 8], F32, "lg")
            dmask = sb([128, 128], F32, "dmask")
            tmpm = sb([128, 128], F32, "tmpm")
            qft = sb([128, 128], F32, "qft")
            qbt = sb([128, 128], F32, "qbt")
            colv = sb([128, 4], F32, "colv")
            qrot = sb([128, 2, SEQ], BF16, "qrot")
            krot = sb([128, 2, SEQ], BF16, "krot")
            ktok = sb([128, NCHK, 256], BF16, "ktok")
            v = sb([128, NCHK, 256], BF16, "v")
            sbb = sb([128, NCHK, 2, 256], BF16, "sbb")
            gst = sb([128, 2, SEQ], BF16, "gst")
            raw = [sb([128, 2, 1024], BF16, "raw") for _ in range(2)]
            cs = [sb([128, 2, 1024], F32, "cs") for _ in range(2)]
            tmp = [sb([128, 1024], F32, "rt") for _ in range(4)]
            St = [sb([128, 2, 256], F32, "St") for _ in range(2)]
            sfc = [sb([128, 2, 256], BF16, "sfc") for _ in range(2)]
            vwa = sb([128, NCHK, 256], BF16, "vwa")
            sT = [sb([128, 128], BF16, "sT") for _ in range(2)]
            qf = [sb([128, 2, 128], BF16, "qf") for _ in range(2)]
            qb = [sb([128, 2, 128], BF16, "qb") for _ in range(2)]
            sqo = [sb([128, 2, 128], BF16, "sqo") for _ in range(2)]
            rs = [sb([128, 128], F32, "rs") for _ in range(2)]
            y1 = [sb([128, 2, 128], F32, "y1") for _ in range(2)]
            rtm = [sb([128, 128], F32, "rtm") for _ in range(2)]
            ps = self.pbanks(es, 6)
            pst = self.pbanks(es, 2, BF16, 1024)

            self.dma("sp", cf[:], self.cf[:, :], [], ["cf"], "cf")
            self.dma("sp", cbt[:], self.cb[:, :], [], ["cb"], "cb")
            self.dma("sp", pvt[:], self.pvec[:, l, :], [], ["pv"], "pv")
            vsrc = self.vtok[:, h * 256:(h + 1) * 256].rearrange("(n p) c -> p n c", p=128)
            for i in range(4):
                self.dma("sp", v[:, i * 8:(i + 1) * 8, :], vsrc[:, i * 8:(i + 1) * 8, :], [], ["v"], "v")
            self.dma("sp", gst[:], self.zT[ZRG + h * 256:ZRG + (h + 1) * 256, :].rearrange("(c p) t -> p c t", p=128),
                     [], ["gst"], "gst")
            ones = cbt[:, CB_ONES:CB_ONES + 128]
            ident = cbt[:, CB_ID:CB_ID + 128]
            self.act(lg[:], pvt[:, 34:42], AF.Exp, ["pv"], ["lg"])
            self.ts("dve", lg[:], lg[:], -1.0, None, ALU.mult, None, ["lg"], ["lg"])
            lgf = lg[:, h:h + 1]
            lgb = lg[:, 4 + h:5 + h]
            self.ts("dve", tmpm[:], cf[:, CF_LF:CF_LF + 128], lgf, None, ALU.mult, None, ["cf", "lg"], ["tmpm"])
            self.stt("dve", tmpm[:], cf[:, CF_LB:CF_LB + 128], lgb, tmpm[:], ALU.mult, ALU.add,
                     ["cf", "lg", "tmpm"], ["tmpm"])
            self.act(dmask[:], tmpm[:], AF.Exp, ["tmpm"], ["dmask"], bias=cf[:, CF_B + 4:CF_B + 5])
            self.act(qft[:], cf[:, CF_JF:CF_JF + 128], AF.Exp, ["cf", "lg"], ["qft"], scale=lgf)
            self.act(qbt[:], cf[:, CF_JB:CF_JB + 128], AF.Exp, ["cf", "lg"], ["qbt"], scale=lgb)
            self.act(colv[:, 0:1], cf[:, CF_COL:CF_COL + 1], AF.Exp, ["cf", "lg"], ["colv"], scale=lgf,
                     bias=cf[:, CF_B + 4:CF_B + 5])
            self.act(colv[:, 1:2], cf[:, CF_COL + 1:CF_COL + 2], AF.Exp, ["cf", "lg", "colv"], ["colv"], scale=lgb,
                     bias=cf[:, CF_B + 4:CF_B + 5])
            self.act(colv[:, 2:3], cf[:, CF_COL + 2:CF_COL + 3], AF.Exp, ["cf", "lg", "colv"], ["colv"], scale=lgf)
            self.act(colv[:, 3:4], cf[:, CF_COL + 2:CF_COL + 3], AF.Exp, ["cf", "lg", "colv"], ["colv"], scale=lgb)
            it = 0
            for which, zr, dst in (("q", ZRQ, qrot), ("k", ZRK, krot)):
                for pc in range(4):
                    s = it % 2
                    it += 1
                    t0 = pc * 1024
                    self.dma("sp", raw[s][:], self.zT[zr + h * 256:zr + (h + 1) * 256, t0:t0 + 1024]
                             .rearrange("(c p) t -> p c t", p=128), [], [("raw", s)], ("raw", s))
                    self.dma("sp", cs[s][:, 0, :], self.rcos[:, t0:t0 + 1024], [], [("cs", s, 0)], ("cs", s, 0))
                    self.dma("sp", cs[s][:, 1, :], self.rsin[:, t0:t0 + 1024], [], [("cs", s, 1)], ("cs", s, 1))
                    x1 = raw[s][:, 0, :]
                    x2 = raw[s][:, 1, :]
                    co = cs[s][:, 0, :]
                    si = cs[s][:, 1, :]
                    rk_ = [("raw", s), ("cs", s, 0), ("cs", s, 1)]
                    self.tt("dve", tmp[0][:], x1, co, ALU.mult, rk_, [("rt", 0)])
                    self.tt("pool", tmp[1][:], x2, si, ALU.mult, rk_, [("rt", 1)])
                    self.tt("dve", dst[:, 0, t0:t0 + 1024], tmp[0][:], tmp[1][:], ALU.subtract,
                            [("rt", 0), ("rt", 1)], [(which, pc, 0)])
                    self.tt("pool", tmp[2][:], x2, co, ALU.mult, rk_, [("rt", 2)])
                    self.tt("dve", tmp[3][:], x1, si, ALU.mult, rk_, [("rt", 3)])
                    self.tt("dve", dst[:, 1, t0:t0 + 1024], tmp[2][:], tmp[3][:], ALU.add,
                            [("rt", 2), ("rt", 3)], [(which, pc, 1)])
            qkeys = lambda n: [("q", n // 8, 0), ("q", n // 8, 1)]
            kkeys = lambda n: [("k", n // 8, 0), ("k", n // 8, 1)]
            for n4 in range(NCHK // 4):
                b = n4 % 2
                for i in range(4):
                    n = n4 * 4 + i
                    for dc in range(2):
                        o = pst[b][:, (i * 2 + dc) * 128:(i * 2 + dc + 1) * 128]
                        self.S.op("pe", lambda e, o=o, n=n, dc=dc: e.transpose(o, krot[:, dc, n * 128:(n + 1) * 128], ident),
                                  reads=kkeys(n) + ["cb"], writes=[("pst", b)], is_mm=True, inc=(i == 3 and dc == 1))
                self.cp("act" if n4 % 2 else "dve", ktok[:, n4 * 4:(n4 + 1) * 4, :].rearrange("p n c -> p (n c)"),
                        pst[b][:, :], [("pst", b)], [("ktok", n4)])
            self.memset("dve", St[0][:], 0.0, [("St", 0)])
            self.act(vwa[:].rearrange("p n c -> p (n c)"), v[:].rearrange("p n c -> p (n c)"), AF.Copy,
                     ["v", "colv"], ["vwa"], scale=colv[:, 1:2])
            for n in range(NCHK - 1, -1, -1):
                cur = (n + 1) % 2
                nxt = n % 2
                self.cp("act", sbb[:, n].rearrange("p a b -> p (a b)"), St[cur][:].rearrange("p a b -> p (a b)"),
                        [("St", cur)], [("sbb", n)])
                if n == 0:
                    break
                b = 4 + n % 2
                for dc in range(2):
                    self.mm(ps[b][:, dc * 256:(dc + 1) * 256], ktok[:, n, dc * 128:(dc + 1) * 128], vwa[:, n, :], True, True,
                            [("ktok", n // 4), "vwa"], [("ps", b)], inc=(dc == 1))
                self.stt("dve", St[nxt][:].rearrange("p a b -> p (a b)"), St[cur][:].rearrange("p a b -> p (a b)"),
                         colv[:, 3:4], ps[b][:], ALU.mult, ALU.add, [("St", cur), ("ps", b), "colv"], [("St", nxt)])
            self.memset("dve", St[0][:], 0.0, [("St", 0)])
            self.act(vwa[:].rearrange("p n c -> p (n c)"), v[:].rearrange("p n c -> p (n c)"), AF.Copy,
                     ["v", "colv"], ["vwa"], scale=colv[:, 0:1])
            def stage_a(n):
                s = n % 2
                cur = n % 2
                nxt = (n + 1) % 2
                tk = slice(n * 128, (n + 1) * 128)
                self.cp("act", sfc[s][:].rearrange("p a b -> p (a b)"), St[cur][:].rearrange("p a b -> p (a b)"),
                        [("St", cur)], [("sfc", s)])
                b_s = 0 + s
                b_o = 2 + s
                b_k = 4
                b_n = 5
                for dc in range(2):
                    self.mm(ps[b_s][:, 0:128], krot[:, dc, tk], qrot[:, dc, tk], dc == 0, dc == 1,
                            kkeys(n) + qkeys(n), [("ps", b_s)])
                self.tt("dve", sT[s][:], ps[b_s][:, 0:128], dmask[:], ALU.mult, [("ps", b_s), "dmask"], [("sT", s)])
                self.tt("dve", qf[s][:], qrot[:, :, tk], qft[:].unsqueeze(1).to_broadcast([128, 2, 128]), ALU.mult,
                        qkeys(n) + ["qft"], [("qf", s)])
                self.tt("dve", qb[s][:], qrot[:, :, tk], qbt[:].unsqueeze(1).to_broadcast([128, 2, 128]), ALU.mult,
                        qkeys(n) + ["qbt"], [("qb", s)])
                for ec in range(2):
                    o = ps[b_o][:, ec * 128:(ec + 1) * 128]
                    es_ = slice(ec * 128, (ec + 1) * 128)
                    self.mm(o, v[:, n, es_], sT[s][:], True, False, ["v", ("sT", s)], [("ps", b_o)])
                    for dc in range(2):
                        self.mm(o, sfc[s][:, dc, es_], qf[s][:, dc, :], False, False, [("sfc", s), ("qf", s)], [("ps", b_o)])
                    for dc in range(2):
                        self.mm(o, sbb[:, n, dc, es_], qb[s][:, dc, :], False, dc == 1, [("sbb", n), ("qb", s)],
                                [("ps", b_o)], inc=(dc == 1 and ec == 1))
                if n < NCHK - 1:
                    for dc in range(2):
                        self.mm(ps[b_k][:, dc * 256:(dc + 1) * 256], ktok[:, n, dc * 128:(dc + 1) * 128], vwa[:, n, :], True, True,
                                [("ktok", n // 4), "vwa"], [("ps", b_k)], inc=(dc == 1))
                    self.stt("dve", St[nxt][:].rearrange("p a b -> p (a b)"), St[cur][:].rearrange("p a b -> p (a b)"),
                             colv[:, 2:3], ps[b_k][:], ALU.mult, ALU.add, [("St", cur), ("ps", b_k), "colv"], [("St", nxt)])
            def stage_b(n):
                s = n % 2
                b_o = 2 + s
                b_n = 5
                tk = slice(n * 128, (n + 1) * 128)
                self.act(sqo[s][:].rearrange("p a b -> p (a b)"), ps[b_o][:, 0:256], AF.Square, [("ps", b_o)], [("sqo", s)])
                for ec in range(2):
                    self.mm(ps[b_n][:, 0:128], ones, sqo[s][:, ec, :], ec == 0, ec == 1, [("sqo", s), "cb"], [("ps", b_n)])
                self.rsqrt(rs[s][:], ps[b_n][:, 0:128], cf[:, CF_B + 1:CF_B + 2], rtm[s][:], [("ps", b_n), "cf"], ("rtm", s), ("rs", s))
                self.stt("dve", y1[s][:], ps[b_o][:, 0:256].rearrange("p (a b) -> p a b", a=2), 16.0,
                         rs[s][:].unsqueeze(1).to_broadcast([128, 2, 128]), ALU.mult, ALU.mult,
                         [("ps", b_o), ("rs", s)], [("y1", s)])
                self.tt("dve", gst[:, :, tk], y1[s][:], gst[:, :, tk], ALU.mult, [("y1", s), "gst", ("go", n - 1)], [("go", n)])
            for n in range(NCHK):
                stage_a(n)
                if n >= 1:
                    stage_b(n - 1)
            stage_b(NCHK - 1)
            self.dma("sp", self.bin[h * 256:(h + 1) * 256, :].rearrange("(c p) t -> p c t", p=128), gst[:],
                     [("go", NCHK - 1), "gst"], [], "gst")
            S.end_phase()

    def phase_pool(self, l):
        S = self.S
        PADL = 16
        W = SEQ + 32
        with contextlib.ExitStack() as es:
            sb = lambda shape, dtype, name=None: self.sb(es, shape, dtype, name)
            cf = sb([128, NCF], F32, "cf")
            pvt = sb([128, NPV], F32, "pv")
            wpl_f = sb([128, 4, 2, 256], BF16, "wpl")
            ub = [sb([128, SEQ], BF16, "ub") for _ in range(2)]
            bufs = [[sb([128, W], F32, "pb") for _ in range(3)] for _ in range(2)]
            pT = [sb([128, SEQ], BF16, "pT") for _ in range(4)]
            pg = [sb([128, SEQ], BF16, "pg") for _ in range(2)]
            et = [sb([128, 16], F32, "et") for _ in range(2)]
            ps = self.pbanks(es)
            self.dma("sp", cf[:], self.cf[:, :], [], ["cf"], "cf")
            self.dma("sp", pvt[:], self.pvec[:, l, :], [], ["pv"], "pv")
            self.dma("pool", wpl_f[:], self.w_pl[l], [], ["wpl"], "wpl")
            for st in range(2):
                for i in range(3):
                    self.memset("pool" if st else "dve", bufs[st][i][:], 0.0, [("pb", st, i)])
            nb = 0
            for g in range(4):
                w = (2, 4, 8, 16)[g]
                for dc in range(2):
                    ct = g * 2 + dc
                    st = ct % 2
                    eng = "dve" if st == 0 else "pool"
                    U, A, B = bufs[st]
                    kU, kA, kB = ("pb", st, 0), ("pb", st, 1), ("pb", st, 2)
                    row = ZPV + ct * 128
                    self.dma("sp", ub[st][:], self.zT[row:row + 128, :], [], [("ub", st)], ("ub", st))
                    self.cp("act", U[:, PADL:PADL + SEQ], ub[st][:], [("ub", st)], [kU])
                    lo, hi = PADL - 8, PADL + SEQ + 8
                    src, ksrc = U, kU
                    dsts = [(A, kA), (B, kB)]
                    k = 1
                    di = 0
                    while 2 * k < w:
                        d, kd = dsts[di % 2]
                        self.tt(eng, d[:, lo:hi], src[:, lo:hi], src[:, lo + k:hi + k], ALU.add, [ksrc], [kd])
                        src, ksrc = d, kd
                        k *= 2
                        di += 1
                    d, kd = dsts[di % 2]
                    hw = w // 2
                    self.tt(eng, d[:, PADL:PADL + SEQ], src[:, PADL - hw:PADL - hw + SEQ], src[:, PADL:PADL + SEQ],
                            ALU.add, [ksrc], [kd])
                    pk = ("pT", g % 2, dc)
                    pdst = pT[(g % 2) * 2 + dc]
                    if True:
                        self.stt("dve", pdst[:], d[:, PADL:PADL + SEQ], 1.0 / w, U[:, PADL:PADL + SEQ], ALU.mult, ALU.subtract,
                                 [kd, kU], [pk])
                    else:
                        o_, ko_ = dsts[(di + 1) % 2]
                        self.ts(eng, o_[:, PADL:PADL + SEQ], d[:, PADL:PADL + SEQ], 1.0 / w, None, ALU.mult, None, [kd], [ko_])
                        self.tt(eng, pdst[:], o_[:, PADL:PADL + SEQ], U[:, PADL:PADL + SEQ], ALU.subtract, [ko_, kU], [pk])
                    e_ = et[st]
                    c0 = CF_PINV + g * 16
                    self.tt(eng, e_[:, 0:8], d[:, PADL:PADL + 8], cf[:, c0:c0 + 8], ALU.mult, [kd, "cf"], [("et", st)])
                    self.tt(eng, pdst[:, 0:8], e_[:, 0:8], U[:, PADL:PADL + 8], ALU.subtract, [("et", st), kU, pk], [pk])
                    self.tt(eng, e_[:, 8:16], d[:, PADL + SEQ - 8:PADL + SEQ], cf[:, c0 + 8:c0 + 16], ALU.mult,
                            [kd, "cf", ("et", st)], [("et", st)])
                    self.tt(eng, pdst[:, SEQ - 8:SEQ], e_[:, 8:16], U[:, PADL + SEQ - 8:PADL + SEQ], ALU.subtract,
                            [("et", st), kU, pk], [pk])
                for ec in range(2):
                    row = ZPG + g * 256 + ec * 128
                    self.dma("sp", pg[ec][:], self.zT[row:row + 128, :], [], [("pg", ec)], ("pg", ec))
                for ec in range(2):
                    s = ec
                    for tt in range(8):
                        b = nb % 8
                        nb += 1
                        tk = slice(tt * 512, (tt + 1) * 512)
                        for dc in range(2):
                            self.mm(ps[b][:], wpl_f[:, g, dc, ec * 128:(ec + 1) * 128], pT[(g % 2) * 2 + dc][:, tk],
                                    dc == 0, dc == 1, ["wpl", ("pT", g % 2, dc)], [("ps", b)])
                        self.stt("dve", pg[s][:, tk], ps[b][:], pvt[:, 16 + g * 2 + ec:17 + g * 2 + ec], pg[s][:, tk],
                                 ALU.mult, ALU.mult, [("ps", b), "pv", ("pg", s)], [("pg", s)])
                    orow = 1024 + g * 256 + ec * 128
                    self.dma("sp", self.bin[orow:orow + 128, :], pg[s][:], [("pg", s)], [], ("pg", s))
            S.end_phase()

    def phase_att(self, l, g):
        S = self.S
        with contextlib.ExitStack() as es:
            sb = lambda shape, dtype, name=None: self.sb(es, shape, dtype, name)
            cbt = sb([128, NCB], BF16, "cb")
            pvt = sb([128, NPV], F32, "pv")
            gains = sb([128, 2], F32, "gains")
            sinkt = sb([128, 4, 128], F32, "sinkt")
            sinke = sb([128, 8], F32, "sinke")
            qn = sb([128, 5, SEQ], BF16, "qn")
            v = sb([128, NCHK, 128], BF16, "v")
            agst = sb([128, 4, SEQ], BF16, "agst")
            ps = self.pbanks(es)
            cfb = sb([128, 8], F32, "cfb")
            self.dma("sp", cfb[:], self.cf[:, CF_B:CF_B + 8], [], ["cfb"], "cfb")
            self.dma("sp", cbt[:], self.cb[:, :], [], ["cb"], "cb")
            self.dma("sp", pvt[:], self.pvec[:, l, :], [], ["pv"], "pv")
            vsrc = self.vtok[:, 1024 + g * 128:1024 + (g + 1) * 128].rearrange("(n p) c -> p n c", p=128)
            for i in range(4):
                self.dma("sp", v[:, i * 8:(i + 1) * 8, :], vsrc[:, i * 8:(i + 1) * 8, :], [], ["v"], "v")
            self.dma("sp", agst[:], self.zT[ZAG + g * 512:ZAG + (g + 1) * 512, :].rearrange("(c p) t -> p c t", p=128),
                     [], ["agst"], "agst")
            ones = cbt[:, CB_ONES:CB_ONES + 128]
            PT = cbt[:, CB_PT:CB_PT + 128]
            self.ts("dve", gains[:], pvt[:, 24:26], float(np.sqrt(128.0)), None, ALU.mult, None, ["pv"], ["gains"])
            self.act(sinke[:], pvt[:, 26:34], AF.Exp, ["pv", "cfb"], ["sinke"], bias=cfb[:, 3:4])
            self.cp("dve", sinkt[:], sinke[:, g * 4:(g + 1) * 4].unsqueeze(2).to_broadcast([128, 4, 128]),
                    ["sinke"], ["sinkt"])
            sinkrow = sb([128, 4, 128], BF16, "sinkrow")
            self.cp("dve", sinkrow[:], sinkt[:], ["sinkt"], ["sinkrow"])
            with contextlib.ExitStack() as es2:
                sb2 = lambda shape, dtype, name=None: self.sb(es2, shape, dtype, name)
                ctab = sb2([128, SEQ], F32, "ctab")
                stab = sb2([128, SEQ], F32, "stab")
                raw = [sb2([128, SEQ], BF16, "raw") for _ in range(2)]
                sq = [sb2([128, 512], BF16, "sq") for _ in range(4)]
                rs = [sb2([128, 512], F32, "rs") for _ in range(4)]
                qq = [sb2([128, 512], BF16, "qq") for _ in range(4)]
                t1 = [sb2([128, 512], F32, "t1") for _ in range(4)]
                t2 = [sb2([128, 512], F32, "t2") for _ in range(4)]
                rtm = [sb2([128, 512], F32, "rtm") for _ in range(4)]
                self.dma("sp", ctab[:], self.acos[:, :], [], ["ctab"], "ctab")
                self.dma("sp", stab[:], self.asin[:, :], [], ["stab"], "stab")
                it = 0
                for hh in range(5):
                    r = hh % 2
                    row = (ZAQ + (g * 4 + hh) * 128) if hh < 4 else (ZAK + g * 128)
                    gcol = gains[:, 0:1] if hh < 4 else gains[:, 1:2]
                    self.dma("sp", raw[r][:], self.zT[row:row + 128, :], [], [("raw", r)], ("raw", r))
                    for pc in range(8):
                        s = it % 4
                        it += 1
                        tk = slice(pc * 512, (pc + 1) * 512)
                        self.act(sq[s][:], raw[r][:, tk], AF.Square, [("raw", r)], [("sq", s)])
                        self.mm(ps[s][:], ones, sq[s][:], True, True, [("sq", s), "cb"], [("ps", s)])
                        self.rsqrt(rs[s][:], ps[s][:], cfb[:, 2:3], rtm[s][:], [("ps", s), "cfb"], ("rtm", s), ("rs", s))
                        self.stt("dve", qq[s][:], raw[r][:, tk], gcol, rs[s][:], ALU.mult, ALU.mult,
                                 [("raw", r), ("rs", s), "gains"], [("qq", s)])
                        self.mm(ps[4 + s][:], PT, qq[s][:], True, True, [("qq", s), "cb"], [("ps", 4 + s)])
                        self.tt("pool", t1[s][:], qq[s][:], ctab[:, tk], ALU.mult, [("qq", s), "ctab"], [("t1", s)])
                        self.tt("dve", t2[s][:], ps[4 + s][:], stab[:, tk], ALU.mult, [("ps", 4 + s), "stab"], [("t2", s)])
                        self.tt("pool", qn[:, hh, tk], t1[s][:], t2[s][:], ALU.add, [("t1", s), ("t2", s)], [("qn", hh, pc)])
                S.end_phase()
            with contextlib.ExitStack() as es3:
                sb3 = lambda shape, dtype, name=None: self.sb(es3, shape, dtype, name)
                pT = [sb3([128, 3, 512], BF16, "pT") for _ in range(2)]
                den = [sb3([128, 512], F32, "den") for _ in range(2)]
                o1 = [sb3([128, 512], F32, "o1") for _ in range(2)]
                MP = cbt[:, CB_MP:CB_MP + 128]
                MN = cbt[:, CB_MN:CB_MN + 128]
                scale = float(128.0 ** -0.5)
                for n in range(NCHK):
                    s = n % 2
                    segs = [m for m in (n - 1, n, n + 1) if 0 <= m < NCHK]
                    qtk = slice(n * 128, (n + 1) * 128)
                    for si, m in enumerate(segs):
                        b = s * 3 + si
                        self.mm(ps[b][:].rearrange("p (a b) -> p a b", a=4), qn[:, 4, m * 128:(m + 1) * 128], qn[:, 0:4, qtk], True, True, [], [("ps", b)])
                        self.act(pT[s][:, si, :], ps[b][:], AF.Exp, [("ps", b)], [("pT", s, si)], scale=scale, bias=cfb[:, 3:4])
                        if m != n:
                            mk = MP if m < n else MN
                            pv_ = pT[s][:, si, :].rearrange("p (a b) -> p a b", a=4)
                            self.tt("dve", pv_, pv_, mk.unsqueeze(1).to_broadcast([128, 4, 128]),
                                    ALU.mult, [("pT", s, si), "cb"], [("pT", s, si)])
                    for si, m in enumerate(segs):
                        self.mm(ps[6][:], v[:, m, :], pT[s][:, si, :], si == 0, si == len(segs) - 1,
                                [("pT", s, si), "v"], [("ps", 6)])
                    for si, m in enumerate(segs):
                        self.mm(ps[7][:], ones, pT[s][:, si, :], si == 0, False,
                                [("pT", s, si), "cb"], [("ps", 7)], inc=False)
                    self.mm(ps[7][:], cbt[0:1, CB_ONES:CB_ONES + 128], sinkrow[0:1].rearrange("p a b -> p (a b)"), False, True,
                            ["cb", "sinkrow"], [("ps", 7)])
                    self.act(den[s][:], ps[7][:], AF.Ln, [("ps", 7)], [("den", s)])
                    self.act(den[s][:], den[s][:], AF.Exp, [("den", s)], [("den", s)], scale=-1.0)
                    self.tt("dve", o1[s][:], ps[6][:], den[s][:], ALU.mult, [("ps", 6), ("den", s)], [("o1", s)])
                    self.tt("dve", agst[:, :, qtk], o1[s][:].rearrange("p (a b) -> p a b", a=4), agst[:, :, qtk], ALU.mult,
                            [("o1", s), "agst", ("ao", n - 1)], [("ao", n)])
                orow = 2048 + g * 512
                self.dma("sp", self.bin[orow:orow + 512, :].rearrange("(c p) t -> p c t", p=128), agst[:],
                         [("ao", NCHK - 1), "agst"], [], "agst")
                S.end_phase()

    def phase_merge(self, l, tq, x_cur, x_nxt):
        S = self.S
        T0 = tq * 1024
        with contextlib.ExitStack() as es:
            sb = lambda shape, dtype, name=None: self.sb(es, shape, dtype, name)
            binq = sb([128, 24, 1024], BF16, "binq")
            mT = sb([128, 16, 1024], BF16, "mT")
            wbr = [sb([128, 3, 8, 128], BF16, "wbr") for _ in range(2)]
            wbs = [sb([128, 3, 8, 128], F32, "wbs") for _ in range(2)]
            wos = [sb([128, 16, 128], F32, "wos") for _ in range(2)]
            gq = [sb([128, 3, 1024], BF16, "gq") for _ in range(2)]
            wo = [sb([128, 16, 128], BF16, "wo") for _ in range(2)]
            xq = [sb([128, 1024], F32, "xq") for _ in range(3)]
            s1 = [sb([128, 512], F32, "s1") for _ in range(2)]
            s2 = [sb([128, 512], F32, "s2") for _ in range(2)]
            ta = [sb([128, 512], F32, "ta") for _ in range(2)]
            ps = self.pbanks(es)
            for br in range(3):
                self.dma("sp", binq[:, br * 8:(br + 1) * 8, :],
                         self.bin[br * 1024:(br + 1) * 1024, T0:T0 + 1024].rearrange("(c p) t -> p c t", p=128),
                         [], [("binq", br)], ("binq", br))
            gv = self.zT[ZMG:ZMG + 6144, :].rearrange("(br jj p) t -> p br jj t", br=3, p=128)
            it = 0
            def load_b(j):
                s = j % 2
                self.dma("sp", wbs[s][:], self.w_br[l, j], [], [("wbs", s)], ("wbs", s))
                self.dma("sp", gq[s][:], gv[:, :, j, T0:T0 + 1024], [], [("gq", s)], ("gq", s))

            load_b(0)
            for j in range(16):
                s = j % 2
                self.cp("act", wbr[s][:].rearrange("p a b c -> p (a b c)"), wbs[s][:].rearrange("p a b c -> p (a b c)"),
                        [("wbs", s)], [("wbr", s)])
                if j + 1 < 16:
                    load_b(j + 1)
                for tt in range(2):
                    u = it % 2
                    it += 1
                    tk = slice(tt * 512, (tt + 1) * 512)
                    bb = [(it % 2) * 3 + br for br in range(3)]
                    for br in range(3):
                        for ec in range(8):
                            self.mm(ps[bb[br]][:], wbr[s][:, br, ec, :], binq[:, br * 8 + ec, tk], ec == 0, ec == 7,
                                    [("wbr", s), ("binq", br)], [("ps", bb[br])])
                    self.tt("dve", ta[u][:], ps[bb[0]][:], gq[s][:, 0, tk], ALU.mult, [("ps", bb[0]), ("gq", s)], [("ta", u)])
                    self.tt("dve", s1[u][:], ps[bb[1]][:], gq[s][:, 1, tk], ALU.mult, [("ps", bb[1]), ("gq", s)], [("s1", u)])
                    self.tt("dve", s2[u][:], ps[bb[2]][:], gq[s][:, 2, tk], ALU.mult, [("ps", bb[2]), ("gq", s)], [("s2", u)])
                    self.tt("pool", ta[u][:], ta[u][:], s1[u][:], ALU.add, [("ta", u), ("s1", u)], [("ta", u)])
                    self.tt("dve", mT[:, j, tk], ta[u][:], s2[u][:], ALU.add, [("ta", u), ("s2", u)], [("mT", j, tt)])
            mkeys = lambda tt: [("mT", j, tt) for j in range(16)]
            def load_o(i):
                s = i % 2
                x3 = i % 3
                self.dma("sp", wos[s][:], self.w_o[l, i], [], [("wos", s)], ("wos", s))
                self.dma("sp", xq[x3][:], x_cur[i * 128:(i + 1) * 128, T0:T0 + 1024], [], [("xq", x3)], ("xq", x3))

            load_o(0)
            for i in range(16):
                s = i % 2
                x3 = i % 3
                if i + 1 < 16:
                    load_o(i + 1)
                self.cp("act", wo[s][:].rearrange("p a b -> p (a b)"), wos[s][:].rearrange("p a b -> p (a b)"),
                        [("wos", s)], [("wo", s)])
                for tt in range(2):
                    b = 6 + tt
                    tk = slice(tt * 512, (tt + 1) * 512)
                    for jc in range(16):
                        self.mm(ps[b][:], wo[s][:, jc, :], mT[:, jc, tk], jc == 0, jc == 15,
                                [("wo", s)] + mkeys(tt), [("ps", b)])
                    self.tt("dve", xq[x3][:, tk], ps[b][:], xq[x3][:, tk], ALU.add, [("ps", b), ("xq", x3)], [("xq", x3)])
                self.dma("sp", x_nxt[i * 128:(i + 1) * 128, T0:T0 + 1024], xq[x3][:], [("xq", x3)], [], ("xq", x3))
            S.end_phase()


def _consts():
    f32 = np.float32
    t = np.arange(SEQ, dtype=f32)
    inv_r = (1.0 / (f32(10000.0) ** np.linspace(0.0, 1.0, 128, dtype=f32))).astype(f32)
    ang = (t[:, None] * inv_r[None, :]).astype(f32)
    rcos = np.ascontiguousarray(np.cos(ang).T.astype(f32))
    rsin = np.ascontiguousarray(np.sin(ang).T.astype(f32))
    inv_a = (f32(500000.0) ** (-np.arange(16, dtype=f32) / f32(16.0))).astype(f32)
    anga = (t[:, None] * inv_a[None, :]).astype(f32)
    acos = np.ones((128, SEQ), f32)
    asin = np.zeros((128, SEQ), f32)
    acos[0:16] = np.cos(anga).T
    acos[16:32] = np.cos(anga).T
    asin[0:16] = np.sin(anga).T
    asin[16:32] = np.sin(anga).T
    cf = np.zeros((128, NCF), f32)
    li = np.arange(128, dtype=f32)[:, None]
    ji = np.arange(128, dtype=f32)[None, :]
    cf[:, CF_LF:CF_LF + 128] = np.maximum(ji - li, 0)
    cf[:, CF_LB:CF_LB + 128] = np.maximum(li - ji, 0)
    cf[:, CF_JF:CF_JF + 128] = ji + 1.0
    cf[:, CF_JB:CF_JB + 128] = 128.0 - ji
    cf[:, CF_COL] = 127.0 - li[:, 0]
    cf[:, CF_COL + 1] = li[:, 0]
    cf[:, CF_COL + 2] = 128.0
    cf[:, CF_B:CF_B + 5] = np.array([2048.0 * EPS, 256.0 * EPS, 128.0 * EPS, -M0, -np.log(16.0)], f32)[None, :]
    for g, w in enumerate((2, 4, 8, 16)):
        hw = w // 2
        left = np.zeros(8, f32)
        right = np.zeros(8, f32)
        for i in range(8):
            n = i
            lo, hi = max(n - hw, 0), min(n + hw, SEQ)
            left[i] = 1.0 / (hi - lo)
            n = SEQ - 8 + i
            lo, hi = max(n - hw, 0), min(n + hw, SEQ)
            right[i] = 1.0 / (hi - lo)
        cf[:, CF_PINV + g * 16:CF_PINV + g * 16 + 8] = left[None, :]
        cf[:, CF_PINV + g * 16 + 8:CF_PINV + g * 16 + 16] = right[None, :]
    cb = np.zeros((128, NCB), f32)
    cb[:, CB_ONES:CB_ONES + 128] = 1.0
    cb[:, CB_ID:CB_ID + 128] = np.eye(128, dtype=f32)
    PT = np.zeros((128, 128), f32)
    for m in range(16):
        PT[m + 16, m] = -1.0
        PT[m, m + 16] = 1.0
    cb[:, CB_PT:CB_PT + 128] = PT
    cb[:, CB_MP:CB_MP + 128] = (li >= ji).astype(f32)
    cb[:, CB_MN:CB_MN + 128] = (li <= ji).astype(f32)
    return dict(rcos=rcos, rsin=rsin, acos=acos, asin=asin, cf=cf, cb=cb.astype(ml_dtypes.bfloat16))


def _prep_weights(inputs):
    f32 = np.float32
    w_in = inputs["w_in"]
    fmcols = np.concatenate([np.arange(z0, z0 + w) for (z0, w, _) in FM_SEGS])
    vcols = np.concatenate([np.arange(z0, z0 + w) for (z0, w) in V_SEGS])
    win_fm = np.empty((NLAYER, NFM, 128, 16, 128), f32)
    win_v = np.empty((NLAYER, 128, 16, 1280), f32)
    for l in range(NLAYER):
        a = w_in[l][:, fmcols].reshape(16, 128, NFM, 128)
        win_fm[l] = a.transpose(2, 1, 0, 3)
        b = w_in[l][:, vcols].reshape(16, 128, 1280)
        win_v[l] = b.transpose(1, 0, 2)
    wb = np.stack([inputs["w_ret"], inputs["w_pool"], inputs["w_att"]], axis=1)
    w_br = np.ascontiguousarray(wb.reshape(NLAYER, 3, 8, 128, 16, 128).transpose(0, 4, 3, 1, 2, 5))
    w_o = np.ascontiguousarray(inputs["w_out"].reshape(NLAYER, 16, 128, 16, 128).transpose(0, 3, 2, 1, 4))
    w_pl = np.ascontiguousarray(inputs["pool_w"].reshape(NLAYER, 4, 2, 128, 256).transpose(0, 3, 1, 2, 4))
    pvec = np.zeros((128, NLAYER, NPV), f32)
    for l in range(NLAYER):
        pvec[:, l, 0:16] = inputs["norm_g"][l].reshape(16, 128).T
        pvec[:, l, 16:24] = inputs["pool_scale"][l].reshape(8, 128).T
        pvec[:, l, 24] = inputs["attn_q_gain"][l]
        pvec[:, l, 25] = inputs["attn_k_gain"][l]
        pvec[:, l, 26:34] = inputs["attn_sink"][l][None, :]
        pvec[:, l, 34:38] = inputs["ret_decay_fwd"][l][None, :]
        pvec[:, l, 38:42] = inputs["ret_decay_bwd"][l][None, :]
    return dict(win_fm=win_fm, win_v=win_v, w_br=w_br, w_o=w_o, w_pl=w_pl, pvec=pvec)


def kernel(**inputs):
    inputs = {k: np.asarray(v) for k, v in inputs.items()}
    x = inputs["x"]
    B = x.shape[0]
    shared = _prep_weights(inputs)
    shared.update(_consts())
    prog = Prog()
    nc = prog.build()
    in_maps = []
    for b in range(B):
        m = dict(shared)
        m["xT"] = np.ascontiguousarray(x[b].T)
        in_maps.append(m)
    res = run_bass_kernel_spmd(nc, in_maps, core_ids=list(range(B)))
    out = np.stack([np.ascontiguousarray(r["outT"].T) for r in res.results], axis=0)
    return out.astype(np.float32)
```

```python
import contextlib
import numpy as np
import ml_dtypes
import concourse.bass as bass
import concourse.mybir as mybir
from concourse.bass_utils import run_bass_kernel_spmd

F32 = mybir.dt.float32
BF16 = mybir.dt.bfloat16
AF = mybir.ActivationFunctionType
ALU = mybir.AluOpType

SEQ = 4096
DM = 2048
NLAYER = 4
NCHK = SEQ // 128
EPS = 1e-6
M0 = 12.0

ZRQ, ZRK, ZRG, ZPV, ZPG, ZAQ, ZAK, ZAG, ZMG = 0, 1024, 2048, 3072, 4096, 5120, 6144, 6400, 7424
ZROWS = 13568
NFM = ZROWS // 128
FM_SEGS = [(0, 1024, "copy"), (1024, 1024, "copy"), (3072, 1024, "silu"), (4096, 1024, "copy"),
           (5120, 1024, "silu"), (6144, 1024, "copy"), (7168, 256, "copy"), (7680, 1024, "silu"),
           (8704, 6144, "sigmoid")]
V_SEGS = [(2048, 1024), (7424, 256)]
NPV = 42
CF_LF, CF_LB, CF_JF, CF_JB, CF_COL, CF_PINV = 0, 128, 256, 384, 512, 516
NCF = 516 + 64 + 8
CF_B = 580
CB_ONES, CB_ID, CB_PT, CB_MP, CB_MN = 0, 128, 256, 384, 512
NCB = 640


class Sched:
    ENG = ("pe", "act", "dve", "pool", "sp")

    def __init__(self, nc, sems):
        self.nc = nc
        pool = list(sems)
        self.esem = {e: pool.pop() for e in self.ENG[:4]}
        self.ecnt = {e: 0 for e in self.ENG[:4]}
        self.pe_pending = False
        self.dma_pool = pool
        self.dsem = {}
        self.dcnt = {id(s): 0 for s in pool}
        self.semobj = {id(s): s for s in pool}
        for s in self.esem.values():
            self.semobj[id(s)] = s
        self.known = {e: {} for e in self.ENG}
        self.lastw = {}
        self.readers = {}
        self.ops = {e: [] for e in self.ENG}
        self.nops = 0
        self.check = False
        self.maxops = None
        self.simval = {}

    def _dma_sem(self, key):
        if key not in self.dsem:
            idx = len(self.dsem)
            assert idx < len(self.dma_pool), "out of DMA semaphores"
            self.dsem[key] = self.dma_pool[idx]
        return self.dsem[key]

    def _need(self, eng, dep, waits, is_mm):
        semid, val, src = dep
        if src == "pe" and eng == "pe" and is_mm:
            return
        if semid in self.dcnt:
            val = self.dcnt[semid]
        if self.known[eng].get(semid, -1) >= val:
            return
        self.known[eng][semid] = val
        waits[semid] = max(waits.get(semid, 0), val)

    def op(self, eng, fn, reads=(), writes=(), dma_key=None, is_mm=False, inc=True):
        if self.maxops is not None and self.nops >= self.maxops:
            return
        waits = {}
        for k in list(reads) + list(writes):
            d = self.lastw.get(k)
            if d is not None:
                self._need(eng, d, waits, is_mm)
        for k in writes:
            for semid, (val, src) in self.readers.get(k, {}).items():
                self._need(eng, (semid, val, src), waits, is_mm)
        if dma_key is not None:
            s = self._dma_sem(dma_key)
            self.dcnt[id(s)] += 16
            ev = (id(s), self.dcnt[id(s)], "dma")
            incr = (s, 16)
        else:
            s = self.esem[eng]
            if inc:
                self.ecnt[eng] += 1
                ev = (id(s), self.ecnt[eng], eng)
                incr = (s, 1)
                if eng == "pe":
                    self.pe_pending = False
            else:
                assert eng == "pe" and is_mm
                ev = (id(s), self.ecnt[eng] + 1, eng)
                incr = None
                self.pe_pending = True
        for k in writes:
            self.lastw[k] = ev
            self.readers[k] = {}
        for k in reads:
            r = self.readers.setdefault(k, {})
            r[ev[0]] = (ev[1], ev[2])
        self.ops[eng].append((fn, [(self.semobj[sid], v) for sid, v in waits.items()], incr))
        self.nops += 1

    def _simulate(self, ops):
        val = self.simval
        pc = {e: 0 for e in self.ENG}
        progress = True
        while progress:
            progress = False
            for e in self.ENG:
                lst = ops[e]
                while pc[e] < len(lst):
                    fn, waits, incr = lst[pc[e]]
                    if any(val.get(id(s), 0) < v for s, v in waits):
                        break
                    if incr is not None:
                        val[id(incr[0])] = val.get(id(incr[0]), 0) + incr[1]
                    pc[e] += 1
                    progress = True
        stuck = {e: (pc[e], len(ops[e])) for e in self.ENG if pc[e] < len(ops[e])}
        if stuck:
            msg = []
            for e, (p, n) in stuck.items():
                fn, waits, incr = ops[e][p]
                msg.append(f"{e}: op {p}/{n} waits " + str([(s.name if hasattr(s, 'name') else id(s), v, val.get(id(s), 0)) for s, v in waits]))
            raise RuntimeError("DEADLOCK in recorded program: " + " | ".join(msg))

    def end_phase(self):
        assert self.maxops is not None or not self.pe_pending
        targets = []
        for e in self.ENG[:4]:
            if self.ecnt[e] > 0:
                targets.append((id(self.esem[e]), self.ecnt[e]))
        for sid, c in self.dcnt.items():
            if c > 0:
                targets.append((sid, c))
        for e in self.ENG:
            waits = []
            for sid, v in targets:
                if self.known[e].get(sid, -1) >= v:
                    continue
                self.known[e][sid] = v
                waits.append((self.semobj[sid], v))
            if waits:
                self.ops[e].append((None, waits, None))
        self.lastw = {}
        self.readers = {}
        self.dsem = {}
        if self.check:
            self._simulate(self.ops)
        nc = self.nc
        ops = self.ops
        self.ops = {e: [] for e in self.ENG}

        def replay(engine, lst):
            for fn, waits, incr in lst:
                for s, v in waits:
                    engine.wait_ge(s, v)
                if fn is not None:
                    ins = fn(engine)
                    if incr is not None:
                        ins.then_inc(incr[0], incr[1])

        with nc.Block() as block:
            @block.tensor
            def _(e):
                replay(e, ops["pe"])

            @block.scalar
            def _(e):
                replay(e, ops["act"])

            @block.vector
            def _(e):
                replay(e, ops["dve"])

            @block.gpsimd
            def _(e):
                replay(e, ops["pool"])

            @block.sync
            def _(e):
                replay(e, ops["sp"])


class Prog:
    def __init__(self, nlayers=NLAYER, dbg=False, stop_after=None, only=None):
        self.only = only
        self.nlayers = nlayers
        self.dbg = dbg
        self.stop_after = stop_after
        nc = bass.Bass("TRN2", target_bir_lowering=False)
        self.nc = nc
        ein = "ExternalInput"
        sk = "ExternalOutput" if dbg else "Internal"
        dt = nc.dram_tensor
        if not only:
            self.xT = dt("xT", [DM, SEQ], F32, kind=ein).ap()
            self.win_fm = dt("win_fm", [NLAYER, NFM, 128, 16, 128], F32, kind=ein).ap()
            self.win_v = dt("win_v", [NLAYER, 128, 16, 1280], F32, kind=ein).ap()
            self.w_br = dt("w_br", [NLAYER, 16, 128, 3, 8, 128], F32, kind=ein).ap()
            self.w_o = dt("w_o", [NLAYER, 16, 128, 16, 128], F32, kind=ein).ap()
        self.w_pl = dt("w_pl", [NLAYER, 128, 4, 2, 256], F32, kind=ein).ap()
        self.pvec = dt("pvec", [128, NLAYER, NPV], F32, kind=ein).ap()
        self.cf = dt("cf", [128, NCF], F32, kind=ein).ap()
        self.cb = dt("cb", [128, NCB], BF16, kind=ein).ap()
        self.rcos = dt("rcos", [128, SEQ], F32, kind=ein).ap()
        self.rsin = dt("rsin", [128, SEQ], F32, kind=ein).ap()
        self.acos = dt("acos", [128, SEQ], F32, kind=ein).ap()
        self.asin = dt("asin", [128, SEQ], F32, kind=ein).ap()
        if not only:
            self.outT = dt("outT", [DM, SEQ], F32, kind="ExternalOutput").ap()
        self.zT = dt("zT", [ZROWS, SEQ], BF16, kind=(ein if only else sk)).ap()
        self.vtok = dt("vtok", [SEQ, 1280], BF16, kind=(ein if only else sk)).ap()
        self.bin = dt("bin", [3072, SEQ], BF16, kind=sk).ap()
        if not only:
            self.xs = [dt("xs0", [DM, SEQ], F32, kind=sk).ap(), dt("xs1", [DM, SEQ], F32, kind=sk).ap()]
        self._uid = 0

    def sb(self, es, shape, dtype, name=None):
        self._uid += 1
        return es.enter_context(self.nc.sbuf_tensor(f"{name or 't'}_{self._uid}", list(shape), dtype))

    def pbanks(self, es, n=8, dtype=F32, cols=512):
        out = []
        for _ in range(n):
            self._uid += 1
            out.append(es.enter_context(self.nc.psum_tensor(f"ps_{self._uid}", [128, cols], dtype)))
        return out

    def dma(self, q, out, in_, reads, writes, key):
        self.S.op(q, lambda e: e.dma_start(out=out, in_=in_), reads=reads, writes=writes, dma_key=key)

    def mm(self, out, lhsT, rhs, start, stop, reads, writes, inc=None):
        if inc is None:
            inc = stop
        self.S.op("pe", lambda e: e.matmul(out, lhsT=lhsT, rhs=rhs, start=start, stop=stop),
                  reads=reads, writes=writes, is_mm=True, inc=inc)

    def act(self, out, in_, func, reads, writes, scale=None, bias=None):
        kw = {}
        if scale is not None:
            kw["scale"] = scale
        if bias is not None:
            kw["bias"] = bias
        self.S.op("act", lambda e: e.activation(out=out, in_=in_, func=func, **kw), reads=reads, writes=writes)

    def tt(self, eng, out, in0, in1, op, reads, writes):
        self.S.op(eng, lambda e: e.tensor_tensor(out=out, in0=in0, in1=in1, op=op), reads=reads, writes=writes)

    def ts(self, eng, out, in0, s1, s2, op0, op1, reads, writes):
        if op1 is None:
            self.S.op(eng, lambda e: e.tensor_scalar(out=out, in0=in0, scalar1=s1, scalar2=None, op0=op0),
                      reads=reads, writes=writes)
        else:
            self.S.op(eng, lambda e: e.tensor_scalar(out=out, in0=in0, scalar1=s1, scalar2=s2, op0=op0, op1=op1),
                      reads=reads, writes=writes)

    def stt(self, eng, out, in0, scalar, in1, op0, op1, reads, writes):
        self.S.op(eng, lambda e: e.scalar_tensor_tensor(out=out, in0=in0, scalar=scalar, in1=in1, op0=op0, op1=op1),
                  reads=reads, writes=writes)

    def cp(self, eng, out, in_, reads, writes):
        if eng == "act":
            self.act(out, in_, AF.Copy, reads, writes)
        else:
            self.S.op(eng, lambda e: e.tensor_copy(out=out, in_=in_), reads=reads, writes=writes)

    def rsqrt(self, out, ps_ap, bias_ap, tmp, reads, ktmp, kout):
        self.act(tmp, ps_ap, AF.Ln, reads, [ktmp], bias=bias_ap)
        self.act(out, tmp, AF.Exp, [ktmp], [kout], scale=-0.5)

    def memset(self, eng, ap, val, writes):
        self.S.op(eng, lambda e: e.memset(ap, val), writes=writes)

    def build(self):
        nc = self.nc
        with contextlib.ExitStack() as es:
            sems = [es.enter_context(nc.semaphore(f"sem{i}")) for i in range(96)]
            fins = [es.enter_context(nc.semaphore(f"fin{i}")) for i in range(4)]
            for s_ in sems + fins:
                nc.gpsimd.sem_clear(s_)
            nc.all_engine_barrier()
            self.S = Sched(nc, sems)
            self._build_body()
            with nc.Block() as block:
                @block.tensor
                def _(e):
                    e.sem_inc(fins[0], 1)

                @block.scalar
                def _(e):
                    e.sem_inc(fins[1], 1)

                @block.vector
                def _(e):
                    e.sem_inc(fins[2], 1)

                @block.sync
                def _(e):
                    e.sem_inc(fins[3], 1)

                @block.gpsimd
                def _(e):
                    for f_ in fins:
                        e.wait_ge(f_, 1)
                    for s_ in sems + fins:
                        e.sem_clear(s_)
            nc.all_engine_barrier()
        return nc

    def _build_body(self):
        nc = self.nc
        if True:
            if self.only:
                if self.only.startswith("ret"):
                    for h in ([int(self.only[3:])] if len(self.only) > 3 else range(4)):
                        self.phase_ret(0, h)
                elif self.only == "pool":
                    self.phase_pool(0)
                elif self.only == "att":
                    for g in range(2):
                        self.phase_att(0, g)
                return
            for l in range(self.nlayers):
                x_cur = self.xT if l == 0 else self.xs[(l - 1) % 2]
                x_nxt = self.outT if l == self.nlayers - 1 else self.xs[l % 2]
                for th in range(2):
                    self.phase_p1(l, th, x_cur)
                if self.stop_after == "p1":
                    break
                for h in range(4):
                    self.phase_ret(l, h)
                if self.stop_after == "ret":
                    break
                self.phase_pool(l)
                if self.stop_after == "pool":
                    break
                for g in range(2):
                    self.phase_att(l, g)
                if self.stop_after == "att":
                    break
                for tq in range(4):
                    self.phase_merge(l, tq, x_cur, x_nxt)

    def phase_p1(self, l, th, x_cur):
        S = self.S
        T0 = th * 2048
        with contextlib.ExitStack() as es:
            hT = self.sb(es, [128, 16, 2048], BF16, "hT")
            xin = [self.sb(es, [128, 16, 256], F32, "xin") for _ in range(2)]
            sq = [self.sb(es, [128, 16, 256], BF16, "sq") for _ in range(2)]
            rstd = [self.sb(es, [128, 256], F32, "rstd") for _ in range(2)]
            gcol = self.sb(es, [128, 16], F32, "gcol")
            ones = self.sb(es, [128, 128], BF16, "ones")
            wch = [self.sb(es, [128, 16, 128], BF16, "wch") for _ in range(3)]
            zst = [self.sb(es, [128, 2048], BF16, "zst") for _ in range(3)]
            wv = [self.sb(es, [128, 16, 256], BF16, "wv") for _ in range(2)]
            vst = [self.sb(es, [128, 16, 256], BF16, "vst") for _ in range(2)]
            ps = self.pbanks(es)

            cfb = self.sb(es, [128, 8], F32, "cfb")
            rtmp = [self.sb(es, [128, 256], F32, "rtmp") for _ in range(2)]
            self.dma("sp", cfb[:], self.cf[:, CF_B:CF_B + 8], [], ["cfb"], "cfb")
            self.dma("sp", gcol[:], self.pvec[:, l, 0:16], [], ["gcol"], "gcol")
            self.dma("sp", ones[:], self.cb[:, CB_ONES:CB_ONES + 128], [], ["ones"], "ones")
            self.ts("dve", gcol[:], gcol[:], float(np.sqrt(2048.0)), None, ALU.mult, None, ["gcol"], ["gcol"])
            xv = x_cur.rearrange("(kc p) t -> p kc t", p=128)
            for t in range(8):
                s = t % 2
                self.dma("sp", xin[s][:], xv[:, :, T0 + t * 256:T0 + (t + 1) * 256], [], [("xin", s)], ("xin", s))
                self.act(sq[s][:].rearrange("p k t -> p (k t)"), xin[s][:].rearrange("p k t -> p (k t)"),
                         AF.Square, [("xin", s)], [("sq", s)])
                for kc in range(16):
                    self.mm(ps[s][:, 0:256], ones[:], sq[s][:, kc, :], kc == 0, kc == 15,
                            [("sq", s), "ones"], [("ps", s)])
                self.rsqrt(rstd[s][:], ps[s][:, 0:256], cfb[:, 0:1], rtmp[s][:], [("ps", s), "cfb"], ("rtmp", s), ("rstd", s))
                self.tt("pool", xin[s][:], xin[s][:], rstd[s][:].unsqueeze(1).to_broadcast([128, 16, 256]),
                        ALU.mult, [("xin", s), ("rstd", s)], [("xin", s)])
                self.tt("dve", hT[:, :, t * 256:(t + 1) * 256], xin[s][:],
                        gcol[:].unsqueeze(2).to_broadcast([128, 16, 256]), ALU.mult,
                        [("xin", s), "gcol"], [("hT", t)])
            hkeys = [("hT", t) for t in range(8)]
            nb = 0
            for c in range(NFM):
                s = c % 3
                fn = "copy"
                row = c * 128
                for (z0, w, f), r0 in zip(FM_SEGS, (ZRQ, ZRK, ZRG, ZPV, ZPG, ZAQ, ZAK, ZAG, ZMG)):
                    if r0 <= row < r0 + w:
                        fn = f
                self.dma("pool", wch[s][:], self.win_fm[l, c], [], [("wch", s)], ("wch", s))
                for tt in range(4):
                    b = 2 + nb % 6
                    nb += 1
                    for kc in range(16):
                        self.mm(ps[b][:], wch[s][:, kc, :], hT[:, kc, tt * 512:(tt + 1) * 512], kc == 0, kc == 15,
                                [("wch", s), ("hT", 2 * tt), ("hT", 2 * tt + 1)], [("ps", b)])
                    o = zst[s][:, tt * 512:(tt + 1) * 512]
                    if fn == "copy":
                        eng = "dve" if (tt % 2 == 0) else "act"
                        self.cp(eng, o, ps[b][:], [("ps", b)], [("zst", s, tt)])
                    else:
                        self.act(o, ps[b][:], AF.Silu if fn == "silu" else AF.Sigmoid, [("ps", b)], [("zst", s, tt)])
                self.dma("sp", self.zT[row:row + 128, T0:T0 + 2048], zst[s][:],
                         [("zst", s, i) for i in range(4)], [], ("zst", s))
            vv = self.vtok[T0:T0 + 2048, :].rearrange("(k p) c -> p k c", p=128)
            for cpi in range(5):
                s = cpi % 2
                self.dma("pool", wv[s][:], self.win_v[l, :, :, cpi * 256:(cpi + 1) * 256], [], [("wv", s)], ("wv", s))
                for tk in range(16):
                    b = 2 + nb % 6
                    nb += 1
                    for kc in range(16):
                        self.mm(ps[b][:, 0:256], hT[:, kc, tk * 128:(tk + 1) * 128], wv[s][:, kc, :], kc == 0, kc == 15,
                                [("wv", s), ("hT", tk // 2)], [("ps", b)])
                    eng = "dve" if (tk % 2 == 0) else "act"
                    self.cp(eng, vst[s][:, tk, :], ps[b][:, 0:256], [("ps", b)], [("vst", s, tk)])
                self.dma("sp", vv[:, :, cpi * 256:(cpi + 1) * 256], vst[s][:],
                         [("vst", s, i) for i in range(16)], [], ("vst", s))
            S.end_phase()

    def phase_ret(self, l, h):
        S = self.S
        with contextlib.ExitStack() as es:
            sb = lambda shape, dtype, name=None: self.sb(es, shape, dtype, name)
            cf = sb([128, NCF], F32, "cf")
            cbt = sb([128, NCB], BF16, "cb")
            pvt = sb([128, NPV], F32, "pv")
            lg = sb([128, 8], F32, "lg")
            dmask = sb([128, 128], F32, "dmask")
            tmpm = sb([128, 128], F32, "tmpm")
            qft = sb([128, 128], F32, "qft")
            qbt = sb([128, 128], F32, "qbt")
            colv = sb([128, 4], F32, "colv")
            qrot = sb([128, 2, SEQ], BF16, "qrot")
            krot = sb([128, 2, SEQ], BF16, "krot")
            ktok = sb([128, NCHK, 256], BF16, "ktok")
            v = sb([128, NCHK, 256], BF16, "v")
            sbb = sb([128, NCHK, 2, 256], BF16, "sbb")
            gst = sb([128, 2, SEQ], BF16, "gst")
            raw = [sb([128, 2, 1024], BF16, "raw") for _ in range(2)]
            cs = [sb([128, 2, 1024], F32, "cs") for _ in range(2)]
            tmp = [sb([128, 1024], F32, "rt") for _ in range(4)]
            St = [sb([128, 2, 256], F32, "St") for _ in range(2)]
            sfc = [sb([128, 2, 256], BF16, "sfc") for _ in range(3)]
            vwa = sb([128, NCHK, 256], BF16, "vwa")
            sT = [sb([128, 128], BF16, "sT") for _ in range(2)]
            qf = [sb([128, 2, 128], BF16, "qf") for _ in range(2)]
            qb = [sb([128, 2, 128], BF16, "qb") for _ in range(2)]
            sqo = [sb([128, 2, 128], BF16, "sqo") for _ in range(2)]
            rs = [sb([128, 128], F32, "rs") for _ in range(2)]
            y1 = [sb([128, 2, 128], F32, "y1") for _ in range(2)]
            rtm = [sb([128, 128], F32, "rtm") for _ in range(2)]
            ps = self.pbanks(es, 6)
            pst = self.pbanks(es, 2, BF16, 1024)

            self.dma("sp", cf[:], self.cf[:, :], [], ["cf"], "cf")
            self.dma("sp", cbt[:], self.cb[:, :], [], ["cb"], "cb")
            self.dma("sp", pvt[:], self.pvec[:, l, :], [], ["pv"], "pv")
            vsrc = self.vtok[:, h * 256:(h + 1) * 256].rearrange("(n p) c -> p n c", p=128)
            for i in range(4):
                self.dma("sp", v[:, i * 8:(i + 1) * 8, :], vsrc[:, i * 8:(i + 1) * 8, :], [], ["v"], "v")
            self.dma("sp", gst[:], self.zT[ZRG + h * 256:ZRG + (h + 1) * 256, :].rearrange("(c p) t -> p c t", p=128),
                     [], ["gst"], "gst")
            ones = cbt[:, CB_ONES:CB_ONES + 128]
            ident = cbt[:, CB_ID:CB_ID + 128]
            self.act(lg[:], pvt[:, 34:42], AF.Exp, ["pv"], ["lg"])
            self.ts("dve", lg[:], lg[:], -1.0, None, ALU.mult, None, ["lg"], ["lg"])
            lgf = lg[:, h:h + 1]
            lgb = lg[:, 4 + h:5 + h]
            self.ts("dve", tmpm[:], cf[:, CF_LF:CF_LF + 128], lgf, None, ALU.mult, None, ["cf", "lg"], ["tmpm"])
            self.stt("dve", tmpm[:], cf[:, CF_LB:CF_LB + 128], lgb, tmpm[:], ALU.mult, ALU.add,
                     ["cf", "lg", "tmpm"], ["tmpm"])
            self.act(dmask[:], tmpm[:], AF.Exp, ["tmpm"], ["dmask"], bias=cf[:, CF_B + 4:CF_B + 5])
            self.act(qft[:], cf[:, CF_JF:CF_JF + 128], AF.Exp, ["cf", "lg"], ["qft"], scale=lgf)
            self.act(qbt[:], cf[:, CF_JB:CF_JB + 128], AF.Exp, ["cf", "lg"], ["qbt"], scale=lgb)
            self.act(colv[:, 0:1], cf[:, CF_COL:CF_COL + 1], AF.Exp, ["cf", "lg"], ["colv"], scale=lgf,
                     bias=cf[:, CF_B + 4:CF_B + 5])
            self.act(colv[:, 1:2], cf[:, CF_COL + 1:CF_COL + 2], AF.Exp, ["cf", "lg", "colv"], ["colv"], scale=lgb,
                     bias=cf[:, CF_B + 4:CF_B + 5])
            self.act(colv[:, 2:3], cf[:, CF_COL + 2:CF_COL + 3], AF.Exp, ["cf", "lg", "colv"], ["colv"], scale=lgf)
            self.act(colv[:, 3:4], cf[:, CF_COL + 2:CF_COL + 3], AF.Exp, ["cf", "lg", "colv"], ["colv"], scale=lgb)
            it = 0
            for which, zr, dst in (("q", ZRQ, qrot), ("k", ZRK, krot)):
                for pc in range(4):
                    s = it % 2
                    it += 1
                    t0 = pc * 1024
                    self.dma("sp", raw[s][:], self.zT[zr + h * 256:zr + (h + 1) * 256, t0:t0 + 1024]
                             .rearrange("(c p) t -> p c t", p=128), [], [("raw", s)], ("raw", s))
                    self.dma("sp", cs[s][:, 0, :], self.rcos[:, t0:t0 + 1024], [], [("cs", s, 0)], ("cs", s, 0))
                    self.dma("sp", cs[s][:, 1, :], self.rsin[:, t0:t0 + 1024], [], [("cs", s, 1)], ("cs", s, 1))
                    x1 = raw[s][:, 0, :]
                    x2 = raw[s][:, 1, :]
                    co = cs[s][:, 0, :]
                    si = cs[s][:, 1, :]
                    rk_ = [("raw", s), ("cs", s, 0), ("cs", s, 1)]
                    self.tt("dve", tmp[0][:], x1, co, ALU.mult, rk_, [("rt", 0)])
                    self.tt("pool", tmp[1][:], x2, si, ALU.mult, rk_, [("rt", 1)])
                    self.tt("dve", dst[:, 0, t0:t0 + 1024], tmp[0][:], tmp[1][:], ALU.subtract,
                            [("rt", 0), ("rt", 1)], [(which, pc, 0)])
                    self.tt("pool", tmp[2][:], x2, co, ALU.mult, rk_, [("rt", 2)])
                    self.tt("dve", tmp[3][:], x1, si, ALU.mult, rk_, [("rt", 3)])
                    self.tt("dve", dst[:, 1, t0:t0 + 1024], tmp[2][:], tmp[3][:], ALU.add,
                            [("rt", 2), ("rt", 3)], [(which, pc, 1)])
            qkeys = lambda n: [("q", n // 8, 0), ("q", n // 8, 1)]
            kkeys = lambda n: [("k", n // 8, 0), ("k", n // 8, 1)]
            for n4 in range(NCHK // 4):
                b = n4 % 2
                for i in range(4):
                    n = n4 * 4 + i
                    for dc in range(2):
                        o = pst[b][:, (i * 2 + dc) * 128:(i * 2 + dc + 1) * 128]
                        self.S.op("pe", lambda e, o=o, n=n, dc=dc: e.transpose(o, krot[:, dc, n * 128:(n + 1) * 128], ident),
                                  reads=kkeys(n) + ["cb"], writes=[("pst", b)], is_mm=True, inc=(i == 3 and dc == 1))
                self.cp("act" if n4 % 2 else "dve", ktok[:, n4 * 4:(n4 + 1) * 4, :].rearrange("p n c -> p (n c)"),
                        pst[b][:, :], [("pst", b)], [("ktok", n4)])
            self.memset("dve", St[0][:], 0.0, [("St", 0)])
            self.act(vwa[:].rearrange("p n c -> p (n c)"), v[:].rearrange("p n c -> p (n c)"), AF.Copy,
                     ["v", "colv"], ["vwa"], scale=colv[:, 1:2])
            for n in range(NCHK - 1, -1, -1):
                cur = (n + 1) % 2
                nxt = n % 2
                self.cp("act", sbb[:, n].rearrange("p a b -> p (a b)"), St[cur][:].rearrange("p a b -> p (a b)"),
                        [("St", cur)], [("sbb", n)])
                if n == 0:
                    break
                b = 4 + n % 2
                for dc in range(2):
                    self.mm(ps[b][:, dc * 256:(dc + 1) * 256], ktok[:, n, dc * 128:(dc + 1) * 128], vwa[:, n, :], True, True,
                            [("ktok", n // 4), "vwa"], [("ps", b)], inc=(dc == 1))
                self.stt("dve", St[nxt][:].rearrange("p a b -> p (a b)"), St[cur][:].rearrange("p a b -> p (a b)"),
                         colv[:, 3:4], ps[b][:], ALU.mult, ALU.add, [("St", cur), ("ps", b), "colv"], [("St", nxt)])
            self.memset("dve", St[0][:], 0.0, [("St", 0)])
            self.act(vwa[:].rearrange("p n c -> p (n c)"), v[:].rearrange("p n c -> p (n c)"), AF.Copy,
                     ["v", "colv"], ["vwa"], scale=colv[:, 0:1])
            b_k, b_n = 4, 5

            def st1(n):
                s = n % 2
                cur, nxt = n % 2, (n + 1) % 2
                tk = slice(n * 128, (n + 1) * 128)
                b_s = s
                for dc in range(2):
                    self.mm(ps[b_s][:, 0:128], krot[:, dc, tk], qrot[:, dc, tk], dc == 0, dc == 1,
                            kkeys(n) + qkeys(n), [("ps", b_s)])
                if n < NCHK - 1:
                    for dc in range(2):
                        self.mm(ps[b_k][:, dc * 256:(dc + 1) * 256], ktok[:, n, dc * 128:(dc + 1) * 128], vwa[:, n, :], True, True,
                                [("ktok", n // 4), "vwa"], [("ps", b_k)], inc=(dc == 1))
                self.tt("dve", qf[s][:], qrot[:, :, tk], qft[:].unsqueeze(1).to_broadcast([128, 2, 128]), ALU.mult,
                        qkeys(n) + ["qft"], [("qf", s)])
                self.tt("dve", qb[s][:], qrot[:, :, tk], qbt[:].unsqueeze(1).to_broadcast([128, 2, 128]), ALU.mult,
                        qkeys(n) + ["qbt"], [("qb", s)])
                self.tt("dve", sT[s][:], ps[b_s][:, 0:128], dmask[:], ALU.mult, [("ps", b_s), "dmask"], [("sT", s)])
                if n < NCHK - 1:
                    self.stt("dve", St[nxt][:].rearrange("p a b -> p (a b)"), St[cur][:].rearrange("p a b -> p (a b)"),
                             colv[:, 2:3], ps[b_k][:], ALU.mult, ALU.add, [("St", cur), ("ps", b_k), "colv"], [("St", nxt)])
                    f3 = (n + 1) % 3
                    self.cp("act", sfc[f3][:].rearrange("p a b -> p (a b)"), St[nxt][:].rearrange("p a b -> p (a b)"),
                            [("St", nxt)], [("sfc", f3)])

            def st2(n):
                s = n % 2
                b_o = 2 + s
                for ec in range(2):
                    o = ps[b_o][:, ec * 128:(ec + 1) * 128]
                    es_ = slice(ec * 128, (ec + 1) * 128)
                    self.mm(o, v[:, n, es_], sT[s][:], True, False, ["v", ("sT", s)], [("ps", b_o)])
                    for dc in range(2):
                        self.mm(o, sfc[n % 3][:, dc, es_], qf[s][:, dc, :], False, False, [("sfc", n % 3), ("qf", s)], [("ps", b_o)])
                    for dc in range(2):
                        self.mm(o, sbb[:, n, dc, es_], qb[s][:, dc, :], False, dc == 1, [("sbb", n), ("qb", s)],
                                [("ps", b_o)], inc=(dc == 1 and ec == 1))
                self.act(sqo[s][:].rearrange("p a b -> p (a b)"), ps[b_o][:, 0:256], AF.Square, [("ps", b_o)], [("sqo", s)])

            def st3(n):
                s = n % 2
                b_o = 2 + s
                tk = slice(n * 128, (n + 1) * 128)
                for ec in range(2):
                    self.mm(ps[b_n][:, 0:128], ones, sqo[s][:, ec, :], ec == 0, ec == 1, [("sqo", s), "cb"], [("ps", b_n)])
                self.rsqrt(rs[s][:], ps[b_n][:, 0:128], cf[:, CF_B + 1:CF_B + 2], rtm[s][:], [("ps", b_n), "cf"], ("rtm", s), ("rs", s))
                self.stt("dve", y1[s][:], ps[b_o][:, 0:256].rearrange("p (a b) -> p a b", a=2), 16.0,
                         rs[s][:].unsqueeze(1).to_broadcast([128, 2, 128]), ALU.mult, ALU.mult,
                         [("ps", b_o), ("rs", s)], [("y1", s)])
                self.tt("dve", gst[:, :, tk], y1[s][:], gst[:, :, tk], ALU.mult, [("y1", s), "gst", ("go", n - 1)], [("go", n)])

            self.cp("act", sfc[0][:].rearrange("p a b -> p (a b)"), St[0][:].rearrange("p a b -> p (a b)"),
                    [("St", 0)], [("sfc", 0)])
            st1(0)
            for n in range(NCHK):
                if n + 1 < NCHK:
                    st1(n + 1)
                if n >= 1:
                    st3(n - 1)
                st2(n)
            st3(NCHK - 1)
            self.dma("sp", self.bin[h * 256:(h + 1) * 256, :].rearrange("(c p) t -> p c t", p=128), gst[:],
                     [("go", NCHK - 1), "gst"], [], "gst")
            S.end_phase()

    def phase_pool(self, l):
        S = self.S
        PADL = 16
        W = SEQ + 32
        with contextlib.ExitStack() as es:
            sb = lambda shape, dtype, name=None: self.sb(es, shape, dtype, name)
            cf = sb([128, NCF], F32, "cf")
            pvt = sb([128, NPV], F32, "pv")
            wpl_f = sb([128, 4, 2, 256], BF16, "wpl")
            ub = [sb([128, SEQ], BF16, "ub") for _ in range(2)]
            bufs = [[sb([128, W], F32, "pb") for _ in range(3)] for _ in range(2)]
            pT = [sb([128, SEQ], BF16, "pT") for _ in range(4)]
            pg = [sb([128, SEQ], BF16, "pg") for _ in range(2)]
            et = [sb([128, 16], F32, "et") for _ in range(2)]
            ps = self.pbanks(es)
            self.dma("sp", cf[:], self.cf[:, :], [], ["cf"], "cf")
            self.dma("sp", pvt[:], self.pvec[:, l, :], [], ["pv"], "pv")
            self.dma("pool", wpl_f[:], self.w_pl[l], [], ["wpl"], "wpl")
            for st in range(2):
                for i in range(3):
                    self.memset("pool" if st else "dve", bufs[st][i][:], 0.0, [("pb", st, i)])
            nb = 0
            for g in range(4):
                w = (2, 4, 8, 16)[g]
                for dc in range(2):
                    ct = g * 2 + dc
                    st = ct % 2
                    eng = "dve" if st == 0 else "pool"
                    U, A, B = bufs[st]
                    kU, kA, kB = ("pb", st, 0), ("pb", st, 1), ("pb", st, 2)
                    row = ZPV + ct * 128
                    self.dma("sp", ub[st][:], self.zT[row:row + 128, :], [], [("ub", st)], ("ub", st))
                    self.cp("act", U[:, PADL:PADL + SEQ], ub[st][:], [("ub", st)], [kU])
                    lo, hi = PADL - 8, PADL + SEQ + 8
                    src, ksrc = U, kU
                    dsts = [(A, kA), (B, kB)]
                    k = 1
                    di = 0
                    while 2 * k < w:
                        d, kd = dsts[di % 2]
                        self.tt(eng, d[:, lo:hi], src[:, lo:hi], src[:, lo + k:hi + k], ALU.add, [ksrc], [kd])
                        src, ksrc = d, kd
                        k *= 2
                        di += 1
                    d, kd = dsts[di % 2]
                    hw = w // 2
                    self.tt(eng, d[:, PADL:PADL + SEQ], src[:, PADL - hw:PADL - hw + SEQ], src[:, PADL:PADL + SEQ],
                            ALU.add, [ksrc], [kd])
                    pk = ("pT", g % 2, dc)
                    pdst = pT[(g % 2) * 2 + dc]
                    if True:
                        self.stt("dve", pdst[:], d[:, PADL:PADL + SEQ], 1.0 / w, U[:, PADL:PADL + SEQ], ALU.mult, ALU.subtract,
                                 [kd, kU], [pk])
                    else:
                        o_, ko_ = dsts[(di + 1) % 2]
                        self.ts(eng, o_[:, PADL:PADL + SEQ], d[:, PADL:PADL + SEQ], 1.0 / w, None, ALU.mult, None, [kd], [ko_])
                        self.tt(eng, pdst[:], o_[:, PADL:PADL + SEQ], U[:, PADL:PADL + SEQ], ALU.subtract, [ko_, kU], [pk])
                    e_ = et[st]
                    c0 = CF_PINV + g * 16
                    self.tt(eng, e_[:, 0:8], d[:, PADL:PADL + 8], cf[:, c0:c0 + 8], ALU.mult, [kd, "cf"], [("et", st)])
                    self.tt(eng, pdst[:, 0:8], e_[:, 0:8], U[:, PADL:PADL + 8], ALU.subtract, [("et", st), kU, pk], [pk])
                    self.tt(eng, e_[:, 8:16], d[:, PADL + SEQ - 8:PADL + SEQ], cf[:, c0 + 8:c0 + 16], ALU.mult,
                            [kd, "cf", ("et", st)], [("et", st)])
                    self.tt(eng, pdst[:, SEQ - 8:SEQ], e_[:, 8:16], U[:, PADL + SEQ - 8:PADL + SEQ], ALU.subtract,
                            [("et", st), kU, pk], [pk])
                for ec in range(2):
                    row = ZPG + g * 256 + ec * 128
                    self.dma("sp", pg[ec][:], self.zT[row:row + 128, :], [], [("pg", ec)], ("pg", ec))
                for ec in range(2):
                    s = ec
                    for tt in range(8):
                        b = nb % 8
                        nb += 1
                        tk = slice(tt * 512, (tt + 1) * 512)
                        for dc in range(2):
                            self.mm(ps[b][:], wpl_f[:, g, dc, ec * 128:(ec + 1) * 128], pT[(g % 2) * 2 + dc][:, tk],
                                    dc == 0, dc == 1, ["wpl", ("pT", g % 2, dc)], [("ps", b)])
                        self.stt("dve", pg[s][:, tk], ps[b][:], pvt[:, 16 + g * 2 + ec:17 + g * 2 + ec], pg[s][:, tk],
                                 ALU.mult, ALU.mult, [("ps", b), "pv", ("pg", s)], [("pg", s)])
                    orow = 1024 + g * 256 + ec * 128
                    self.dma("sp", self.bin[orow:orow + 128, :], pg[s][:], [("pg", s)], [], ("pg", s))
            S.end_phase()

    def phase_att(self, l, g):
        S = self.S
        with contextlib.ExitStack() as es:
            sb = lambda shape, dtype, name=None: self.sb(es, shape, dtype, name)
            cbt = sb([128, NCB], BF16, "cb")
            pvt = sb([128, NPV], F32, "pv")
            gains = sb([128, 2], F32, "gains")
            sinkt = sb([128, 4, 128], F32, "sinkt")
            sinke = sb([128, 8], F32, "sinke")
            qn = sb([128, 5, SEQ], BF16, "qn")
            v = sb([128, NCHK, 128], BF16, "v")
            agst = sb([128, 4, SEQ], BF16, "agst")
            ps = self.pbanks(es)
            cfb = sb([128, 8], F32, "cfb")
            self.dma("sp", cfb[:], self.cf[:, CF_B:CF_B + 8], [], ["cfb"], "cfb")
            self.dma("sp", cbt[:], self.cb[:, :], [], ["cb"], "cb")
            self.dma("sp", pvt[:], self.pvec[:, l, :], [], ["pv"], "pv")
            vsrc = self.vtok[:, 1024 + g * 128:1024 + (g + 1) * 128].rearrange("(n p) c -> p n c", p=128)
            for i in range(4):
                self.dma("sp", v[:, i * 8:(i + 1) * 8, :], vsrc[:, i * 8:(i + 1) * 8, :], [], ["v"], "v")
            self.dma("sp", agst[:], self.zT[ZAG + g * 512:ZAG + (g + 1) * 512, :].rearrange("(c p) t -> p c t", p=128),
                     [], ["agst"], "agst")
            ones = cbt[:, CB_ONES:CB_ONES + 128]
            PT = cbt[:, CB_PT:CB_PT + 128]
            self.ts("dve", gains[:], pvt[:, 24:26], float(np.sqrt(128.0)), None, ALU.mult, None, ["pv"], ["gains"])
            self.act(sinke[:], pvt[:, 26:34], AF.Exp, ["pv", "cfb"], ["sinke"], bias=cfb[:, 3:4])
            self.cp("dve", sinkt[:], sinke[:, g * 4:(g + 1) * 4].unsqueeze(2).to_broadcast([128, 4, 128]),
                    ["sinke"], ["sinkt"])
            sinkrow = sb([128, 4, 128], BF16, "sinkrow")
            self.cp("dve", sinkrow[:], sinkt[:], ["sinkt"], ["sinkrow"])
            with contextlib.ExitStack() as es2:
                sb2 = lambda shape, dtype, name=None: self.sb(es2, shape, dtype, name)
                ctab = sb2([128, SEQ], F32, "ctab")
                stab = sb2([128, SEQ], F32, "stab")
                raw = [sb2([128, SEQ], BF16, "raw") for _ in range(2)]
                sq = [sb2([128, 512], BF16, "sq") for _ in range(4)]
                rs = [sb2([128, 512], F32, "rs") for _ in range(4)]
                qq = [sb2([128, 512], BF16, "qq") for _ in range(4)]
                t1 = [sb2([128, 512], F32, "t1") for _ in range(4)]
                t2 = [sb2([128, 512], F32, "t2") for _ in range(4)]
                rtm = [sb2([128, 512], F32, "rtm") for _ in range(4)]
                self.dma("sp", ctab[:], self.acos[:, :], [], ["ctab"], "ctab")
                self.dma("sp", stab[:], self.asin[:, :], [], ["stab"], "stab")
                pieces = []
                for hh in range(5):
                    r = hh % 2
                    row = (ZAQ + (g * 4 + hh) * 128) if hh < 4 else (ZAK + g * 128)
                    gcol = gains[:, 0:1] if hh < 4 else gains[:, 1:2]
                    for pc in range(8):
                        pieces.append((hh, r, row, gcol, pc))

                def p1(i):
                    hh, r, row, gcol, pc = pieces[i]
                    s = i % 4
                    tk = slice(pc * 512, (pc + 1) * 512)
                    if pc == 0:
                        self.dma("sp", raw[r][:], self.zT[row:row + 128, :], [], [("raw", r)], ("raw", r))
                    self.act(sq[s][:], raw[r][:, tk], AF.Square, [("raw", r)], [("sq", s)])
                    self.mm(ps[s][:], ones, sq[s][:], True, True, [("sq", s), "cb"], [("ps", s)])

                def p2(i):
                    hh, r, row, gcol, pc = pieces[i]
                    s = i % 4
                    tk = slice(pc * 512, (pc + 1) * 512)
                    self.rsqrt(rs[s][:], ps[s][:], cfb[:, 2:3], rtm[s][:], [("ps", s), "cfb"], ("rtm", s), ("rs", s))
                    self.stt("dve", qq[s][:], raw[r][:, tk], gcol, rs[s][:], ALU.mult, ALU.mult,
                             [("raw", r), ("rs", s), "gains"], [("qq", s)])
                    self.mm(ps[4 + s][:], PT, qq[s][:], True, True, [("qq", s), "cb"], [("ps", 4 + s)])

                def p3(i):
                    hh, r, row, gcol, pc = pieces[i]
                    s = i % 4
                    tk = slice(pc * 512, (pc + 1) * 512)
                    self.tt("dve", t1[s][:], qq[s][:], ctab[:, tk], ALU.mult, [("qq", s), "ctab"], [("t1", s)])
                    self.tt("dve", t2[s][:], ps[4 + s][:], stab[:, tk], ALU.mult, [("ps", 4 + s), "stab"], [("t2", s)])
                    self.tt("pool", qn[:, hh, tk], t1[s][:], t2[s][:], ALU.add, [("t1", s), ("t2", s)], [("qn", hh, pc)])

                NP = len(pieces)
                for i in range(NP + 2):
                    if i < NP:
                        p1(i)
                    if 0 <= i - 2 < NP:
                        p3(i - 2)
                    if 0 <= i - 1 < NP:
                        p2(i - 1)
                S.end_phase()
            with contextlib.ExitStack() as es3:
                sb3 = lambda shape, dtype, name=None: self.sb(es3, shape, dtype, name)
                pT = [sb3([128, 3, 512], BF16, "pT") for _ in range(2)]
                den = [sb3([128, 512], F32, "den") for _ in range(2)]
                o1 = [sb3([128, 512], F32, "o1") for _ in range(2)]
                MP = cbt[:, CB_MP:CB_MP + 128]
                MN = cbt[:, CB_MN:CB_MN + 128]
                scale = float(128.0 ** -0.5)
                def stage_s(n):
                    s = n % 2
                    segs = [m for m in (n - 1, n, n + 1) if 0 <= m < NCHK]
                    qtk = slice(n * 128, (n + 1) * 128)
                    for si, m in enumerate(segs):
                        b = s * 3 + si
                        self.mm(ps[b][:].rearrange("p (a b) -> p a b", a=4), qn[:, 4, m * 128:(m + 1) * 128], qn[:, 0:4, qtk], True, True, [], [("ps", b)])
                        self.act(pT[s][:, si, :], ps[b][:], AF.Exp, [("ps", b)], [("pT", s, si)], scale=scale, bias=cfb[:, 3:4])
                        if m != n:
                            mk = MP if m < n else MN
                            pv_ = pT[s][:, si, :].rearrange("p (a b) -> p a b", a=4)
                            self.tt("dve", pv_, pv_, mk.unsqueeze(1).to_broadcast([128, 4, 128]),
                                    ALU.mult, [("pT", s, si), "cb"], [("pT", s, si)])
                def stage_o(n):
                    s = n % 2
                    segs = [m for m in (n - 1, n, n + 1) if 0 <= m < NCHK]
                    qtk = slice(n * 128, (n + 1) * 128)
                    for si, m in enumerate(segs):
                        self.mm(ps[6][:], v[:, m, :], pT[s][:, si, :], si == 0, si == len(segs) - 1,
                                [("pT", s, si), "v"], [("ps", 6)])
                    for si, m in enumerate(segs):
                        self.mm(ps[7][:], ones, pT[s][:, si, :], si == 0, False,
                                [("pT", s, si), "cb"], [("ps", 7)], inc=False)
                    self.mm(ps[7][:], cbt[0:1, CB_ONES:CB_ONES + 128], sinkrow[0:1].rearrange("p a b -> p (a b)"), False, True,
                            ["cb", "sinkrow"], [("ps", 7)])
                    self.act(den[s][:], ps[7][:], AF.Ln, [("ps", 7)], [("den", s)])
                    self.act(den[s][:], den[s][:], AF.Exp, [("den", s)], [("den", s)], scale=-1.0)
                    self.tt("dve", o1[s][:], ps[6][:], den[s][:], ALU.mult, [("ps", 6), ("den", s)], [("o1", s)])
                    self.tt("dve", agst[:, :, qtk], o1[s][:].rearrange("p (a b) -> p a b", a=4), agst[:, :, qtk], ALU.mult,
                            [("o1", s), "agst", ("ao", n - 1)], [("ao", n)])
                stage_s(0)
                for n in range(NCHK):
                    if n + 1 < NCHK:
                        stage_s(n + 1)
                    stage_o(n)
                orow = 2048 + g * 512
                self.dma("sp", self.bin[orow:orow + 512, :].rearrange("(c p) t -> p c t", p=128), agst[:],
                         [("ao", NCHK - 1), "agst"], [], "agst")
                S.end_phase()

    def phase_merge(self, l, tq, x_cur, x_nxt):
        S = self.S
        T0 = tq * 1024
        with contextlib.ExitStack() as es:
            sb = lambda shape, dtype, name=None: self.sb(es, shape, dtype, name)
            binq = sb([128, 24, 1024], BF16, "binq")
            mT = sb([128, 16, 1024], BF16, "mT")
            wbr = [sb([128, 3, 8, 128], BF16, "wbr") for _ in range(2)]
            wbs = [sb([128, 3, 8, 128], F32, "wbs") for _ in range(2)]
            wos = [sb([128, 16, 128], F32, "wos") for _ in range(2)]
            gq = [sb([128, 3, 1024], BF16, "gq") for _ in range(2)]
            wo = [sb([128, 16, 128], BF16, "wo") for _ in range(2)]
            xq = [sb([128, 1024], F32, "xq") for _ in range(3)]
            s1 = [sb([128, 512], F32, "s1") for _ in range(2)]
            s2 = [sb([128, 512], F32, "s2") for _ in range(2)]
            ta = [sb([128, 512], F32, "ta") for _ in range(2)]
            ps = self.pbanks(es)
            for br in range(3):
                self.dma("sp", binq[:, br * 8:(br + 1) * 8, :],
                         self.bin[br * 1024:(br + 1) * 1024, T0:T0 + 1024].rearrange("(c p) t -> p c t", p=128),
                         [], [("binq", br)], ("binq", br))
            gv = self.zT[ZMG:ZMG + 6144, :].rearrange("(br jj p) t -> p br jj t", br=3, p=128)
            it = 0
            def load_b(j):
                s = j % 2
                self.dma("sp", wbs[s][:], self.w_br[l, j], [], [("wbs", s)], ("wbs", s))
                self.dma("sp", gq[s][:], gv[:, :, j, T0:T0 + 1024], [], [("gq", s)], ("gq", s))

            load_b(0)
            for j in range(16):
                s = j % 2
                self.cp("act", wbr[s][:].rearrange("p a b c -> p (a b c)"), wbs[s][:].rearrange("p a b c -> p (a b c)"),
                        [("wbs", s)], [("wbr", s)])
                if j + 1 < 16:
                    load_b(j + 1)
                for tt in range(2):
                    u = it % 2
                    it += 1
                    tk = slice(tt * 512, (tt + 1) * 512)
                    bb = [(it % 2) * 3 + br for br in range(3)]
                    for br in range(3):
                        for ec in range(8):
                            self.mm(ps[bb[br]][:], wbr[s][:, br, ec, :], binq[:, br * 8 + ec, tk], ec == 0, ec == 7,
                                    [("wbr", s), ("binq", br)], [("ps", bb[br])])
                    self.tt("dve", ta[u][:], ps[bb[0]][:], gq[s][:, 0, tk], ALU.mult, [("ps", bb[0]), ("gq", s)], [("ta", u)])
                    self.tt("dve", s1[u][:], ps[bb[1]][:], gq[s][:, 1, tk], ALU.mult, [("ps", bb[1]), ("gq", s)], [("s1", u)])
                    self.tt("dve", s2[u][:], ps[bb[2]][:], gq[s][:, 2, tk], ALU.mult, [("ps", bb[2]), ("gq", s)], [("s2", u)])
                    self.tt("pool", ta[u][:], ta[u][:], s1[u][:], ALU.add, [("ta", u), ("s1", u)], [("ta", u)])
                    self.tt("dve", mT[:, j, tk], ta[u][:], s2[u][:], ALU.add, [("ta", u), ("s2", u)], [("mT", j, tt)])
            mkeys = lambda tt: [("mT", j, tt) for j in range(16)]
            def load_o(i):
                s = i % 2
                x3 = i % 3
                self.dma("sp", wos[s][:], self.w_o[l, i], [], [("wos", s)], ("wos", s))
                self.dma("sp", xq[x3][:], x_cur[i * 128:(i + 1) * 128, T0:T0 + 1024], [], [("xq", x3)], ("xq", x3))

            load_o(0)
            for i in range(16):
                s = i % 2
                x3 = i % 3
                if i + 1 < 16:
                    load_o(i + 1)
                self.cp("act", wo[s][:].rearrange("p a b -> p (a b)"), wos[s][:].rearrange("p a b -> p (a b)"),
                        [("wos", s)], [("wo", s)])
                for tt in range(2):
                    b = 6 + tt
                    tk = slice(tt * 512, (tt + 1) * 512)
                    for jc in range(16):
                        self.mm(ps[b][:], wo[s][:, jc, :], mT[:, jc, tk], jc == 0, jc == 15,
                                [("wo", s)] + mkeys(tt), [("ps", b)])
                    self.tt("dve", xq[x3][:, tk], ps[b][:], xq[x3][:, tk], ALU.add, [("ps", b), ("xq", x3)], [("xq", x3)])
                self.dma("sp", x_nxt[i * 128:(i + 1) * 128, T0:T0 + 1024], xq[x3][:], [("xq", x3)], [], ("xq", x3))
            S.end_phase()


def _consts():
    f32 = np.float32
    t = np.arange(SEQ, dtype=f32)
    inv_r = (1.0 / (f32(10000.0) ** np.linspace(0.0, 1.0, 128, dtype=f32))).astype(f32)
    ang = (t[:, None] * inv_r[None, :]).astype(f32)
    rcos = np.ascontiguousarray(np.cos(ang).T.astype(f32))
    rsin = np.ascontiguousarray(np.sin(ang).T.astype(f32))
    inv_a = (f32(500000.0) ** (-np.arange(16, dtype=f32) / f32(16.0))).astype(f32)
    anga = (t[:, None] * inv_a[None, :]).astype(f32)
    acos = np.ones((128, SEQ), f32)
    asin = np.zeros((128, SEQ), f32)
    acos[0:16] = np.cos(anga).T
    acos[16:32] = np.cos(anga).T
    asin[0:16] = np.sin(anga).T
    asin[16:32] = np.sin(anga).T
    cf = np.zeros((128, NCF), f32)
    li = np.arange(128, dtype=f32)[:, None]
    ji = np.arange(128, dtype=f32)[None, :]
    cf[:, CF_LF:CF_LF + 128] = np.maximum(ji - li, 0)
    cf[:, CF_LB:CF_LB + 128] = np.maximum(li - ji, 0)
    cf[:, CF_JF:CF_JF + 128] = ji + 1.0
    cf[:, CF_JB:CF_JB + 128] = 128.0 - ji
    cf[:, CF_COL] = 127.0 - li[:, 0]
    cf[:, CF_COL + 1] = li[:, 0]
    cf[:, CF_COL + 2] = 128.0
    cf[:, CF_B:CF_B + 5] = np.array([2048.0 * EPS, 256.0 * EPS, 128.0 * EPS, -M0, -np.log(16.0)], f32)[None, :]
    for g, w in enumerate((2, 4, 8, 16)):
        hw = w // 2
        left = np.zeros(8, f32)
        right = np.zeros(8, f32)
        for i in range(8):
            n = i
            lo, hi = max(n - hw, 0), min(n + hw, SEQ)
            left[i] = 1.0 / (hi - lo)
            n = SEQ - 8 + i
            lo, hi = max(n - hw, 0), min(n + hw, SEQ)
            right[i] = 1.0 / (hi - lo)
        cf[:, CF_PINV + g * 16:CF_PINV + g * 16 + 8] = left[None, :]
        cf[:, CF_PINV + g * 16 + 8:CF_PINV + g * 16 + 16] = right[None, :]
    cb = np.zeros((128, NCB), f32)
    cb[:, CB_ONES:CB_ONES + 128] = 1.0
    cb[:, CB_ID:CB_ID + 128] = np.eye(128, dtype=f32)
    PT = np.zeros((128, 128), f32)
    for m in range(16):
        PT[m + 16, m] = -1.0
        PT[m, m + 16] = 1.0
    cb[:, CB_PT:CB_PT + 128] = PT
    cb[:, CB_MP:CB_MP + 128] = (li >= ji).astype(f32)
    cb[:, CB_MN:CB_MN + 128] = (li <= ji).astype(f32)
    return dict(rcos=rcos, rsin=rsin, acos=acos, asin=asin, cf=cf, cb=cb.astype(ml_dtypes.bfloat16))


def _prep_weights(inputs):
    f32 = np.float32
    w_in = inputs["w_in"]
    fmcols = np.concatenate([np.arange(z0, z0 + w) for (z0, w, _) in FM_SEGS])
    vcols = np.concatenate([np.arange(z0, z0 + w) for (z0, w) in V_SEGS])
    win_fm = np.empty((NLAYER, NFM, 128, 16, 128), f32)
    win_v = np.empty((NLAYER, 128, 16, 1280), f32)
    for l in range(NLAYER):
        a = w_in[l][:, fmcols].reshape(16, 128, NFM, 128)
        win_fm[l] = a.transpose(2, 1, 0, 3)
        b = w_in[l][:, vcols].reshape(16, 128, 1280)
        win_v[l] = b.transpose(1, 0, 2)
    wb = np.stack([inputs["w_ret"], inputs["w_pool"], inputs["w_att"]], axis=1)
    w_br = np.ascontiguousarray(wb.reshape(NLAYER, 3, 8, 128, 16, 128).transpose(0, 4, 3, 1, 2, 5))
    w_o = np.ascontiguousarray(inputs["w_out"].reshape(NLAYER, 16, 128, 16, 128).transpose(0, 3, 2, 1, 4))
    w_pl = np.ascontiguousarray(inputs["pool_w"].reshape(NLAYER, 4, 2, 128, 256).transpose(0, 3, 1, 2, 4))
    pvec = np.zeros((128, NLAYER, NPV), f32)
    for l in range(NLAYER):
        pvec[:, l, 0:16] = inputs["norm_g"][l].reshape(16, 128).T
        pvec[:, l, 16:24] = inputs["pool_scale"][l].reshape(8, 128).T
        pvec[:, l, 24] = inputs["attn_q_gain"][l]
        pvec[:, l, 25] = inputs["attn_k_gain"][l]
        pvec[:, l, 26:34] = inputs["attn_sink"][l][None, :]
        pvec[:, l, 34:38] = inputs["ret_decay_fwd"][l][None, :]
        pvec[:, l, 38:42] = inputs["ret_decay_bwd"][l][None, :]
    return dict(win_fm=win_fm, win_v=win_v, w_br=w_br, w_o=w_o, w_pl=w_pl, pvec=pvec)


def kernel(**inputs):
    inputs = {k: np.asarray(v) for k, v in inputs.items()}
    x = inputs["x"]
    B = x.shape[0]
    shared = _prep_weights(inputs)
    shared.update(_consts())
    prog = Prog()
    nc = prog.build()
    in_maps = []
    for b in range(B):
        m = dict(shared)
        m["xT"] = np.ascontiguousarray(x[b].T)
        in_maps.append(m)
    res = run_bass_kernel_spmd(nc, in_maps, core_ids=list(range(B)))
    out = np.stack([np.ascontiguousarray(r["outT"].T) for r in res.results], axis=0)
    return out.astype(np.float32)
```

```python
import contextlib
import numpy as np
import ml_dtypes
import concourse.bass as bass
import concourse.mybir as mybir
from concourse.bass_utils import run_bass_kernel_spmd

F32 = mybir.dt.float32
BF16 = mybir.dt.bfloat16
AF = mybir.ActivationFunctionType
ALU = mybir.AluOpType

SEQ = 4096
DM = 2048
NLAYER = 4
NCHK = SEQ // 128
EPS = 1e-6
M0 = 12.0

ZRQ, ZRK, ZRG, ZPV, ZPG, ZAQ, ZAK, ZAG, ZMG = 0, 1024, 2048, 3072, 4096, 5120, 6144, 6400, 7424
ZROWS = 13568
NFM = ZROWS // 128
FM_SEGS = [(0, 1024, "copy"), (1024, 1024, "copy"), (3072, 1024, "silu"), (4096, 1024, "copy"),
           (5120, 1024, "silu"), (6144, 1024, "copy"), (7168, 256, "copy"), (7680, 1024, "silu"),
           (8704, 6144, "sigmoid")]
V_SEGS = [(2048, 1024), (7424, 256)]
NPV = 42
CF_LF, CF_LB, CF_JF, CF_JB, CF_COL, CF_PINV = 0, 128, 256, 384, 512, 516
NCF = 516 + 64 + 8
CF_B = 580
CB_ONES, CB_ID, CB_PT, CB_MP, CB_MN = 0, 128, 256, 384, 512
NCB = 640 + 20 * 128
CB_POOL = 640


class Sched:
    ENG = ("pe", "act", "dve", "pool", "sp")

    def __init__(self, nc, sems):
        self.nc = nc
        pool = list(sems)
        self.esem = {e: pool.pop() for e in self.ENG[:4]}
        self.ecnt = {e: 0 for e in self.ENG[:4]}
        self.pe_pending = False
        self.dma_pool = pool
        self.dsem = {}
        self.dcnt = {id(s): 0 for s in pool}
        self.semobj = {id(s): s for s in pool}
        for s in self.esem.values():
            self.semobj[id(s)] = s
        self.known = {e: {} for e in self.ENG}
        self.lastw = {}
        self.readers = {}
        self.ops = {e: [] for e in self.ENG}
        self.nops = 0
        self.check = False
        self.maxops = None
        self.simval = {}

    def _dma_sem(self, key):
        if key not in self.dsem:
            idx = len(self.dsem)
            assert idx < len(self.dma_pool), "out of DMA semaphores"
            self.dsem[key] = self.dma_pool[idx]
        return self.dsem[key]

    def _need(self, eng, dep, waits, is_mm):
        semid, val, src = dep
        if src == "pe" and eng == "pe" and is_mm:
            return
        if semid in self.dcnt:
            val = self.dcnt[semid]
        if self.known[eng].get(semid, -1) >= val:
            return
        self.known[eng][semid] = val
        waits[semid] = max(waits.get(semid, 0), val)

    def op(self, eng, fn, reads=(), writes=(), dma_key=None, is_mm=False, inc=True):
        if self.maxops is not None and self.nops >= self.maxops:
            return
        waits = {}
        for k in list(reads) + list(writes):
            d = self.lastw.get(k)
            if d is not None:
                self._need(eng, d, waits, is_mm)
        for k in writes:
            for semid, (val, src) in self.readers.get(k, {}).items():
                self._need(eng, (semid, val, src), waits, is_mm)
        if dma_key is not None:
            s = self._dma_sem(dma_key)
            self.dcnt[id(s)] += 16
            ev = (id(s), self.dcnt[id(s)], "dma")
            incr = (s, 16)
        else:
            s = self.esem[eng]
            if inc:
                self.ecnt[eng] += 1
                ev = (id(s), self.ecnt[eng], eng)
                incr = (s, 1)
                if eng == "pe":
                    self.pe_pending = False
            else:
                assert eng == "pe" and is_mm
                ev = (id(s), self.ecnt[eng] + 1, eng)
                incr = None
                self.pe_pending = True
        for k in writes:
            self.lastw[k] = ev
            self.readers[k] = {}
        for k in reads:
            r = self.readers.setdefault(k, {})
            r[ev[0]] = (ev[1], ev[2])
        self.ops[eng].append((fn, [(self.semobj[sid], v) for sid, v in waits.items()], incr))
        self.nops += 1

    def _simulate(self, ops):
        val = self.simval
        pc = {e: 0 for e in self.ENG}
        progress = True
        while progress:
            progress = False
            for e in self.ENG:
                lst = ops[e]
                while pc[e] < len(lst):
                    fn, waits, incr = lst[pc[e]]
                    if any(val.get(id(s), 0) < v for s, v in waits):
                        break
                    if incr is not None:
                        val[id(incr[0])] = val.get(id(incr[0]), 0) + incr[1]
                    pc[e] += 1
                    progress = True
        stuck = {e: (pc[e], len(ops[e])) for e in self.ENG if pc[e] < len(ops[e])}
        if stuck:
            msg = []
            for e, (p, n) in stuck.items():
                fn, waits, incr = ops[e][p]
                msg.append(f"{e}: op {p}/{n} waits " + str([(s.name if hasattr(s, 'name') else id(s), v, val.get(id(s), 0)) for s, v in waits]))
            raise RuntimeError("DEADLOCK in recorded program: " + " | ".join(msg))

    def end_phase(self):
        assert self.maxops is not None or not self.pe_pending
        targets = []
        for e in self.ENG[:4]:
            if self.ecnt[e] > 0:
                targets.append((id(self.esem[e]), self.ecnt[e]))
        for sid, c in self.dcnt.items():
            if c > 0:
                targets.append((sid, c))
        for e in self.ENG:
            waits = []
            for sid, v in targets:
                if self.known[e].get(sid, -1) >= v:
                    continue
                self.known[e][sid] = v
                waits.append((self.semobj[sid], v))
            if waits:
                self.ops[e].append((None, waits, None))
        self.lastw = {}
        self.readers = {}
        self.dsem = {}
        if self.check:
            self._simulate(self.ops)
        nc = self.nc
        ops = self.ops
        self.ops = {e: [] for e in self.ENG}

        def replay(engine, lst):
            for fn, waits, incr in lst:
                for s, v in waits:
                    engine.wait_ge(s, v)
                if fn is not None:
                    ins = fn(engine)
                    if incr is not None:
                        ins.then_inc(incr[0], incr[1])

        with nc.Block() as block:
            @block.tensor
            def _(e):
                replay(e, ops["pe"])

            @block.scalar
            def _(e):
                replay(e, ops["act"])

            @block.vector
            def _(e):
                replay(e, ops["dve"])

            @block.gpsimd
            def _(e):
                replay(e, ops["pool"])

            @block.sync
            def _(e):
                replay(e, ops["sp"])


class Prog:
    def __init__(self, nlayers=NLAYER, dbg=False, stop_after=None, only=None):
        self.only = only
        self.nlayers = nlayers
        self.dbg = dbg
        self.stop_after = stop_after
        nc = bass.Bass("TRN2", target_bir_lowering=False)
        self.nc = nc
        ein = "ExternalInput"
        sk = "ExternalOutput" if dbg else "Internal"
        dt = nc.dram_tensor
        if not only:
            self.xT = dt("xT", [DM, SEQ], F32, kind=ein).ap()
            self.win_fm = dt("win_fm", [NLAYER, NFM, 128, 16, 128], F32, kind=ein).ap()
            self.win_v = dt("win_v", [NLAYER, 128, 16, 1280], F32, kind=ein).ap()
            self.w_br = dt("w_br", [NLAYER, 16, 128, 3, 8, 128], F32, kind=ein).ap()
            self.w_o = dt("w_o", [NLAYER, 16, 128, 16, 128], F32, kind=ein).ap()
        self.w_pl = dt("w_pl", [NLAYER, 128, 4, 2, 256], F32, kind=ein).ap()
        self.pvec = dt("pvec", [128, NLAYER, NPV], F32, kind=ein).ap()
        self.cf = dt("cf", [128, NCF], F32, kind=ein).ap()
        self.cb = dt("cb", [128, NCB], BF16, kind=ein).ap()
        self.rcos = dt("rcos", [128, SEQ], F32, kind=ein).ap()
        self.rsin = dt("rsin", [128, SEQ], F32, kind=ein).ap()
        self.acos = dt("acos", [128, SEQ], F32, kind=ein).ap()
        self.asin = dt("asin", [128, SEQ], F32, kind=ein).ap()
        if not only:
            self.outT = dt("outT", [DM, SEQ], F32, kind="ExternalOutput").ap()
        self.zT = dt("zT", [ZROWS, SEQ], BF16, kind=(ein if only else sk)).ap()
        self.vtok = dt("vtok", [SEQ, 1280], BF16, kind=(ein if only else sk)).ap()
        self.bin = dt("bin", [3072, SEQ], BF16, kind=sk).ap()
        if not only:
            self.xs = [dt("xs0", [DM, SEQ], F32, kind=sk).ap(), dt("xs1", [DM, SEQ], F32, kind=sk).ap()]
        self._uid = 0

    def sb(self, es, shape, dtype, name=None):
        self._uid += 1
        return es.enter_context(self.nc.sbuf_tensor(f"{name or 't'}_{self._uid}", list(shape), dtype))

    def pbanks(self, es, n=8, dtype=F32, cols=512):
        out = []
        for _ in range(n):
            self._uid += 1
            out.append(es.enter_context(self.nc.psum_tensor(f"ps_{self._uid}", [128, cols], dtype)))
        return out

    def dma(self, q, out, in_, reads, writes, key):
        self.S.op(q, lambda e: e.dma_start(out=out, in_=in_), reads=reads, writes=writes, dma_key=key)

    def mm(self, out, lhsT, rhs, start, stop, reads, writes, inc=None):
        if inc is None:
            inc = stop
        self.S.op("pe", lambda e: e.matmul(out, lhsT=lhsT, rhs=rhs, start=start, stop=stop),
                  reads=reads, writes=writes, is_mm=True, inc=inc)

    def act(self, out, in_, func, reads, writes, scale=None, bias=None):
        kw = {}
        if scale is not None:
            kw["scale"] = scale
        if bias is not None:
            kw["bias"] = bias
        self.S.op("act", lambda e: e.activation(out=out, in_=in_, func=func, **kw), reads=reads, writes=writes)

    def tt(self, eng, out, in0, in1, op, reads, writes):
        self.S.op(eng, lambda e: e.tensor_tensor(out=out, in0=in0, in1=in1, op=op), reads=reads, writes=writes)

    def ts(self, eng, out, in0, s1, s2, op0, op1, reads, writes):
        if op1 is None:
            self.S.op(eng, lambda e: e.tensor_scalar(out=out, in0=in0, scalar1=s1, scalar2=None, op0=op0),
                      reads=reads, writes=writes)
        else:
            self.S.op(eng, lambda e: e.tensor_scalar(out=out, in0=in0, scalar1=s1, scalar2=s2, op0=op0, op1=op1),
                      reads=reads, writes=writes)

    def stt(self, eng, out, in0, scalar, in1, op0, op1, reads, writes):
        self.S.op(eng, lambda e: e.scalar_tensor_tensor(out=out, in0=in0, scalar=scalar, in1=in1, op0=op0, op1=op1),
                  reads=reads, writes=writes)

    def cp(self, eng, out, in_, reads, writes):
        if eng == "act":
            self.act(out, in_, AF.Copy, reads, writes)
        else:
            self.S.op(eng, lambda e: e.tensor_copy(out=out, in_=in_), reads=reads, writes=writes)

    def rsqrt(self, out, ps_ap, bias_ap, tmp, reads, ktmp, kout):
        self.act(tmp, ps_ap, AF.Ln, reads, [ktmp], bias=bias_ap)
        self.act(out, tmp, AF.Exp, [ktmp], [kout], scale=-0.5)

    def memset(self, eng, ap, val, writes):
        self.S.op(eng, lambda e: e.memset(ap, val), writes=writes)

    def build(self):
        nc = self.nc
        with contextlib.ExitStack() as es:
            sems = [es.enter_context(nc.semaphore(f"sem{i}")) for i in range(96)]
            fins = [es.enter_context(nc.semaphore(f"fin{i}")) for i in range(4)]
            for s_ in sems + fins:
                nc.gpsimd.sem_clear(s_)
            nc.all_engine_barrier()
            self.S = Sched(nc, sems)
            self._build_body()
            with nc.Block() as block:
                @block.tensor
                def _(e):
                    e.sem_inc(fins[0], 1)

                @block.scalar
                def _(e):
                    e.sem_inc(fins[1], 1)

                @block.vector
                def _(e):
                    e.sem_inc(fins[2], 1)

                @block.sync
                def _(e):
                    e.sem_inc(fins[3], 1)

                @block.gpsimd
                def _(e):
                    for f_ in fins:
                        e.wait_ge(f_, 1)
                    for s_ in sems + fins:
                        e.sem_clear(s_)
            nc.all_engine_barrier()
        return nc

    def _build_body(self):
        nc = self.nc
        if True:
            if self.only:
                if self.only.startswith("ret"):
                    for h in ([int(self.only[3:])] if len(self.only) > 3 else range(4)):
                        self.phase_ret(0, h)
                elif self.only == "pool":
                    self.phase_pool(0)
                elif self.only == "att":
                    for g in range(2):
                        self.phase_att(0, g)
                return
            for l in range(self.nlayers):
                x_cur = self.xT if l == 0 else self.xs[(l - 1) % 2]
                x_nxt = self.outT if l == self.nlayers - 1 else self.xs[l % 2]
                for th in range(2):
                    self.phase_p1(l, th, x_cur)
                if self.stop_after == "p1":
                    break
                for h in range(4):
                    self.phase_ret(l, h)
                if self.stop_after == "ret":
                    break
                self.phase_pool(l)
                if self.stop_after == "pool":
                    break
                for g in range(2):
                    self.phase_att(l, g)
                if self.stop_after == "att":
                    break
                for tq in range(4):
                    self.phase_merge(l, tq, x_cur, x_nxt)

    def phase_p1(self, l, th, x_cur):
        S = self.S
        T0 = th * 2048
        with contextlib.ExitStack() as es:
            hT = self.sb(es, [128, 16, 2048], BF16, "hT")
            xin = [self.sb(es, [128, 16, 256], F32, "xin") for _ in range(2)]
            sq = [self.sb(es, [128, 16, 256], BF16, "sq") for _ in range(2)]
            rstd = [self.sb(es, [128, 256], F32, "rstd") for _ in range(2)]
            gcol = self.sb(es, [128, 16], F32, "gcol")
            ones = self.sb(es, [128, 128], BF16, "ones")
            wch = [self.sb(es, [128, 16, 128], BF16, "wch") for _ in range(3)]
            zst = [self.sb(es, [128, 2048], BF16, "zst") for _ in range(3)]
            wv = [self.sb(es, [128, 16, 256], BF16, "wv") for _ in range(2)]
            vst = [self.sb(es, [128, 16, 256], BF16, "vst") for _ in range(2)]
            ps = self.pbanks(es)

            cfb = self.sb(es, [128, 8], F32, "cfb")
            rtmp = [self.sb(es, [128, 256], F32, "rtmp") for _ in range(2)]
            self.dma("sp", cfb[:], self.cf[:, CF_B:CF_B + 8], [], ["cfb"], "cfb")
            self.dma("sp", gcol[:], self.pvec[:, l, 0:16], [], ["gcol"], "gcol")
            self.dma("sp", ones[:], self.cb[:, CB_ONES:CB_ONES + 128], [], ["ones"], "ones")
            self.ts("dve", gcol[:], gcol[:], float(np.sqrt(2048.0)), None, ALU.mult, None, ["gcol"], ["gcol"])
            xv = x_cur.rearrange("(kc p) t -> p kc t", p=128)
            for t in range(8):
                s = t % 2
                self.dma("sp", xin[s][:], xv[:, :, T0 + t * 256:T0 + (t + 1) * 256], [], [("xin", s)], ("xin", s))
                self.act(sq[s][:].rearrange("p k t -> p (k t)"), xin[s][:].rearrange("p k t -> p (k t)"),
                         AF.Square, [("xin", s)], [("sq", s)])
                for kc in range(16):
                    self.mm(ps[s][:, 0:256], ones[:], sq[s][:, kc, :], kc == 0, kc == 15,
                            [("sq", s), "ones"], [("ps", s)])
                self.rsqrt(rstd[s][:], ps[s][:, 0:256], cfb[:, 0:1], rtmp[s][:], [("ps", s), "cfb"], ("rtmp", s), ("rstd", s))
                self.tt("pool", xin[s][:], xin[s][:], rstd[s][:].unsqueeze(1).to_broadcast([128, 16, 256]),
                        ALU.mult, [("xin", s), ("rstd", s)], [("xin", s)])
                self.tt("dve", hT[:, :, t * 256:(t + 1) * 256], xin[s][:],
                        gcol[:].unsqueeze(2).to_broadcast([128, 16, 256]), ALU.mult,
                        [("xin", s), "gcol"], [("hT", t)])
            hkeys = [("hT", t) for t in range(8)]
            nb = 0
            for c in range(NFM):
                s = c % 3
                fn = "copy"
                row = c * 128
                for (z0, w, f), r0 in zip(FM_SEGS, (ZRQ, ZRK, ZRG, ZPV, ZPG, ZAQ, ZAK, ZAG, ZMG)):
                    if r0 <= row < r0 + w:
                        fn = f
                self.dma("pool", wch[s][:], self.win_fm[l, c], [], [("wch", s)], ("wch", s))
                for tt in range(4):
                    b = 2 + nb % 6
                    nb += 1
                    for kc in range(16):
                        self.mm(ps[b][:], wch[s][:, kc, :], hT[:, kc, tt * 512:(tt + 1) * 512], kc == 0, kc == 15,
                                [("wch", s), ("hT", 2 * tt), ("hT", 2 * tt + 1)], [("ps", b)])
                    o = zst[s][:, tt * 512:(tt + 1) * 512]
                    if fn == "copy":
                        eng = "dve" if (tt % 2 == 0) else "act"
                        self.cp(eng, o, ps[b][:], [("ps", b)], [("zst", s, tt)])
                    else:
                        self.act(o, ps[b][:], AF.Silu if fn == "silu" else AF.Sigmoid, [("ps", b)], [("zst", s, tt)])
                self.dma("sp", self.zT[row:row + 128, T0:T0 + 2048], zst[s][:],
                         [("zst", s, i) for i in range(4)], [], ("zst", s))
            vv = self.vtok[T0:T0 + 2048, :].rearrange("(k p) c -> p k c", p=128)
            for cpi in range(5):
                s = cpi % 2
                self.dma("pool", wv[s][:], self.win_v[l, :, :, cpi * 256:(cpi + 1) * 256], [], [("wv", s)], ("wv", s))
                for tk in range(16):
                    b = 2 + nb % 6
                    nb += 1
                    for kc in range(16):
                        self.mm(ps[b][:, 0:256], hT[:, kc, tk * 128:(tk + 1) * 128], wv[s][:, kc, :], kc == 0, kc == 15,
                                [("wv", s), ("hT", tk // 2)], [("ps", b)])
                    eng = "dve" if (tk % 2 == 0) else "act"
                    self.cp(eng, vst[s][:, tk, :], ps[b][:, 0:256], [("ps", b)], [("vst", s, tk)])
                self.dma("sp", vv[:, :, cpi * 256:(cpi + 1) * 256], vst[s][:],
                         [("vst", s, i) for i in range(16)], [], ("vst", s))
            S.end_phase()

    def phase_ret(self, l, h):
        S = self.S
        with contextlib.ExitStack() as es:
            sb = lambda shape, dtype, name=None: self.sb(es, shape, dtype, name)
            cf = sb([128, NCF], F32, "cf")
            cbt = sb([128, NCB], BF16, "cb")
            pvt = sb([128, NPV], F32, "pv")
            lg = sb([128, 8], F32, "lg")
            dmask = sb([128, 128], F32, "dmask")
            tmpm = sb([128, 128], F32, "tmpm")
            qft = sb([128, 128], F32, "qft")
            qbt = sb([128, 128], F32, "qbt")
            colv = sb([128, 4], F32, "colv")
            qrot = sb([128, 2, SEQ], BF16, "qrot")
            krot = sb([128, 2, SEQ], BF16, "krot")
            ktok = sb([128, NCHK, 256], BF16, "ktok")
            v = sb([128, NCHK, 256], BF16, "v")
            sbb = sb([128, NCHK, 2, 256], BF16, "sbb")
            gst = sb([128, 2, SEQ], BF16, "gst")
            raw = [sb([128, 2, 1024], BF16, "raw") for _ in range(2)]
            cs = [sb([128, 2, 1024], F32, "cs") for _ in range(2)]
            tmp = [sb([128, 1024], F32, "rt") for _ in range(4)]
            St = [sb([128, 2, 256], F32, "St") for _ in range(2)]
            sfc = [sb([128, 2, 256], BF16, "sfc") for _ in range(3)]
            vwa = sb([128, NCHK, 256], BF16, "vwa")
            sT = [sb([128, 128], BF16, "sT") for _ in range(2)]
            qf = [sb([128, 2, 128], BF16, "qf") for _ in range(2)]
            qb = [sb([128, 2, 128], BF16, "qb") for _ in range(2)]
            sqo = [sb([128, 2, 128], BF16, "sqo") for _ in range(2)]
            rs = [sb([128, 128], F32, "rs") for _ in range(2)]
            y1 = [sb([128, 2, 128], F32, "y1") for _ in range(2)]
            rtm = [sb([128, 128], F32, "rtm") for _ in range(2)]
            ps = self.pbanks(es, 6)
            pst = self.pbanks(es, 2, BF16, 1024)

            self.dma("sp", cf[:], self.cf[:, :], [], ["cf"], "cf")
            self.dma("sp", cbt[:], self.cb[:, :], [], ["cb"], "cb")
            self.dma("sp", pvt[:], self.pvec[:, l, :], [], ["pv"], "pv")
            vsrc = self.vtok[:, h * 256:(h + 1) * 256].rearrange("(n p) c -> p n c", p=128)
            for i in range(4):
                self.dma("sp", v[:, i * 8:(i + 1) * 8, :], vsrc[:, i * 8:(i + 1) * 8, :], [], ["v"], "v")
            self.dma("sp", gst[:], self.zT[ZRG + h * 256:ZRG + (h + 1) * 256, :].rearrange("(c p) t -> p c t", p=128),
                     [], ["gst"], "gst")
            ones = cbt[:, CB_ONES:CB_ONES + 128]
            ident = cbt[:, CB_ID:CB_ID + 128]
            self.act(lg[:], pvt[:, 34:42], AF.Exp, ["pv"], ["lg"])
            self.ts("dve", lg[:], lg[:], -1.0, None, ALU.mult, None, ["lg"], ["lg"])
            lgf = lg[:, h:h + 1]
            lgb = lg[:, 4 + h:5 + h]
            self.ts("dve", tmpm[:], cf[:, CF_LF:CF_LF + 128], lgf, None, ALU.mult, None, ["cf", "lg"], ["tmpm"])
            self.stt("dve", tmpm[:], cf[:, CF_LB:CF_LB + 128], lgb, tmpm[:], ALU.mult, ALU.add,
                     ["cf", "lg", "tmpm"], ["tmpm"])
            self.act(dmask[:], tmpm[:], AF.Exp, ["tmpm"], ["dmask"], bias=cf[:, CF_B + 4:CF_B + 5])
            self.act(qft[:], cf[:, CF_JF:CF_JF + 128], AF.Exp, ["cf", "lg"], ["qft"], scale=lgf)
            self.act(qbt[:], cf[:, CF_JB:CF_JB + 128], AF.Exp, ["cf", "lg"], ["qbt"], scale=lgb)
            self.act(colv[:, 0:1], cf[:, CF_COL:CF_COL + 1], AF.Exp, ["cf", "lg"], ["colv"], scale=lgf,
                     bias=cf[:, CF_B + 4:CF_B + 5])
            self.act(colv[:, 1:2], cf[:, CF_COL + 1:CF_COL + 2], AF.Exp, ["cf", "lg", "colv"], ["colv"], scale=lgb,
                     bias=cf[:, CF_B + 4:CF_B + 5])
            self.act(colv[:, 2:3], cf[:, CF_COL + 2:CF_COL + 3], AF.Exp, ["cf", "lg", "colv"], ["colv"], scale=lgf)
            self.act(colv[:, 3:4], cf[:, CF_COL + 2:CF_COL + 3], AF.Exp, ["cf", "lg", "colv"], ["colv"], scale=lgb)
            it = 0
            for which, zr, dst in (("q", ZRQ, qrot), ("k", ZRK, krot)):
                for pc in range(4):
                    s = it % 2
                    it += 1
                    t0 = pc * 1024
                    self.dma("sp", raw[s][:], self.zT[zr + h * 256:zr + (h + 1) * 256, t0:t0 + 1024]
                             .rearrange("(c p) t -> p c t", p=128), [], [("raw", s)], ("raw", s))
                    self.dma("sp", cs[s][:, 0, :], self.rcos[:, t0:t0 + 1024], [], [("cs", s, 0)], ("cs", s, 0))
                    self.dma("sp", cs[s][:, 1, :], self.rsin[:, t0:t0 + 1024], [], [("cs", s, 1)], ("cs", s, 1))
                    x1 = raw[s][:, 0, :]
                    x2 = raw[s][:, 1, :]
                    co = cs[s][:, 0, :]
                    si = cs[s][:, 1, :]
                    rk_ = [("raw", s), ("cs", s, 0), ("cs", s, 1)]
                    self.tt("dve", tmp[0][:], x1, co, ALU.mult, rk_, [("rt", 0)])
                    self.tt("pool", tmp[1][:], x2, si, ALU.mult, rk_, [("rt", 1)])
                    self.tt("dve", dst[:, 0, t0:t0 + 1024], tmp[0][:], tmp[1][:], ALU.subtract,
                            [("rt", 0), ("rt", 1)], [(which, pc, 0)])
                    self.tt("pool", tmp[2][:], x2, co, ALU.mult, rk_, [("rt", 2)])
                    self.tt("dve", tmp[3][:], x1, si, ALU.mult, rk_, [("rt", 3)])
                    self.tt("dve", dst[:, 1, t0:t0 + 1024], tmp[2][:], tmp[3][:], ALU.add,
                            [("rt", 2), ("rt", 3)], [(which, pc, 1)])
            qkeys = lambda n: [("q", n // 8, 0), ("q", n // 8, 1)]
            kkeys = lambda n: [("k", n // 8, 0), ("k", n // 8, 1)]
            for n4 in range(NCHK // 4):
                b = n4 % 2
                for i in range(4):
                    n = n4 * 4 + i
                    for dc in range(2):
                        o = pst[b][:, (i * 2 + dc) * 128:(i * 2 + dc + 1) * 128]
                        self.S.op("pe", lambda e, o=o, n=n, dc=dc: e.transpose(o, krot[:, dc, n * 128:(n + 1) * 128], ident),
                                  reads=kkeys(n) + ["cb"], writes=[("pst", b)], is_mm=True, inc=(i == 3 and dc == 1))
                self.cp("act" if n4 % 2 else "dve", ktok[:, n4 * 4:(n4 + 1) * 4, :].rearrange("p n c -> p (n c)"),
                        pst[b][:, :], [("pst", b)], [("ktok", n4)])
            self.memset("dve", St[0][:], 0.0, [("St", 0)])
            self.act(vwa[:].rearrange("p n c -> p (n c)"), v[:].rearrange("p n c -> p (n c)"), AF.Copy,
                     ["v", "colv"], ["vwa"], scale=colv[:, 1:2])
            for n in range(NCHK - 1, -1, -1):
                cur = (n + 1) % 2
                nxt = n % 2
                self.cp("act", sbb[:, n].rearrange("p a b -> p (a b)"), St[cur][:].rearrange("p a b -> p (a b)"),
                        [("St", cur)], [("sbb", n)])
                if n == 0:
                    break
                b = 4 + n % 2
                for dc in range(2):
                    self.mm(ps[b][:, dc * 256:(dc + 1) * 256], ktok[:, n, dc * 128:(dc + 1) * 128], vwa[:, n, :], True, True,
                            [("ktok", n // 4), "vwa"], [("ps", b)], inc=(dc == 1))
                self.stt("dve", St[nxt][:].rearrange("p a b -> p (a b)"), St[cur][:].rearrange("p a b -> p (a b)"),
                         colv[:, 3:4], ps[b][:], ALU.mult, ALU.add, [("St", cur), ("ps", b), "colv"], [("St", nxt)])
            self.memset("dve", St[0][:], 0.0, [("St", 0)])
            self.act(vwa[:].rearrange("p n c -> p (n c)"), v[:].rearrange("p n c -> p (n c)"), AF.Copy,
                     ["v", "colv"], ["vwa"], scale=colv[:, 0:1])
            b_k, b_n = 4, 5
            tks = lambda n: slice(n * 128, (n + 1) * 128)

            def e_norm(n):
                s = n % 2
                for ec in range(2):
                    self.mm(ps[b_n][:, 0:128], ones, sqo[s][:, ec, :], ec == 0, ec == 1, [("sqo", s), "cb"], [("ps", b_n)])

            def e_rstd(n):
                s = n % 2
                self.rsqrt(rs[s][:], ps[b_n][:, 0:128], cf[:, CF_B + 1:CF_B + 2], rtm[s][:], [("ps", b_n), "cf"], ("rtm", s), ("rs", s))

            def e_qfb(n):
                s = n % 2
                self.tt("dve", qf[s][:], qrot[:, :, tks(n)], qft[:].unsqueeze(1).to_broadcast([128, 2, 128]), ALU.mult,
                        qkeys(n) + ["qft"], [("qf", s)])
                self.tt("dve", qb[s][:], qrot[:, :, tks(n)], qbt[:].unsqueeze(1).to_broadcast([128, 2, 128]), ALU.mult,
                        qkeys(n) + ["qbt"], [("qb", s)])

            def e_scores(n):
                s = n % 2
                for dc in range(2):
                    self.mm(ps[s][:, 0:128], krot[:, dc, tks(n)], qrot[:, dc, tks(n)], dc == 0, dc == 1,
                            kkeys(n) + qkeys(n), [("ps", s)])
                if n < NCHK - 1:
                    for dc in range(2):
                        self.mm(ps[b_k][:, dc * 256:(dc + 1) * 256], ktok[:, n, dc * 128:(dc + 1) * 128], vwa[:, n, :], True, True,
                                [("ktok", n // 4), "vwa"], [("ps", b_k)], inc=(dc == 1))

            def e_out(n):
                s = n % 2
                b_o = 2 + s
                for ec in range(2):
                    o = ps[b_o][:, ec * 128:(ec + 1) * 128]
                    es_ = slice(ec * 128, (ec + 1) * 128)
                    self.mm(o, v[:, n, es_], sT[s][:], True, False, ["v", ("sT", s)], [("ps", b_o)])
                    for dc in range(2):
                        self.mm(o, sfc[n % 3][:, dc, es_], qf[s][:, dc, :], False, False, [("sfc", n % 3), ("qf", s)], [("ps", b_o)])
                    for dc in range(2):
                        self.mm(o, sbb[:, n, dc, es_], qb[s][:, dc, :], False, dc == 1, [("sbb", n), ("qb", s)],
                                [("ps", b_o)], inc=(dc == 1 and ec == 1))

            def e_y(n):
                s = n % 2
                b_o = 2 + s
                self.stt("dve", y1[s][:], ps[b_o][:, 0:256].rearrange("p (a b) -> p a b", a=2), 16.0,
                         rs[s][:].unsqueeze(1).to_broadcast([128, 2, 128]), ALU.mult, ALU.mult,
                         [("ps", b_o), ("rs", s)], [("y1", s)])
                self.tt("dve", gst[:, :, tks(n)], y1[s][:], gst[:, :, tks(n)], ALU.mult, [("y1", s), "gst", ("go", n - 1)], [("go", n)])

            def e_sq(n):
                s = n % 2
                self.act(sqo[s][:].rearrange("p a b -> p (a b)"), ps[2 + s][:, 0:256], AF.Square, [("ps", 2 + s)], [("sqo", s)])

            def e_state(n):
                s = n % 2
                cur, nxt = n % 2, (n + 1) % 2
                self.tt("dve", sT[s][:], ps[s][:, 0:128], dmask[:], ALU.mult, [("ps", s), "dmask"], [("sT", s)])
                if n < NCHK - 1:
                    self.stt("dve", St[nxt][:].rearrange("p a b -> p (a b)"), St[cur][:].rearrange("p a b -> p (a b)"),
                             colv[:, 2:3], ps[b_k][:], ALU.mult, ALU.add, [("St", cur), ("ps", b_k), "colv"], [("St", nxt)])
                    f3 = (n + 1) % 3
                    self.cp("act", sfc[f3][:].rearrange("p a b -> p (a b)"), St[nxt][:].rearrange("p a b -> p (a b)"),
                            [("St", nxt)], [("sfc", f3)])

            self.cp("act", sfc[0][:].rearrange("p a b -> p (a b)"), St[0][:].rearrange("p a b -> p (a b)"),
                    [("St", 0)], [("sfc", 0)])
            for n in range(-1, NCHK + 1):
                a, b, c = n + 1, n, n - 1
                if 0 <= c < NCHK:
                    e_norm(c)
                    e_rstd(c)
                if 0 <= a < NCHK:
                    e_qfb(a)
                    e_scores(a)
                if 0 <= b < NCHK:
                    e_out(b)
                if 0 <= c < NCHK:
                    e_y(c)
                if 0 <= b < NCHK:
                    e_sq(b)
                if 0 <= a < NCHK:
                    e_state(a)
            self.dma("sp", self.bin[h * 256:(h + 1) * 256, :].rearrange("(c p) t -> p c t", p=128), gst[:],
                     [("go", NCHK - 1), "gst"], [], "gst")
            S.end_phase()

    def phase_pool(self, l):
        S = self.S
        with contextlib.ExitStack() as es:
            sb = lambda shape, dtype, name=None: self.sb(es, shape, dtype, name)
            cbt = sb([128, NCB], BF16, "cb")
            pvt = sb([128, NPV], F32, "pv")
            wpl_f = sb([128, 4, 2, 256], BF16, "wpl")
            ub = [sb([128, 2, SEQ], BF16, "ub") for _ in range(2)]
            Y = [sb([128, NCHK, 256], BF16, "Y") for _ in range(2)]
            pg = [sb([128, SEQ], BF16, "pg") for _ in range(4)]
            ps = self.pbanks(es)
            self.dma("sp", cbt[:], self.cb[:, :], [], ["cb"], "cb")
            self.dma("sp", pvt[:], self.pvec[:, l, :], [], ["pv"], "pv")
            self.dma("pool", wpl_f[:], self.w_pl[l], [], ["wpl"], "wpl")
            nb_ = 0
            for g in range(4):
                gs = g % 2
                row = ZPV + g * 256
                self.dma("sp", ub[gs][:], self.zT[row:row + 256, :].rearrange("(c p) t -> p c t", p=128),
                         [], [("ub", gs)], ("ub", gs))
                for ec in range(2):
                    r2 = ZPG + g * 256 + ec * 128
                    self.dma("sp", pg[gs * 2 + ec][:], self.zT[r2:r2 + 128, :], [], [("pg", gs, ec)], ("pg", gs, ec))
                for m2 in range(NCHK // 2):
                    b = nb_ % 8
                    nb_ += 1
                    for i in range(2):
                        m = m2 * 2 + i
                        for dc in range(2):
                            self.mm(ps[b][:, i * 256:(i + 1) * 256], ub[gs][:, dc, m * 128:(m + 1) * 128], wpl_f[:, g, dc, :],
                                    dc == 0, dc == 1, [("ub", gs), "wpl"], [("ps", b)], inc=(dc == 1 and i == 1))
                    self.cp("act" if m2 % 2 else "dve", Y[gs][:, m2 * 2:m2 * 2 + 2, :].rearrange("p a b -> p (a b)"), ps[b][:],
                            [("ps", b)], [("Y", gs, m2)])
                cbase = CB_POOL + g * 5 * 128
                for ec in range(2):
                    for n4 in range(NCHK // 4):
                        b = nb_ % 8
                        nb_ += 1
                        for i in range(4):
                            n = n4 * 4 + i
                            srcs = []
                            if n > 0:
                                srcs.append((n - 1, 0))
                            srcs.append((n, 3 if n == 0 else (4 if n == NCHK - 1 else 1)))
                            if n < NCHK - 1:
                                srcs.append((n + 1, 2))
                            for k, (m, kind) in enumerate(srcs):
                                self.mm(ps[b][:, i * 128:(i + 1) * 128], Y[gs][:, m, ec * 128:(ec + 1) * 128],
                                        cbt[:, cbase + kind * 128:cbase + (kind + 1) * 128], k == 0, k == len(srcs) - 1,
                                        [("Y", gs, m // 2), "cb"], [("ps", b)], inc=(k == len(srcs) - 1 and i == 3))
                        tk = slice(n4 * 512, (n4 + 1) * 512)
                        pgt = pg[gs * 2 + ec]
                        self.stt("dve", pgt[:, tk], ps[b][:], pvt[:, 16 + g * 2 + ec:17 + g * 2 + ec], pgt[:, tk],
                                 ALU.mult, ALU.mult, [("ps", b), "pv", ("pg", gs, ec)], [("pg", gs, ec)])
                    orow = 1024 + g * 256 + ec * 128
                    self.dma("sp", self.bin[orow:orow + 128, :], pg[gs * 2 + ec][:], [("pg", gs, ec)], [], ("pg", gs, ec))
            S.end_phase()

    def phase_att(self, l, g):
        S = self.S
        with contextlib.ExitStack() as es:
            sb = lambda shape, dtype, name=None: self.sb(es, shape, dtype, name)
            cbt = sb([128, NCB], BF16, "cb")
            pvt = sb([128, NPV], F32, "pv")
            gains = sb([128, 2], F32, "gains")
            sinkt = sb([128, 4, 128], F32, "sinkt")
            sinke = sb([128, 8], F32, "sinke")
            qn = sb([128, 5, SEQ], BF16, "qn")
            v = sb([128, NCHK, 128], BF16, "v")
            agst = sb([128, 4, SEQ], BF16, "agst")
            ps = self.pbanks(es)
            cfb = sb([128, 8], F32, "cfb")
            self.dma("sp", cfb[:], self.cf[:, CF_B:CF_B + 8], [], ["cfb"], "cfb")
            self.dma("sp", cbt[:], self.cb[:, :], [], ["cb"], "cb")
            self.dma("sp", pvt[:], self.pvec[:, l, :], [], ["pv"], "pv")
            vsrc = self.vtok[:, 1024 + g * 128:1024 + (g + 1) * 128].rearrange("(n p) c -> p n c", p=128)
            for i in range(4):
                self.dma("sp", v[:, i * 8:(i + 1) * 8, :], vsrc[:, i * 8:(i + 1) * 8, :], [], ["v"], "v")
            self.dma("sp", agst[:], self.zT[ZAG + g * 512:ZAG + (g + 1) * 512, :].rearrange("(c p) t -> p c t", p=128),
                     [], ["agst"], "agst")
            ones = cbt[:, CB_ONES:CB_ONES + 128]
            PT = cbt[:, CB_PT:CB_PT + 128]
            self.ts("dve", gains[:], pvt[:, 24:26], float(np.sqrt(128.0)), None, ALU.mult, None, ["pv"], ["gains"])
            self.act(sinke[:], pvt[:, 26:34], AF.Exp, ["pv", "cfb"], ["sinke"], bias=cfb[:, 3:4])
            self.cp("dve", sinkt[:], sinke[:, g * 4:(g + 1) * 4].unsqueeze(2).to_broadcast([128, 4, 128]),
                    ["sinke"], ["sinkt"])
            sinkrow = sb([128, 4, 128], BF16, "sinkrow")
            self.cp("dve", sinkrow[:], sinkt[:], ["sinkt"], ["sinkrow"])
            with contextlib.ExitStack() as es2:
                sb2 = lambda shape, dtype, name=None: self.sb(es2, shape, dtype, name)
                ctab = sb2([128, SEQ], F32, "ctab")
                stab = sb2([128, SEQ], F32, "stab")
                raw = [sb2([128, SEQ], BF16, "raw") for _ in range(2)]
                sq = [sb2([128, 512], BF16, "sq") for _ in range(4)]
                rs = [sb2([128, 512], F32, "rs") for _ in range(4)]
                qq = [sb2([128, 512], BF16, "qq") for _ in range(4)]
                t1 = [sb2([128, 512], F32, "t1") for _ in range(4)]
                t2 = [sb2([128, 512], F32, "t2") for _ in range(4)]
                rtm = [sb2([128, 512], F32, "rtm") for _ in range(4)]
                self.dma("sp", ctab[:], self.acos[:, :], [], ["ctab"], "ctab")
                self.dma("sp", stab[:], self.asin[:, :], [], ["stab"], "stab")
                pieces = []
                for hh in range(5):
                    r = hh % 2
                    row = (ZAQ + (g * 4 + hh) * 128) if hh < 4 else (ZAK + g * 128)
                    gcol = gains[:, 0:1] if hh < 4 else gains[:, 1:2]
                    for pc in range(8):
                        pieces.append((hh, r, row, gcol, pc))

                def p1(i):
                    hh, r, row, gcol, pc = pieces[i]
                    s = i % 4
                    tk = slice(pc * 512, (pc + 1) * 512)
                    if pc == 0:
                        self.dma("sp", raw[r][:], self.zT[row:row + 128, :], [], [("raw", r)], ("raw", r))
                    self.act(sq[s][:], raw[r][:, tk], AF.Square, [("raw", r)], [("sq", s)])
                    self.mm(ps[s][:], ones, sq[s][:], True, True, [("sq", s), "cb"], [("ps", s)])

                def p2(i):
                    hh, r, row, gcol, pc = pieces[i]
                    s = i % 4
                    tk = slice(pc * 512, (pc + 1) * 512)
                    self.rsqrt(rs[s][:], ps[s][:], cfb[:, 2:3], rtm[s][:], [("ps", s), "cfb"], ("rtm", s), ("rs", s))
                    self.stt("dve", qq[s][:], raw[r][:, tk], gcol, rs[s][:], ALU.mult, ALU.mult,
                             [("raw", r), ("rs", s), "gains"], [("qq", s)])
                    self.mm(ps[4 + s][:], PT, qq[s][:], True, True, [("qq", s), "cb"], [("ps", 4 + s)])

                def p3(i):
                    hh, r, row, gcol, pc = pieces[i]
                    s = i % 4
                    tk = slice(pc * 512, (pc + 1) * 512)
                    self.tt("dve", t1[s][:], qq[s][:], ctab[:, tk], ALU.mult, [("qq", s), "ctab"], [("t1", s)])
                    self.tt("dve", t2[s][:], ps[4 + s][:], stab[:, tk], ALU.mult, [("ps", 4 + s), "stab"], [("t2", s)])
                    self.tt("pool", qn[:, hh, tk], t1[s][:], t2[s][:], ALU.add, [("t1", s), ("t2", s)], [("qn", hh, pc)])

                NP = len(pieces)
                for i in range(NP + 2):
                    if i < NP:
                        p1(i)
                    if 0 <= i - 2 < NP:
                        p3(i - 2)
                    if 0 <= i - 1 < NP:
                        p2(i - 1)
                S.end_phase()
            with contextlib.ExitStack() as es3:
                sb3 = lambda shape, dtype, name=None: self.sb(es3, shape, dtype, name)
                pT = [sb3([128, 3, 512], BF16, "pT") for _ in range(2)]
                den = [sb3([128, 512], F32, "den") for _ in range(2)]
                o1 = [sb3([128, 512], F32, "o1") for _ in range(2)]
                MP = cbt[:, CB_MP:CB_MP + 128]
                MN = cbt[:, CB_MN:CB_MN + 128]
                scale = float(128.0 ** -0.5)
                def stage_s(n):
                    s = n % 2
                    segs = [m for m in (n - 1, n, n + 1) if 0 <= m < NCHK]
                    qtk = slice(n * 128, (n + 1) * 128)
                    for si, m in enumerate(segs):
                        b = s * 3 + si
                        self.mm(ps[b][:].rearrange("p (a b) -> p a b", a=4), qn[:, 4, m * 128:(m + 1) * 128], qn[:, 0:4, qtk], True, True, [], [("ps", b)])
                        self.act(pT[s][:, si, :], ps[b][:], AF.Exp, [("ps", b)], [("pT", s, si)], scale=scale, bias=cfb[:, 3:4])
                        if m != n:
                            mk = MP if m < n else MN
                            pv_ = pT[s][:, si, :].rearrange("p (a b) -> p a b", a=4)
                            self.tt("dve", pv_, pv_, mk.unsqueeze(1).to_broadcast([128, 4, 128]),
                                    ALU.mult, [("pT", s, si), "cb"], [("pT", s, si)])
                def stage_o(n):
                    s = n % 2
                    segs = [m for m in (n - 1, n, n + 1) if 0 <= m < NCHK]
                    qtk = slice(n * 128, (n + 1) * 128)
                    for si, m in enumerate(segs):
                        self.mm(ps[6][:], v[:, m, :], pT[s][:, si, :], si == 0, si == len(segs) - 1,
                                [("pT", s, si), "v"], [("ps", 6)])
                    for si, m in enumerate(segs):
                        self.mm(ps[7][:], ones, pT[s][:, si, :], si == 0, False,
                                [("pT", s, si), "cb"], [("ps", 7)], inc=False)
                    self.mm(ps[7][:], cbt[0:1, CB_ONES:CB_ONES + 128], sinkrow[0:1].rearrange("p a b -> p (a b)"), False, True,
                            ["cb", "sinkrow"], [("ps", 7)])
                    self.act(den[s][:], ps[7][:], AF.Ln, [("ps", 7)], [("den", s)])
                    self.act(den[s][:], den[s][:], AF.Exp, [("den", s)], [("den", s)], scale=-1.0)
                    self.tt("dve", o1[s][:], ps[6][:], den[s][:], ALU.mult, [("ps", 6), ("den", s)], [("o1", s)])
                    self.tt("dve", agst[:, :, qtk], o1[s][:].rearrange("p (a b) -> p a b", a=4), agst[:, :, qtk], ALU.mult,
                            [("o1", s), "agst", ("ao", n - 1)], [("ao", n)])
                stage_s(0)
                for n in range(NCHK):
                    if n + 1 < NCHK:
                        stage_s(n + 1)
                    stage_o(n)
                orow = 2048 + g * 512
                self.dma("sp", self.bin[orow:orow + 512, :].rearrange("(c p) t -> p c t", p=128), agst[:],
                         [("ao", NCHK - 1), "agst"], [], "agst")
                S.end_phase()

    def phase_merge(self, l, tq, x_cur, x_nxt):
        S = self.S
        T0 = tq * 1024
        with contextlib.ExitStack() as es:
            sb = lambda shape, dtype, name=None: self.sb(es, shape, dtype, name)
            binq = sb([128, 24, 1024], BF16, "binq")
            mT = sb([128, 16, 1024], BF16, "mT")
            wbr = [sb([128, 3, 8, 128], BF16, "wbr") for _ in range(2)]
            wbs = [sb([128, 3, 8, 128], F32, "wbs") for _ in range(2)]
            wos = [sb([128, 16, 128], F32, "wos") for _ in range(2)]
            gq = [sb([128, 3, 1024], BF16, "gq") for _ in range(2)]
            wo = [sb([128, 16, 128], BF16, "wo") for _ in range(2)]
            xq = [sb([128, 1024], F32, "xq") for _ in range(3)]
            s1 = [sb([128, 512], F32, "s1") for _ in range(2)]
            s2 = [sb([128, 512], F32, "s2") for _ in range(2)]
            ta = [sb([128, 512], F32, "ta") for _ in range(2)]
            ps = self.pbanks(es)
            for br in range(3):
                self.dma("sp", binq[:, br * 8:(br + 1) * 8, :],
                         self.bin[br * 1024:(br + 1) * 1024, T0:T0 + 1024].rearrange("(c p) t -> p c t", p=128),
                         [], [("binq", br)], ("binq", br))
            gv = self.zT[ZMG:ZMG + 6144, :].rearrange("(br jj p) t -> p br jj t", br=3, p=128)
            it = 0
            def load_b(j):
                s = j % 2
                self.dma("sp", wbs[s][:], self.w_br[l, j], [], [("wbs", s)], ("wbs", s))
                self.dma("sp", gq[s][:], gv[:, :, j, T0:T0 + 1024], [], [("gq", s)], ("gq", s))

            load_b(0)
            for j in range(16):
                s = j % 2
                self.cp("act", wbr[s][:].rearrange("p a b c -> p (a b c)"), wbs[s][:].rearrange("p a b c -> p (a b c)"),
                        [("wbs", s)], [("wbr", s)])
                if j + 1 < 16:
                    load_b(j + 1)
                for tt in range(2):
                    u = it % 2
                    it += 1
                    tk = slice(tt * 512, (tt + 1) * 512)
                    bb = [(it % 2) * 3 + br for br in range(3)]
                    for br in range(3):
                        for ec in range(8):
                            self.mm(ps[bb[br]][:], wbr[s][:, br, ec, :], binq[:, br * 8 + ec, tk], ec == 0, ec == 7,
                                    [("wbr", s), ("binq", br)], [("ps", bb[br])])
                    self.tt("dve", ta[u][:], ps[bb[0]][:], gq[s][:, 0, tk], ALU.mult, [("ps", bb[0]), ("gq", s)], [("ta", u)])
                    self.tt("dve", s1[u][:], ps[bb[1]][:], gq[s][:, 1, tk], ALU.mult, [("ps", bb[1]), ("gq", s)], [("s1", u)])
                    self.tt("dve", s2[u][:], ps[bb[2]][:], gq[s][:, 2, tk], ALU.mult, [("ps", bb[2]), ("gq", s)], [("s2", u)])
                    self.tt("pool", ta[u][:], ta[u][:], s1[u][:], ALU.add, [("ta", u), ("s1", u)], [("ta", u)])
                    self.tt("dve", mT[:, j, tk], ta[u][:], s2[u][:], ALU.add, [("ta", u), ("s2", u)], [("mT", j, tt)])
            mkeys = lambda tt: [("mT", j, tt) for j in range(16)]
            def load_o(i):
                s = i % 2
                x3 = i % 3
                self.dma("sp", wos[s][:], self.w_o[l, i], [], [("wos", s)], ("wos", s))
                self.dma("sp", xq[x3][:], x_cur[i * 128:(i + 1) * 128, T0:T0 + 1024], [], [("xq", x3)], ("xq", x3))

            load_o(0)
            for i in range(16):
                s = i % 2
                x3 = i % 3
                if i + 1 < 16:
                    load_o(i + 1)
                self.cp("act", wo[s][:].rearrange("p a b -> p (a b)"), wos[s][:].rearrange("p a b -> p (a b)"),
                        [("wos", s)], [("wo", s)])
                for tt in range(2):
                    b = 6 + tt
                    tk = slice(tt * 512, (tt + 1) * 512)
                    for jc in range(16):
                        self.mm(ps[b][:], wo[s][:, jc, :], mT[:, jc, tk], jc == 0, jc == 15,
                                [("wo", s)] + mkeys(tt), [("ps", b)])
                    self.tt("dve", xq[x3][:, tk], ps[b][:], xq[x3][:, tk], ALU.add, [("ps", b), ("xq", x3)], [("xq", x3)])
                self.dma("sp", x_nxt[i * 128:(i + 1) * 128, T0:T0 + 1024], xq[x3][:], [("xq", x3)], [], ("xq", x3))
            S.end_phase()


def _consts():
    f32 = np.float32
    t = np.arange(SEQ, dtype=f32)
    inv_r = (1.0 / (f32(10000.0) ** np.linspace(0.0, 1.0, 128, dtype=f32))).astype(f32)
    ang = (t[:, None] * inv_r[None, :]).astype(f32)
    rcos = np.ascontiguousarray(np.cos(ang).T.astype(f32))
    rsin = np.ascontiguousarray(np.sin(ang).T.astype(f32))
    inv_a = (f32(500000.0) ** (-np.arange(16, dtype=f32) / f32(16.0))).astype(f32)
    anga = (t[:, None] * inv_a[None, :]).astype(f32)
    acos = np.ones((128, SEQ), f32)
    asin = np.zeros((128, SEQ), f32)
    acos[0:16] = np.cos(anga).T
    acos[16:32] = np.cos(anga).T
    asin[0:16] = np.sin(anga).T
    asin[16:32] = np.sin(anga).T
    cf = np.zeros((128, NCF), f32)
    li = np.arange(128, dtype=f32)[:, None]
    ji = np.arange(128, dtype=f32)[None, :]
    cf[:, CF_LF:CF_LF + 128] = np.maximum(ji - li, 0)
    cf[:, CF_LB:CF_LB + 128] = np.maximum(li - ji, 0)
    cf[:, CF_JF:CF_JF + 128] = ji + 1.0
    cf[:, CF_JB:CF_JB + 128] = 128.0 - ji
    cf[:, CF_COL] = 127.0 - li[:, 0]
    cf[:, CF_COL + 1] = li[:, 0]
    cf[:, CF_COL + 2] = 128.0
    cf[:, CF_B:CF_B + 5] = np.array([2048.0 * EPS, 256.0 * EPS, 128.0 * EPS, -M0, -np.log(16.0)], f32)[None, :]
    for g, w in enumerate((2, 4, 8, 16)):
        hw = w // 2
        left = np.zeros(8, f32)
        right = np.zeros(8, f32)
        for i in range(8):
            n = i
            lo, hi = max(n - hw, 0), min(n + hw, SEQ)
            left[i] = 1.0 / (hi - lo)
            n = SEQ - 8 + i
            lo, hi = max(n - hw, 0), min(n + hw, SEQ)
            right[i] = 1.0 / (hi - lo)
        cf[:, CF_PINV + g * 16:CF_PINV + g * 16 + 8] = left[None, :]
        cf[:, CF_PINV + g * 16 + 8:CF_PINV + g * 16 + 16] = right[None, :]
    cb = np.zeros((128, NCB), f32)
    cb[:, CB_ONES:CB_ONES + 128] = 1.0
    cb[:, CB_ID:CB_ID + 128] = np.eye(128, dtype=f32)
    PT = np.zeros((128, 128), f32)
    for m in range(16):
        PT[m + 16, m] = -1.0
        PT[m, m + 16] = 1.0
    cb[:, CB_PT:CB_PT + 128] = PT
    cb[:, CB_MP:CB_MP + 128] = (li >= ji).astype(f32)
    cb[:, CB_MN:CB_MN + 128] = (li <= ji).astype(f32)
    def pool_block(w, mb, nb):
        hw = w // 2
        m = (mb * 128 + np.arange(128))[:, None]
        n = (nb * 128 + np.arange(128))[None, :]
        lo = np.maximum(n - hw, 0)
        hi = np.minimum(n + hw, SEQ)
        a = ((m >= lo) & (m < hi)).astype(np.float64) / (hi - lo) - (m == n)
        return a.astype(f32)
    for g, w in enumerate((2, 4, 8, 16)):
        blks = [pool_block(w, 4, 5), pool_block(w, 5, 5), pool_block(w, 6, 5),
                pool_block(w, 0, 0), pool_block(w, NCHK - 1, NCHK - 1)]
        for k, blk in enumerate(blks):
            c0 = CB_POOL + (g * 5 + k) * 128
            cb[:, c0:c0 + 128] = blk
    return dict(rcos=rcos, rsin=rsin, acos=acos, asin=asin, cf=cf, cb=cb.astype(ml_dtypes.bfloat16))


def _prep_weights(inputs):
    f32 = np.float32
    w_in = inputs["w_in"]
    fmcols = np.concatenate([np.arange(z0, z0 + w) for (z0, w, _) in FM_SEGS])
    vcols = np.concatenate([np.arange(z0, z0 + w) for (z0, w) in V_SEGS])
    win_fm = np.empty((NLAYER, NFM, 128, 16, 128), f32)
    win_v = np.empty((NLAYER, 128, 16, 1280), f32)
    for l in range(NLAYER):
        a = w_in[l][:, fmcols].reshape(16, 128, NFM, 128)
        win_fm[l] = a.transpose(2, 1, 0, 3)
        b = w_in[l][:, vcols].reshape(16, 128, 1280)
        win_v[l] = b.transpose(1, 0, 2)
    wb = np.stack([inputs["w_ret"], inputs["w_pool"], inputs["w_att"]], axis=1)
    w_br = np.ascontiguousarray(wb.reshape(NLAYER, 3, 8, 128, 16, 128).transpose(0, 4, 3, 1, 2, 5))
    w_o = np.ascontiguousarray(inputs["w_out"].reshape(NLAYER, 16, 128, 16, 128).transpose(0, 3, 2, 1, 4))
    w_pl = np.ascontiguousarray(inputs["pool_w"].reshape(NLAYER, 4, 2, 128, 256).transpose(0, 3, 1, 2, 4))
    pvec = np.zeros((128, NLAYER, NPV), f32)
    for l in range(NLAYER):
        pvec[:, l, 0:16] = inputs["norm_g"][l].reshape(16, 128).T
        pvec[:, l, 16:24] = inputs["pool_scale"][l].reshape(8, 128).T
        pvec[:, l, 24] = inputs["attn_q_gain"][l]
        pvec[:, l, 25] = inputs["attn_k_gain"][l]
        pvec[:, l, 26:34] = inputs["attn_sink"][l][None, :]
        pvec[:, l, 34:38] = inputs["ret_decay_fwd"][l][None, :]
        pvec[:, l, 38:42] = inputs["ret_decay_bwd"][l][None, :]
    return dict(win_fm=win_fm, win_v=win_v, w_br=w_br, w_o=w_o, w_pl=w_pl, pvec=pvec)


def kernel(**inputs):
    inputs = {k: np.asarray(v) for k, v in inputs.items()}
    x = inputs["x"]
    B = x.shape[0]
    shared = _prep_weights(inputs)
    shared.update(_consts())
    prog = Prog()
    nc = prog.build()
    in_maps = []
    for b in range(B):
        m = dict(shared)
        m["xT"] = np.ascontiguousarray(x[b].T)
        in_maps.append(m)
    res = run_bass_kernel_spmd(nc, in_maps, core_ids=list(range(B)))
    out = np.stack([np.ascontiguousarray(r["outT"].T) for r in res.results], axis=0)
    return out.astype(np.float32)
```
